# Optimizing a Trainium2 kernel written in Bass

```python
import jax
import jax.numpy as jnp
from jax import lax
import numpy as np

D_MODEL = 1024
BATCH = 4
SEQ = 8192
DEPTH = 1
DEC_BATCH = 32
DEC_SEQ = 64
PAST_LEN = 2048

CHUNK = 64
N_META = 16
D_MIX = D_MODEL
H_GDN = 4
GDN_DK = 128
GDN_DV = 128
CONV_W = 4
CONV_DIM = H_GDN * (2 * GDN_DK + GDN_DV)
H_MLA = 4
Q_LORA = 384
KV_LORA = 256
DN = 128
DR = 64
DV_MLA = 128
ROPE_BASE = 10000.0
SM_SCALE = (DN + DR) ** -0.5
QB = 128
D_FF = 2816
EPS = 1e-6
L2_EPS = 1e-6
PROJ_SIZES = (CONV_DIM, H_GDN * GDN_DV, H_GDN, H_GDN, Q_LORA, KV_LORA, DR)
N_PROJ = sum(PROJ_SIZES)

kernel_name = 'hymba_gdn_mla_macaron_stream_step'


def rmsnorm(x, g):
    xf = x.astype(jnp.float32)
    y = xf * lax.rsqrt(jnp.mean(xf * xf, axis=-1, keepdims=True) + EPS)
    return (y * g.astype(jnp.float32)).astype(x.dtype)


def l2norm(x):
    return x * lax.rsqrt(jnp.sum(x * x, axis=-1, keepdims=True) + L2_EPS)


def half_ffn(x, g, wg, wu, wd):
    h = rmsnorm(x, g)
    return x + 0.5 * ((jax.nn.silu(h @ wg) * (h @ wu)) @ wd)


def split_proj(p):
    offs = np.cumsum(PROJ_SIZES)[:-1].tolist()
    return jnp.split(p, offs, axis=-1)


def rope_tables(pos):
    inv = ROPE_BASE ** (-jnp.arange(0, DR, 2, dtype=jnp.float32) / DR)
    ang = pos.astype(jnp.float32)[:, None] * inv[None, :]
    return jnp.cos(ang), jnp.sin(ang)


def apply_rope(x, cos, sin):
    x1 = x[..., :DR // 2].astype(jnp.float32)
    x2 = x[..., DR // 2:].astype(jnp.float32)
    return jnp.concatenate([x1 * cos - x2 * sin, x2 * cos + x1 * sin], axis=-1).astype(x.dtype)


def causal_conv(xe, w):
    t = xe.shape[1] - (CONV_W - 1)
    return sum(xe[:, j:j + t] * w[j] for j in range(CONV_W))


def gdn_features(u, a, b, a_log, dt_bias):
    u = jax.nn.silu(u.astype(jnp.float32))
    q, k, v = jnp.split(u, [H_GDN * GDN_DK, 2 * H_GDN * GDN_DK], axis=-1)
    bsz, t = u.shape[:2]
    q = l2norm(q.reshape(bsz, t, H_GDN, GDN_DK)) * (GDN_DK ** -0.5)
    k = l2norm(k.reshape(bsz, t, H_GDN, GDN_DK))
    v = v.reshape(bsz, t, H_GDN, GDN_DV)
    g = -jnp.exp(a_log.astype(jnp.float32)) * jax.nn.softplus(a.astype(jnp.float32) + dt_bias.astype(jnp.float32))
    beta = jax.nn.sigmoid(b.astype(jnp.float32))
    tr = lambda z: jnp.swapaxes(z, 1, 2)
    return tr(q), tr(k), tr(v), tr(g), tr(beta)


def gdn_block(m0, q, k, v, g, beta):
    L = q.shape[2]
    causal = jnp.tril(jnp.ones((L, L), dtype=bool))
    strict = jnp.tril(jnp.ones((L, L), dtype=bool), -1)
    gc = jnp.cumsum(g, axis=-1)
    diff = gc[..., :, None] - gc[..., None, :]
    decay = jnp.where(causal, jnp.exp(jnp.where(causal, diff, 0.0)), 0.0)
    a = jnp.where(strict, beta[..., :, None] * jnp.einsum('bhtd,bhsd->bhts', k, k) * decay, 0.0)
    eg = jnp.exp(gc)[..., None]
    rhs = beta[..., None] * (v - eg * jnp.einsum('bhtd,bhde->bhte', k, m0))
    u = lax.linalg.triangular_solve(jnp.eye(L, dtype=a.dtype) + a, rhs, left_side=True, lower=True, unit_diagonal=True)
    o = eg * jnp.einsum('bhtd,bhde->bhte', q, m0) + jnp.einsum('bhts,bhse->bhte', jnp.einsum('bhtd,bhsd->bhts', q, k) * decay, u)
    g_last = gc[..., -1:]
    m = jnp.exp(g_last)[..., None] * m0 + jnp.einsum('bhsd,bhse->bhde', k * jnp.exp(g_last - gc)[..., None], u)
    return m, o


def gdn_prompt(q, k, v, g, beta):
    bsz = q.shape[0]
    m0 = jnp.zeros((bsz, H_GDN, GDN_DK, GDN_DV), jnp.float32)
    m, o_meta = gdn_block(m0, q[:, :, :N_META], k[:, :, :N_META], v[:, :, :N_META], g[:, :, :N_META], beta[:, :, :N_META])

    def to_chunks(z):
        z = z[:, :, N_META:]
        nc = z.shape[2] // CHUNK
        return jnp.moveaxis(z.reshape(z.shape[:2] + (nc, CHUNK) + z.shape[3:]), 2, 0)

    m, o_f = lax.scan(lambda mm, blk: gdn_block(mm, *blk), m, tuple(to_chunks(z) for z in (q, k, v, g, beta)))
    o_f = jnp.moveaxis(o_f, 0, 2)
    o_f = o_f.reshape(o_f.shape[:2] + (-1, GDN_DV))
    return jnp.concatenate([o_meta, o_f], axis=2), m


def gdn_output(o, z, w):
    o = jnp.swapaxes(o, 1, 2)
    bsz, t = o.shape[:2]
    n = rmsnorm(o, w)
    gate = jax.nn.silu(z.astype(jnp.float32)).reshape(bsz, t, H_GDN, GDN_DV)
    return (n * gate).reshape(bsz, t, H_GDN * GDN_DV).astype(z.dtype)


def mla_project(cq, ckv, kr, cos, sin, gq, gkv, w_uq):
    bsz, t = cq.shape[:2]
    q = (rmsnorm(cq, gq) @ w_uq).reshape(bsz, t, H_MLA, DN + DR)
    qn = q[..., :DN]
    qr = apply_rope(q[..., DN:], cos[:, None, :], sin[:, None, :])
    c = rmsnorm(ckv, gkv)
    kr = apply_rope(kr, cos, sin)
    return qn, qr, c, kr


def mla_expand(c, w_ukv):
    bsz, t = c.shape[:2]
    kv = (c @ w_ukv).reshape(bsz, t, H_MLA, DN + DV_MLA)
    return kv[..., :DN], kv[..., DN:]


def mla_attend(qn, qr, kn, kr, v, mask):
    s = (jnp.einsum('bqhd,bkhd->bhqk', qn, kn) + jnp.einsum('bqhr,bkr->bhqk', qr, kr)).astype(jnp.float32) * SM_SCALE
    if mask is not None:
        s = jnp.where(mask, s, -jnp.inf)
    p = jax.nn.softmax(s, axis=-1).astype(v.dtype)
    return jnp.einsum('bhqk,bkhd->bqhd', p, v)


def mla_prompt(qn, qr, c, kr, w_ukv, key_chunk):
    kn, v = mla_expand(c, w_ukv)
    bsz, t = c.shape[:2]
    s_len = t - N_META
    o_meta = mla_attend(qn[:, :N_META], qr[:, :N_META], kn[:, :N_META], kr[:, :N_META], v[:, :N_META], None)
    nb = s_len // QB

    def blocks(z):
        z = z[:, N_META:]
        return jnp.moveaxis(z.reshape((bsz, nb, QB) + z.shape[2:]), 1, 0)

    def one(args):
        qn_b, qr_b, j = args
        q_chunk = (j * QB + jnp.arange(QB)) // CHUNK
        mask = key_chunk[None, :] <= q_chunk[:, None]
        return mla_attend(qn_b, qr_b, kn, kr, v, mask)

    o_f = lax.map(one, (blocks(qn), blocks(qr), jnp.arange(nb)))
    o_f = jnp.moveaxis(o_f, 0, 1).reshape(bsz, s_len, H_MLA * DV_MLA)
    return jnp.concatenate([o_meta.reshape(bsz, N_META, H_MLA * DV_MLA), o_f], axis=1)


def setup_inputs(seed: int = 0) -> dict:
    key = jax.random.key(seed)
    ks = iter(jax.random.split(key, 40))
    f32 = jnp.float32
    nrm = lambda shape, scale: jax.random.normal(next(ks), shape, f32) * scale
    gain = lambda shape: 1.0 + 0.05 * jax.random.normal(next(ks), shape, f32)
    a_log = jnp.log(jax.random.uniform(next(ks), (DEPTH, H_GDN), f32, 1.0, 16.0))
    dt = jnp.exp(jax.random.uniform(next(ks), (DEPTH, H_GDN), f32, np.log(1e-3), np.log(1e-1)))
    dt_bias = dt + jnp.log(-jnp.expm1(-dt))
    return {
        'x_prompt': nrm((BATCH, SEQ, D_MODEL), 1.0),
        'x_sample': nrm((DEC_BATCH, DEC_SEQ, D_MODEL), 1.0),
        'cache_mla_ckv': nrm((DEPTH, DEC_BATCH, N_META + PAST_LEN, KV_LORA), 1.0),
        'cache_mla_krope': nrm((DEPTH, DEC_BATCH, N_META + PAST_LEN, DR), 1.0),
        'state_gdn': nrm((DEPTH, DEC_BATCH, H_GDN, GDN_DK, GDN_DV), GDN_DK ** -0.5),
        'state_conv': nrm((DEPTH, DEC_BATCH, CONV_W - 1, CONV_DIM), 1.0),
        'meta': nrm((N_META, D_MODEL), 1.0),
        'ffn1_norm': gain((DEPTH, D_MODEL)),
        'ffn1_wg': nrm((DEPTH, D_MODEL, D_FF), D_MODEL ** -0.5),
        'ffn1_wu': nrm((DEPTH, D_MODEL, D_FF), D_MODEL ** -0.5),
        'ffn1_wd': nrm((DEPTH, D_FF, D_MODEL), D_FF ** -0.5),
        'mix_norm': gain((DEPTH, D_MODEL)),
        'w_in': nrm((DEPTH, D_MODEL, N_PROJ), D_MODEL ** -0.5),
        'conv_w': nrm((DEPTH, CONV_W, CONV_DIM), CONV_W ** -0.5),
        'a_log': a_log,
        'dt_bias': dt_bias,
        'gdn_norm': gain((DEPTH, GDN_DV)),
        'q_norm': gain((DEPTH, Q_LORA)),
        'kv_norm': gain((DEPTH, KV_LORA)),
        'w_uq': nrm((DEPTH, Q_LORA, H_MLA * (DN + DR)), Q_LORA ** -0.5),
        'w_ukv': nrm((DEPTH, KV_LORA, H_MLA * (DN + DV_MLA)), KV_LORA ** -0.5),
        'w_out': nrm((DEPTH, D_MIX, D_MODEL), D_MIX ** -0.5),
        'ffn2_norm': gain((DEPTH, D_MODEL)),
        'ffn2_wg': nrm((DEPTH, D_MODEL, D_FF), D_MODEL ** -0.5),
        'ffn2_wu': nrm((DEPTH, D_MODEL, D_FF), D_MODEL ** -0.5),
        'ffn2_wd': nrm((DEPTH, D_FF, D_MODEL), D_FF ** -0.5),
        'final_norm': gain((D_MODEL,)),
    }


def reference(x_prompt, x_sample, cache_mla_ckv, cache_mla_krope, state_gdn, state_conv, meta,
              ffn1_norm, ffn1_wg, ffn1_wu, ffn1_wd, mix_norm, w_in, conv_w, a_log, dt_bias, gdn_norm,
              q_norm, kv_norm, w_uq, w_ukv, w_out, ffn2_norm, ffn2_wg, ffn2_wu, ffn2_wd, final_norm):
    bsz, s_len = x_prompt.shape[:2]
    d_seq = x_sample.shape[1]
    xp = jnp.concatenate([jnp.broadcast_to(meta[None].astype(x_prompt.dtype), (bsz, N_META, D_MODEL)), x_prompt], axis=1)
    xs = x_sample
    cos_p, sin_p = rope_tables(jnp.arange(N_META + s_len))
    cos_s, sin_s = rope_tables(cache_mla_ckv.shape[2] + jnp.arange(d_seq))
    key_chunk = jnp.concatenate([jnp.full((N_META,), -1, jnp.int32), jnp.arange(s_len, dtype=jnp.int32) // CHUNK])
    p_ckv, p_kr, p_gdn, p_conv, s_ckv, s_kr, s_gdn, s_conv = [], [], [], [], [], [], [], []
    for l in range(DEPTH):
        xp = half_ffn(xp, ffn1_norm[l], ffn1_wg[l], ffn1_wu[l], ffn1_wd[l])
        xs = half_ffn(xs, ffn1_norm[l], ffn1_wg[l], ffn1_wu[l], ffn1_wd[l])
        qkv_p, z_p, a_p, b_p, cq_p, ckv_p, kr_p = split_proj(rmsnorm(xp, mix_norm[l]) @ w_in[l])
        qkv_s, z_s, a_s, b_s, cq_s, ckv_s, kr_s = split_proj(rmsnorm(xs, mix_norm[l]) @ w_in[l])
        ext_p = jnp.pad(qkv_p, ((0, 0), (CONV_W - 1, 0), (0, 0)))
        o_p, m_p = gdn_prompt(*gdn_features(causal_conv(ext_p, conv_w[l]), a_p, b_p, a_log[l], dt_bias[l]))
        gdn_p = gdn_output(o_p, z_p, gdn_norm[l])
        ext_s = jnp.concatenate([state_conv[l].astype(qkv_s.dtype), qkv_s], axis=1)
        m_s, o_s = gdn_block(state_gdn[l].astype(jnp.float32), *gdn_features(causal_conv(ext_s, conv_w[l]), a_s, b_s, a_log[l], dt_bias[l]))
        gdn_s = gdn_output(o_s, z_s, gdn_norm[l])
        qn_p, qr_p, c_p, kro_p = mla_project(cq_p, ckv_p, kr_p, cos_p, sin_p, q_norm[l], kv_norm[l], w_uq[l])
        mla_p = mla_prompt(qn_p, qr_p, c_p, kro_p, w_ukv[l], key_chunk)
        qn_s, qr_s, c_s, kro_s = mla_project(cq_s, ckv_s, kr_s, cos_s, sin_s, q_norm[l], kv_norm[l], w_uq[l])
        c_all = jnp.concatenate([cache_mla_ckv[l].astype(c_s.dtype), c_s], axis=1)
        kr_all = jnp.concatenate([cache_mla_krope[l].astype(kro_s.dtype), kro_s], axis=1)
        kn_s, v_s = mla_expand(c_all, w_ukv[l])
        mla_s = mla_attend(qn_s, qr_s, kn_s, kr_all, v_s, None).reshape(xs.shape[0], d_seq, H_MLA * DV_MLA)
        xp = xp + jnp.concatenate([gdn_p, mla_p.astype(gdn_p.dtype)], axis=-1) @ w_out[l]
        xs = xs + jnp.concatenate([gdn_s, mla_s.astype(gdn_s.dtype)], axis=-1) @ w_out[l]
        xp = half_ffn(xp, ffn2_norm[l], ffn2_wg[l], ffn2_wu[l], ffn2_wd[l])
        xs = half_ffn(xs, ffn2_norm[l], ffn2_wg[l], ffn2_wu[l], ffn2_wd[l])
        p_ckv.append(c_p)
        p_kr.append(kro_p)
        p_gdn.append(m_p.astype(x_prompt.dtype))
        p_conv.append(qkv_p[:, -(CONV_W - 1):])
        s_ckv.append(c_s)
        s_kr.append(kro_s)
        s_gdn.append(m_s.astype(state_gdn.dtype))
        s_conv.append(ext_s[:, -(CONV_W - 1):])
    y_prompt = rmsnorm(xp[:, N_META:], final_norm)
    y_sample = rmsnorm(xs, final_norm)
    return (y_prompt, y_sample, jnp.stack(p_ckv), jnp.stack(p_kr), jnp.stack(p_gdn), jnp.stack(p_conv),
            jnp.stack(s_ckv), jnp.stack(s_kr), jnp.stack(s_gdn), jnp.stack(s_conv))
```

```python
import contextlib
import numpy as np
import concourse.bass as bass
import concourse.mybir as mybir
from concourse.bass_utils import run_bass_kernel_spmd

F32 = mybir.dt.float32
BF16 = mybir.dt.bfloat16
F32R = mybir.dt.float32r
AF = mybir.ActivationFunctionType
ALU = mybir.AluOpType

D = 1024
DFF = 2816
NFC = DFF // 128
KC = D // 128
EPS = 1e-6


class Eng:
    def __init__(self, name, e, sem):
        self.name, self.e, self.sem = name, e, sem
        self.tick = 0
        self.seen = {}
        self.nwait = 0
        self.nins = 0


class DSem:
    def __init__(self, h):
        self.h = h
        self.count = 0


class Buf:
    def __init__(self, t, name, dsem=None):
        self.t = t
        self.name = name
        self.last_w = None
        self.extra_w = []
        self.readers = {}
        self.ds = dsem
        self.native_r = False
        self.rmode = False

    def __getitem__(self, k):
        ap = self.t[k]
        if self.native_r and not self.rmode:
            return ap.bitcast(F32)
        return ap

    def f32(self, k):
        ap = self.t[k]
        return ap.bitcast(F32) if self.native_r else ap


class K:
    def __init__(self, nc, stack):
        self.nc, self.stack = nc, stack
        self.engs = {}
        for name, e in (("pe", nc.tensor), ("act", nc.scalar), ("dve", nc.vector),
                        ("pool", nc.gpsimd), ("sp", nc.sync)):
            sem = stack.enter_context(nc.semaphore("sem_" + name))
            self.engs[name] = Eng(name, e, sem)
        self.pe, self.act, self.dve, self.pool, self.sp = (
            self.engs[n] for n in ("pe", "act", "dve", "pool", "sp"))
        self.nsem = 5
        self.all_ds = []
        self.free_ds = []
        self.scopes = [(stack, [])]
        self.names = {}

    def new_sem(self, name, sw=False):
        if not sw and self.free_ds:
            return self.free_ds.pop()
        self.nsem += 1
        assert self.nsem <= 100, "out of semaphores"
        d = DSem(self.stack.enter_context(self.nc.semaphore("ds%d" % self.nsem)))
        d.sw = sw
        self.all_ds.append(d)
        return d

    def _reg(self, b):
        if b.ds is not None and not getattr(b.ds, "sw", False):
            self.scopes[-1][1].append(b.ds)
        return b

    @contextlib.contextmanager
    def scope(self):
        with contextlib.ExitStack() as st:
            self.scopes.append((st, []))
            try:
                yield
            finally:
                self.barrier()
                _, dss = self.scopes.pop()
                self.free_ds.extend(dss)

    def _uniq(self, name):
        n = self.names.get(name, 0)
        self.names[name] = n + 1
        return name if n == 0 else "%s__%d" % (name, n)

    def sbuf(self, name, shape, dtype, dma=False):
        name = self._uniq(name)
        t = self.scopes[-1][0].enter_context(self.nc.sbuf_tensor(name, list(shape), dtype))
        return self._reg(Buf(t, name, self.new_sem(name, sw=(dma == "sw")) if dma else None))

    def psum(self, name, shape, dtype):
        name = self._uniq(name)
        t = self.scopes[-1][0].enter_context(self.nc.psum_tensor(name, list(shape), dtype))
        return Buf(t, name)

    def dram(self, name, shape, dtype, dma=True):
        name = self._uniq(name)
        t = self.nc.dram_tensor(name, list(shape), dtype, kind="Internal").ap()
        return self._reg(Buf(t, name, self.new_sem(name, sw=(dma == "sw")) if dma else None))

    def wrap(self, t, name, dma=False):
        return self._reg(Buf(t, name, self.new_sem(name, sw=(dma == "sw")) if dma else None))

    @staticmethod
    def _max_per_sem(toks):
        best = {}
        for tok in toks:
            if tok is None:
                continue
            old = best.get(tok[0])
            if old is None or old[2] < tok[2]:
                best[tok[0]] = tok
        return best

    def split(self, whole, parts):
        for p in parts:
            p.last_w = whole.last_w
            p.extra_w = list(whole.extra_w)
            p.readers = dict(whole.readers)

    def merge(self, whole, parts):
        w = self._max_per_sem([whole.last_w] + list(whole.extra_w) +
                              [t for p in parts for t in ([p.last_w] + list(p.extra_w))])
        toks = list(w.values())
        whole.last_w = toks[0] if toks else None
        whole.extra_w = toks[1:]
        whole.readers = self._max_per_sem(list(whole.readers.values()) + [t for p in parts for t in p.readers.values()])

    def _wait(self, eng, tok):
        if tok is None:
            return
        key, sem, val = tok
        if eng.seen.get(key, 0) >= val:
            return
        if key == id(eng.sem) and eng.name == "pe":
            return
        eng.e.wait_ge(sem, val)
        eng.seen[key] = val
        eng.nwait += 1

    def _deps(self, eng, reads, writes):
        for b in reads:
            self._wait(eng, b.last_w)
            for tok in b.extra_w:
                self._wait(eng, tok)
        for b in writes:
            self._wait(eng, b.last_w)
            for tok in b.extra_w:
                self._wait(eng, tok)
            for tok in b.readers.values():
                self._wait(eng, tok)

    def op(self, eng, fn, reads=(), writes=(), inc=True):
        self._deps(eng, reads, writes)
        ins = fn(eng.e)
        eng.nins += 1
        if inc:
            ins.then_inc(eng.sem, 1)
            eng.tick += 1
            t = eng.tick
        else:
            t = eng.tick + 1
        tok = (id(eng.sem), eng.sem, t)
        for b in reads:
            old = b.readers.get(tok[0])
            if old is None or old[2] < t:
                b.readers[tok[0]] = tok
        for b in writes:
            b.last_w = tok
            b.extra_w = []
            b.readers = {}
        return ins

    def dma(self, q, pairs, slot, reads=(), writes=(), **kw):
        self._deps(q, reads, writes)
        ds = slot.ds
        for (o, i) in pairs:
            q.e.dma_start(out=o, in_=i, **kw).then_inc(ds.h, 16)
            ds.count += 1
            q.nins += 1
        tok = (id(ds.h), ds.h, 16 * ds.count)
        for b in reads:
            b.readers[tok[0]] = tok
        for b in writes:
            b.last_w = tok
            b.extra_w = []
            b.readers = {}
        return tok

    def barrier(self, engines=None):
        toks = []
        for e in self.engs.values():
            if e.tick > 0:
                toks.append((id(e.sem), e.sem, e.tick))
        for d in self.all_ds:
            if d.count > 0:
                toks.append((id(d.h), d.h, 16 * d.count))
        for e in (engines or self.engs.values()):
            for tok in toks:
                if tok[0] == id(e.sem):
                    continue
                self._wait(e, tok)

    def finish(self):
        self.barrier(engines=[self.sp])


class Cfg:
    def __init__(self, SEQ=8192, NSB=4, PAST=2048):
        self.SEQ, self.NSB, self.PAST = SEQ, NSB, PAST
        self.NMETA = 16
        self.TP = 16 + SEQ
        self.NS = NSB * 64
        self.NTOK = self.TP + self.NS
        self.CACHE = 16 + PAST
        self.blocks = [(0, 16)]
        self.blocks += [(16 + 128 * i, 128) for i in range(SEQ // 128)]
        sb = []
        r = self.TP
        while r < self.NTOK:
            n = min(128, self.NTOK - r)
            sb.append((r, n))
            r += n
        self.samp_blocks = sb
        g0 = [self.blocks[0]] + sb
        self.groups = []
        cur, tot = [], 0
        for b in g0:
            if tot + b[1] > 512:
                self.groups.append(cur)
                cur, tot = [], 0
            cur.append(b)
            tot += b[1]
        if cur:
            self.groups.append(cur)
        fb = self.blocks[1:]
        for i in range(0, len(fb), 4):
            self.groups.append(fb[i:i + 4])


def run_streams(makers, W):
    pending = list(makers)
    active, free = {}, list(range(W))
    while pending or active:
        while pending and free:
            sl = free.pop(0)
            active[sl] = pending.pop(0)(sl)
        for sl in sorted(active):
            try:
                next(active[sl])
            except StopIteration:
                del active[sl]
                free.append(sl)


def group_layout(grp):
    out, off = [], 0
    for (r, n) in grp:
        out.append((r, n, off))
        off += n
    return out, off


class Prog:
    def __init__(self, cfg, debug=()):
        self.cfg = cfg
        self.debug = set(debug)
        self.nc = bass.Bass("TRN2", target_bir_lowering=False)
        self.ins = {}
        self.outs = {}

    def inp(self, name, shape, dtype=F32):
        self.ins[name] = self.nc.dram_tensor(name, list(shape), dtype, kind="ExternalInput").ap()
        return self.ins[name]

    def out(self, name, shape, dtype=F32):
        self.outs[name] = self.nc.dram_tensor(name, list(shape), dtype, kind="ExternalOutput").ap()
        return self.outs[name]

    def build(self, upto=5, dump=()):
        cfg, nc = self.cfg, self.nc
        C = cfg
        I = {}
        for nm, shp in (("xin", [C.NTOK, D]), ("norms", [4, D]), ("q_norm", [384]), ("kv_norm", [256]),
                        ("gdn_norm", [128]), ("a_log", [4]), ("dt_bias", [4]), ("conv_w", [4, 1536]),
                        ("wgu1", [NFC * 128, 2 * KC * 128]), ("wd1", [128 * NFC, D]),
                        ("wgu2", [NFC * 128, 2 * KC * 128]), ("wd2", [128 * NFC, D]),
                        ("wf", [128, KC, 13 * 128]), ("wt", [128, KC, 1160]), ("wuq", [128, 3, 1024]),
                        ("wuk", [128, 2, 512]), ("wuv", [128, 2, 512]), ("wout", [128, KC, D]),
                        ("state_conv", [C.NSB, 3, 1536]), ("state_gdn", [C.NSB, 4, 128, 128]),
                        ("cache_ckv", [C.NSB, C.CACHE, 256]), ("cache_kr", [C.NSB, C.CACHE, 64])):
            I[nm] = self.inp(nm, shp)
        O = {}
        for nm, shp in (("y", [C.NTOK, D]), ("ckv", [C.NTOK, 256]), ("kr", [C.NTOK, 64]), ("pgdn", [4, 128, 128]),
                        ("pconv", [3, 1536]), ("sgdn", [C.NSB, 4, 128, 128]), ("sconv", [C.NSB, 3, 1536])):
            O[nm] = self.out(nm, shp)
        self.I, self.O = I, O
        with contextlib.ExitStack() as st:
            k = K(nc, st)
            self.k = k
            self.consts(k, I["norms"])
            self.proj_consts(k, I)
            self.wgu1_s = self.cast_weight(k, "wgu1_s", I["wgu1"], part_rows=[2 * 128, 6 * 128, 14 * 128])
            self.wd1_s = self.cast_weight(k, "wd1_s", I["wd1"])
            X1 = k.dram("X1", [C.NTOK, D], F32, dma=False)
            self.X1 = X1
            self.banks = [k.psum("bank%d" % i, [128, 512], F32) for i in range(8)]
            self.hT = k.sbuf("hT", [128, KC, 512], BF16)
            self.xnb = [k.sbuf("xnb%d" % i, [128, D], BF16) for i in range(4)]
            self.stat = [k.sbuf("stat%d" % i, [128, 4], F32) for i in range(4)]
            self.junk = k.sbuf("junk", [128, D], BF16)
            self.cnt = {"wgu": 0, "wd": 0, "xnb": 0, "stat": 0, "sg": 0}
            self.rope_tables(k)
            self.proj_scratch(k)
            with k.scope():
                self.ffn_bufs(k)
                self.phase_ffn1(k, I["xin"], X1)
            if upto >= 2:
                with k.scope():
                    self.phase_proj(k, I, O)
                    self.sbuf_left = nc.sbuf_bytes_remaining
            if upto >= 3:
                with k.scope():
                    self.phase_gdn(k, I, O)
                    self.sbuf_left = nc.sbuf_bytes_remaining
            if upto >= 4:
                with k.scope():
                    self.phase_mla_prompt(k, I, O)
                    self.sbuf_left4 = nc.sbuf_bytes_remaining
                with k.scope():
                    self.phase_mla_sample(k, I, O)
            if upto >= 5:
                with k.scope():
                    self.ffn_bufs(k)
                    self.phase_out(k, I, O)
            k.barrier()
            allscr = dict(self.S)
            allscr.update({"X1": X1, "COS2": self.COS2, "SIN2S": self.SIN2S})
            dsl = k.wrap(None, "dbgslot", dma=True)
            for nm in dump:
                src = allscr[nm]
                o = self.out("dbg_" + nm, list(src.t.shape), src.t.dtype)
                k.dma(k.sp, [(o, src.t)], dsl)
            k.finish()
            self.stats = {n: (e.nins, e.nwait) for n, e in k.engs.items()}
        return nc

    def consts(self, k, norms):
        nc = self.nc
        ones = k.sbuf("c_ones", [128, 128], F32)
        k.op(k.pool, lambda e: e.memset(ones[:], 1.0), writes=[ones])
        self.ones = ones
        identf = k.sbuf("c_identf", [128, 128], F32)
        k.op(k.pool, lambda e: e.affine_select(out=identf[:], in_=ones[:], pattern=[[1, 128]],
                                               compare_op=ALU.is_equal, fill=0.0, base=0,
                                               channel_multiplier=-1), reads=[ones], writes=[identf])
        identb = k.sbuf("c_identb", [128, 128], BF16)
        k.op(k.dve, lambda e: e.tensor_copy(out=identb[:], in_=identf[:]), reads=[identf], writes=[identb])
        self.identf, self.identb = identf, identb
        gT = k.sbuf("c_gT", [128, 4, KC], F32, dma=True)
        with nc.allow_non_contiguous_dma(reason="tiny one-time gain transpose load"):
            k.dma(k.sp, [(gT[:, r, :], norms[r, :].rearrange("(kc p) -> p kc", p=128)) for r in range(4)],
                  gT, writes=[gT])
        self.gT = gT
        gfin = k.sbuf("c_gfin", [128, D], F32, dma=True)
        k.dma(k.sp, [(gfin[:], norms[3:4, :].partition_broadcast(128))], gfin, writes=[gfin])
        self.gfin = gfin
        epsb = k.sbuf("c_eps", [128, 1], F32)
        k.op(k.pool, lambda e: e.memset(epsb[:], EPS), writes=[epsb])
        self.epsb = epsb
        l2b = k.sbuf("c_l2b", [128, 2], F32)
        k.op(k.pool, lambda e: e.memset(l2b[:, 0:1], 128.0 * 1e-6), writes=[l2b])
        k.op(k.pool, lambda e: e.memset(l2b[:, 1:2], 1e-6), writes=[l2b])
        self.l2b = l2b

    def cast_weight(self, k, name, src, part_rows=None):
        R, Cc = src.shape
        scr = k.dram(name, [R, Cc], BF16, dma="sw")
        bounds = [0] + list(part_rows or []) + [R]
        parts = []
        for a, b in zip(bounds[:-1], bounds[1:]):
            pb = scr if len(bounds) == 2 else k.wrap(scr.t, "%s_p%d" % (name, a), dma="sw")
            pairs = [(scr[r0:min(b, r0 + 256), :], src[r0:min(b, r0 + 256), :]) for r0 in range(a, b, 256)]
            k.dma(k.pool, pairs, pb, writes=[pb], max_dma_last_dim=2048 * 4)
            parts.append((a, b, pb))
        scr.parts = parts
        return scr

    def ffn_bufs(self, k):
        self.xt = [[k.sbuf("xt%d_%d" % (s, b), [128, D], F32, dma=True) for b in range(4)] for s in range(2)]
        self.xo = [k.sbuf("xo%d" % b, [128, D], F32, dma=True) for b in range(4)]
        self.actT = k.sbuf("actT", [128, NFC, 512], BF16)
        self.wgu_sl = [k.sbuf("wgu_sl%d" % i, [128, 2, KC, 128], BF16, dma=True) for i in range(3)]
        self.wd_sl = [k.sbuf("wd_sl%d" % i, [128, 11, 512], BF16, dma=True) for i in range(2)]
        self.sg = [k.sbuf("sg%d" % i, [128, 512], F32) for i in range(2)]

    def rstd_of(self, k, x_ap, n, Dn, reads):
        stt = self.stat[self.cnt["stat"] % 4]
        self.cnt["stat"] += 1
        junk = self.junk
        k.op(k.act, lambda e: e.activation(out=junk[0:n, 0:Dn], in_=x_ap, func=AF.Square,
                                           accum_out=stt[0:n, 0:1]), reads=reads, writes=[junk, stt])
        k.op(k.act, lambda e: e.activation(out=stt[0:n, 1:2], in_=stt[0:n, 0:1], func=AF.Sqrt,
                                           bias=self.epsb[0:n, :], scale=1.0 / Dn),
             reads=[stt, self.epsb], writes=[stt])
        k.op(k.dve, lambda e: e.reciprocal(out=stt[0:n, 2:3], in_=stt[0:n, 1:2]), reads=[stt], writes=[stt])
        return stt, stt[0:n, 2:3]

    def norm_A(self, k, xbuf, x_ap, n, nkc):
        Dn = nkc * 128
        stt, rstd = self.rstd_of(k, x_ap, n, Dn, [xbuf])
        xnb = self.xnb[self.cnt["xnb"] % len(self.xnb)]
        self.cnt["xnb"] += 1
        k.op(k.act, lambda e: e.activation(out=xnb[0:n, 0:Dn], in_=x_ap, func=AF.Copy, scale=rstd),
             reads=[xbuf, stt], writes=[xnb])
        return xnb

    def norm_B(self, k, xnb, n, nkc, g_ap, dstT, off, tbank, gbuf=None):
        tb = tbank.t[:].bitcast(BF16)
        for kc in range(nkc):
            k.op(k.pe, lambda e, kc=kc: e.transpose(out=tb[:, kc * 128:kc * 128 + n],
                                                    in_=xnb[0:n, kc * 128:(kc + 1) * 128],
                                                    identity=self.identb[0:n, 0:n]),
                 reads=[xnb, self.identb], writes=[tbank], inc=(kc == nkc - 1))
        src = tb[:, 0:nkc * 128].rearrange("p (k t) -> p k t", t=128)[:, :, 0:n]
        k.op(k.dve, lambda e: e.tensor_tensor(out=dstT[:, 0:nkc, off:off + n], in0=src,
                                              in1=g_ap.unsqueeze(2).to_broadcast([128, nkc, n]), op=ALU.mult),
             reads=[tbank, gbuf or self.gT], writes=[dstT])

    def norm_to_T(self, k, xbuf, x_ap, n, nkc, g_ap, dstT, off, tbank, gbuf=None):
        xnb = self.norm_A(k, xbuf, x_ap, n, nkc)
        self.norm_B(k, xnb, n, nkc, g_ap, dstT, off, tbank, gbuf)

    def norm_group(self, k, blocks, xt, g_ap, tbanks):
        xn = [self.norm_A(k, xt[bi], xt[bi][0:n, :], n, KC) for bi, (r0, n, off) in enumerate(blocks)]
        for bi, (r0, n, off) in enumerate(blocks):
            self.norm_B(k, xn[bi], n, KC, g_ap, self.hT, off, tbanks[bi])

    @staticmethod
    def part_of(scr, row):
        for (a, b, pb) in scr.parts:
            if a <= row < b:
                return pb
        raise AssertionError(row)

    def ffn_core(self, k, blocks, NT, wgu_s, wd_s, epilogue):
        hT, actT = self.hT, self.actT
        wg_v = wgu_s.t.rearrange("(f p) (g k j) -> f p g k j", p=128, g=2, k=KC)
        wd_v = wd_s.t.rearrange("(p f) c -> p f c", f=NFC)
        for fc in range(NFC):
            sl = self.wgu_sl[self.cnt["wgu"] % 3]
            self.cnt["wgu"] += 1
            k.dma(k.sp, [(sl[:], wg_v[fc])], sl, reads=[self.part_of(wgu_s, fc * 128)], writes=[sl])
            pg = self.banks[(2 * fc) % 4]
            pu = self.banks[(2 * fc + 1) % 4]
            for gu, pb in ((0, pg), (1, pu)):
                for kc in range(KC):
                    k.op(k.pe, lambda e, gu=gu, kc=kc, pb=pb: e.matmul(
                        pb[:, 0:NT], lhsT=sl[:, gu, kc, :], rhs=hT[:, kc, 0:NT],
                        start=(kc == 0), stop=(kc == KC - 1)),
                         reads=[sl, hT], writes=[pb], inc=(kc == KC - 1))
            sg = self.sg[self.cnt["sg"] % 2]
            self.cnt["sg"] += 1
            k.op(k.act, lambda e: e.activation(out=sg[:, 0:NT], in_=pg[:, 0:NT], func=AF.Silu),
                 reads=[pg], writes=[sg])
            k.op(k.dve, lambda e: e.tensor_tensor(out=actT[:, fc, 0:NT], in0=sg[:, 0:NT], in1=pu[:, 0:NT],
                                                  op=ALU.mult), reads=[sg, pu], writes=[actT])
        for half in range(2):
            for q in range(2):
                ws = self.wd_sl[self.cnt["wd"] % 2]
                self.cnt["wd"] += 1
                k.dma(k.sp, [(ws[:], wd_v[:, q * 11:(q + 1) * 11, half * 512:(half + 1) * 512])], ws,
                      reads=[wd_s], writes=[ws])
                for fi in range(11):
                    fc = q * 11 + fi
                    for bi, (r0, n, off) in enumerate(blocks):
                        last = (fi == 10 and bi == len(blocks) - 1) or fc == NFC - 1
                        k.op(k.pe, lambda e, fc=fc, fi=fi, bi=bi, n=n, off=off: e.matmul(
                            self.banks[4 + bi][0:n, 0:512], lhsT=actT[:, fc, off:off + n], rhs=ws[:, fi, :],
                            start=(fc == 0), stop=(fc == NFC - 1)),
                             reads=[actT, ws], writes=[self.banks[4 + bi]], inc=last)
            epilogue(half)

    def phase_ffn1(self, k, xin, X1):
        C = self.cfg
        for gi, grp in enumerate(C.groups):
            blocks, NT = group_layout(grp)
            xt = self.xt[gi % 2]
            for bi, (r0, n, off) in enumerate(blocks):
                k.dma(k.sp, [(xt[bi][0:n, :], xin[r0:r0 + n, :])], xt[bi], writes=[xt[bi]])
            self.norm_group(k, blocks, xt, self.gT[:, 0, :], self.banks[4:8])

            def epi(half, blocks=blocks, xt=xt):
                for bi, (r0, n, off) in enumerate(blocks):
                    cs = slice(half * 512, (half + 1) * 512)
                    k.op(k.dve, lambda e, bi=bi, n=n, cs=cs: e.scalar_tensor_tensor(
                        out=self.xo[bi][0:n, cs], in0=self.banks[4 + bi][0:n, 0:512], scalar=0.5,
                        in1=xt[bi][0:n, cs], op0=ALU.mult, op1=ALU.add),
                         reads=[self.banks[4 + bi], xt[bi]], writes=[self.xo[bi]])
                    if half == 1:
                        k.dma(k.sp, [(X1[r0:r0 + n, :], self.xo[bi][0:n, :])], self.xo[bi],
                              reads=[self.xo[bi]], writes=[X1])
            self.ffn_core(k, blocks, NT, self.wgu1_s, self.wd1_s, epi)
            if gi == 0:
                self.wgu2_s = self.cast_weight(k, "wgu2_s", self.I["wgu2"])
                self.wd2_s = self.cast_weight(k, "wd2_s", self.I["wd2"])


    def proj_consts(self, k, I):
        nc, C = self.nc, self.cfg
        gq = k.sbuf("c_gqT", [128, 3], F32, dma=True)
        cw = k.sbuf("c_cwT", [128, 12, 4], F32, dma=True)
        with nc.allow_non_contiguous_dma(reason="tiny one-time constant transposes"):
            k.dma(k.sp, [(gq[:], I["q_norm"].rearrange("(kc p) -> p kc", p=128))], gq, writes=[gq])
            k.dma(k.sp, [(cw[:, :, j], I["conv_w"][j, :].rearrange("(cc p) -> p cc", p=128)) for j in range(4)],
                  cw, writes=[cw])
        self.gqT, self.cwT = gq, cw
        gkv = k.sbuf("c_gkv", [128, 256], F32, dma=True)
        k.dma(k.sp, [(gkv[:], I["kv_norm"].partition_broadcast(128))], gkv, writes=[gkv])
        self.gkv = gkv
        ab = k.sbuf("c_ab", [128, 16], F32, dma=True)
        k.dma(k.sp, [(ab[:, 0:4], I["a_log"].partition_broadcast(128)),
                     (ab[:, 4:8], I["dt_bias"].partition_broadcast(128))], ab, writes=[ab])
        k.op(k.act, lambda e: e.activation(out=ab[:, 8:12], in_=ab[:, 0:4], func=AF.Exp), reads=[ab], writes=[ab])
        k.op(k.dve, lambda e: e.tensor_scalar(out=ab[:, 8:12], in0=ab[:, 8:12], scalar1=-1.0, scalar2=None,
                                              op0=ALU.mult), reads=[ab], writes=[ab])
        k.op(k.pool, lambda e: e.memset(ab[:, 12:16], 0.0), writes=[ab])
        self.abc = ab

    def load_resident(self, k, name, src, shape):
        t = k.sbuf(name, shape, BF16, dma="sw")
        k.dma(k.pool, [(t[:, a, :], src[:, a, :]) for a in range(shape[1])], t, writes=[t],
              max_dma_last_dim=8192)
        return t

    def rope_tables(self, k):
        C = self.cfg
        NPOS = C.TP + 64
        self.COS2 = k.dram("COS2", [64, NPOS], F32, dma=False)
        self.SIN2S = k.dram("SIN2S", [64, NPOS], F32, dma=False)
        I32 = mybir.dt.int32
        W = 2048
        with k.scope():
            pidx = k.sbuf("r_pidx", [64, 1], F32)
            for h in range(2):
                k.op(k.pool, lambda e, h=h: e.iota(pidx[32 * h:32 * h + 32, :], pattern=[[0, 1]], base=0,
                                                   channel_multiplier=1, allow_small_or_imprecise_dtypes=True),
                     writes=[pidx])
            inv = k.sbuf("r_inv", [64, 1], F32)
            k.op(k.act, lambda e: e.activation(out=inv[:], in_=pidx[:], func=AF.Exp,
                                               scale=-float(np.log(10000.0) / 32)), reads=[pidx], writes=[inv])
            sgn = k.sbuf("r_sgn", [64, 1], F32)
            k.op(k.pool, lambda e: e.memset(sgn[0:32, :], -1.0), writes=[sgn])
            k.op(k.pool, lambda e: e.memset(sgn[32:64, :], 1.0), writes=[sgn])
            T = {n: k.sbuf("r_" + n, [64, W], F32, dma=(n in ("co", "si"))) for n in
                 ("pos", "th", "t", "nf", "u", "w", "p", "su", "cu", "co", "si")}
            ni = k.sbuf("r_ni", [64, W], I32)
            chunks = [(c0, min(W, C.TP - c0), c0) for c0 in range(0, C.TP, W)] + [(C.TP, 64, C.CACHE)]
            HI = 6.28125
            LO = float(2 * np.pi - HI)
            sc = [1.0 / 362880, -1.0 / 5040, 1.0 / 120, -1.0 / 6]
            cc_ = [-1.0 / 3628800, 1.0 / 40320, -1.0 / 720, 1.0 / 24, -0.5]
            for (c0, w, p0) in chunks:
                def ts(out, in0, s1, s2=None, o0=ALU.mult, o1=None, rd=()):
                    kw = dict(out=out.t[:, 0:w], in0=in0.t[:, 0:w], scalar1=s1, scalar2=s2, op0=o0)
                    if o1 is not None:
                        kw["op1"] = o1
                    k.op(k.dve, lambda e: e.tensor_scalar(**kw), reads=[in0] + list(rd), writes=[out])

                def stt(out, in0, sca, in1, o0, o1):
                    k.op(k.dve, lambda e: e.scalar_tensor_tensor(out=out.t[:, 0:w], in0=in0.t[:, 0:w], scalar=sca,
                                                                 in1=in1.t[:, 0:w], op0=o0, op1=o1),
                         reads=[in0, in1], writes=[out])
                k.op(k.pool, lambda e: e.iota(T["pos"].t[:, 0:w], pattern=[[1, w]], base=p0, channel_multiplier=0,
                                              allow_small_or_imprecise_dtypes=True), writes=[T["pos"]])
                ts(T["th"], T["pos"], inv[:], rd=[inv])
                ts(T["t"], T["th"], float(1.0 / (2 * np.pi)))
                k.op(k.dve, lambda e: e.tensor_copy(out=ni[:, 0:w], in_=T["t"].t[:, 0:w]), reads=[T["t"]], writes=[ni])
                k.op(k.dve, lambda e: e.tensor_copy(out=T["nf"].t[:, 0:w], in_=ni[:, 0:w]), reads=[ni], writes=[T["nf"]])
                stt(T["u"], T["nf"], -HI, T["th"], ALU.mult, ALU.add)
                stt(T["u"], T["nf"], -LO, T["u"], ALU.mult, ALU.add)
                ts(T["u"], T["u"], 0.5)
                k.op(k.dve, lambda e: e.tensor_tensor(out=T["w"].t[:, 0:w], in0=T["u"].t[:, 0:w], in1=T["u"].t[:, 0:w],
                                                      op=ALU.mult), reads=[T["u"]], writes=[T["w"]])
                ts(T["p"], T["w"], sc[0])
                for c in sc[1:]:
                    stt(T["p"], T["p"], c, T["w"], ALU.add, ALU.mult)
                stt(T["su"], T["p"], 1.0, T["u"], ALU.add, ALU.mult)
                ts(T["p"], T["w"], cc_[0])
                for c in cc_[1:]:
                    stt(T["p"], T["p"], c, T["w"], ALU.add, ALU.mult)
                ts(T["cu"], T["p"], 1.0, o0=ALU.add)
                stt(T["si"], T["su"], 2.0, T["cu"], ALU.mult, ALU.mult)
                ts(T["si"], T["si"], sgn[:], rd=[sgn])
                k.op(k.dve, lambda e: e.tensor_tensor(out=T["p"].t[:, 0:w], in0=T["su"].t[:, 0:w], in1=T["su"].t[:, 0:w],
                                                      op=ALU.mult), reads=[T["su"]], writes=[T["p"]])
                ts(T["co"], T["p"], -2.0, 1.0, ALU.mult, ALU.add)
                k.dma(k.sp, [(self.COS2[:, c0:c0 + w], T["co"].t[:, 0:w])], T["co"], reads=[T["co"]], writes=[self.COS2])
                k.dma(k.sp, [(self.SIN2S[:, c0:c0 + w], T["si"].t[:, 0:w])], T["si"], reads=[T["si"]], writes=[self.SIN2S])

    def proj_scratch(self, k):
        C = self.cfg
        N = C.NTOK
        S = {}
        for nm, shp, dt in (("GQT", [4, 128, N], F32), ("GKT", [4, 128, N], F32), ("GK", [N, 4, 128], F32),
                            ("GV", [N, 4, 128], F32), ("GBc", [N, 8], F32), ("GBr", [8, N], F32),
                            ("ZS", [N, 512], F32), ("QT", [4, 128, N], BF16), ("QRT", [4, 64, N], BF16),
                            ("QN2", [4, N], F32), ("KT", [4, 128, N], BF16), ("KRT", [64, N], BF16),
                            ("K2", [4, N], F32), ("V1", [N, 4, 129], BF16)):
            S[nm] = k.dram(nm, shp, dt, dma=False)
        self.S = S
        self.MIXT = k.dram("MIXT", [8, 128, N], BF16, dma=False)
        S["MIXT"] = self.MIXT

    def group_segments(self, gi, blocks):
        C = self.cfg
        segs = []
        for (r0, n, off) in blocks:
            if r0 == 0:
                segs.append((off, n, "zero", 0))
            elif r0 < C.TP:
                if segs and segs[-1][2] == "prev":
                    o, L, kd, ix = segs[-1]
                    segs[-1] = (o, L + n, kd, ix)
                else:
                    segs.append((off, n, "prev", 0))
            else:
                for j in range(0, n, 64):
                    segs.append((off + j, 64, "state", (r0 - C.TP + j) // 64))
        return segs

    def phase_proj(self, k, I, O):
        C, S, nc = self.cfg, self.S, self.nc
        B = self.banks
        hT = self.hT
        WF = self.load_resident(k, "WF", I["wf"], [128, KC, 13 * 128])
        WT = self.load_resident(k, "WT", I["wt"], [128, KC, 1160])
        WUQ = self.load_resident(k, "WUQ", I["wuq"], [128, 3, 1024])
        WUK = self.load_resident(k, "WUK", I["wuk"], [128, 2, 512])
        WUV = self.load_resident(k, "WUV", I["wuv"], [128, 2, 512])
        xt = [k.sbuf("p_xt%d" % b, [128, D], F32, dma=True) for b in range(4)]
        rawx = k.sbuf("p_rawx", [128, 12, 528], F32, dma=True)
        rawx_cc = [k.wrap(rawx.t, "p_rawx_cc%d" % c) for c in range(12)]
        ccbuf = [(k.sbuf("p_cvu%d" % i, [128, 512], F32), k.sbuf("p_sqsd%d" % i, [128, 512], F32),
                  k.sbuf("p_un%d" % i, [128, 512], F32, dma=True)) for i in range(4)]
        halo = k.sbuf("p_halo", [128, 12, 3], F32, dma=True)
        k.op(k.pool, lambda e: e.memset(halo[:], 0.0), writes=[halo])
        cqnT = k.sbuf("p_cqnT", [128, 3, 512], BF16)
        cT = k.sbuf("p_cT", [128, 2, 512], BF16)
        cos_t = k.sbuf("p_cos", [64, 512], F32, dma=True)
        sin_t = k.sbuf("p_sin", [64, 512], F32, dma=True)
        ft = [k.sbuf("p_ft%d" % i, [128, 512], F32, dma=True) for i in range(2)]
        fb = [k.sbuf("p_fb%d" % i, [128, 512], BF16, dma=True) for i in range(4)]
        tmp = [k.sbuf("p_tmp%d" % i, [128, 512], F32) for i in range(7)]
        krsq_b = k.sbuf("p_krsq", [64, 512], F32)
        tm = [k.sbuf("p_tm%d" % i, [128, 512], F32, dma=True) for i in range(4)]
        c32 = [k.sbuf("p_c32_%d" % i, [128, 256], F32, dma=True) for i in range(2)]
        cb16 = k.sbuf("p_cb16", [128, 256], BF16)
        gbt = [k.sbuf("p_gb%d" % i, [128, 16], F32, dma=True) for i in range(2)]
        gbr = k.sbuf("p_gbr", [8, 512], F32, dma=True)
        krt = k.sbuf("p_krt", [128, 256], F32, dma=True)
        v1s = [k.sbuf("p_v1_%d" % i, [128, 4, 129], BF16, dma=True) for i in range(2)]
        for v in v1s:
            k.op(k.pool, lambda e, v=v: e.memset(v[:, :, 128:129], 1.0), writes=[v])
        rows = [k.sbuf("p_row%d" % i, [1, 512], F32, dma=True) for i in range(4)]
        rr = {"ft": 0, "fb": 0, "tmp": 0, "tm": 0, "c32": 0, "gb": 0, "v1": 0, "row": 0, "fm": 0}

        def nxt(lst, key):
            b = lst[rr[key] % len(lst)]
            rr[key] += 1
            return b

        def fm_bank():
            return nxt(B[0:2], "fm")

        last_prompt_group = max(gi for gi, g in enumerate(C.groups) if any(r0 < C.TP for (r0, n) in g))
        for gi, grp in enumerate(C.groups):
            blocks, NT = group_layout(grp)
            segs = self.group_segments(gi, blocks)
            for bi, (r0, n, off) in enumerate(blocks):
                k.dma(k.sp, [(xt[bi][0:n, :], self.X1[r0:r0 + n, :])], xt[bi], reads=[self.X1], writes=[xt[bi]])
            self.norm_group(k, blocks, xt, self.gT[:, 1, :], [B[6], B[7], B[6], B[7]])
            pr = []
            for (r0, n, off) in blocks:
                if r0 < C.TP:
                    pr.append((off, n, r0))
                else:
                    for j in range(0, n, 64):
                        pr.append((off + j, 64, C.TP))
            k.dma(k.sp, [(cos_t[:, o:o + n], self.COS2[:, c:c + n]) for (o, n, c) in pr], cos_t,
                  reads=[self.COS2], writes=[cos_t])
            k.dma(k.sp, [(sin_t[:, o:o + n], self.SIN2S[:, c:c + n]) for (o, n, c) in pr], sin_t,
                  reads=[self.SIN2S], writes=[sin_t])

            for bi, (r0, n, off) in enumerate(blocks):
                def tok_mm(bank, c0, c1):
                    for kc in range(KC):
                        k.op(k.pe, lambda e, kc=kc: e.matmul(bank[0:n, 0:c1 - c0], lhsT=hT[:, kc, off:off + n],
                                                             rhs=WT[:, kc, c0:c1], start=(kc == 0), stop=(kc == KC - 1)),
                             reads=[hT, WT], writes=[bank], inc=(kc == KC - 1))
                tok_mm(B[3], 0, 512)
                tok_mm(B[4], 512, 896)
                tok_mm(B[5], 896, 1160)
                zs = nxt(tm, "tm")
                k.op(k.act, lambda e: e.activation(out=zs[0:n, :], in_=B[3][0:n, :], func=AF.Silu), reads=[B[3]], writes=[zs])
                k.dma(k.sp, [(S["ZS"][r0:r0 + n, :], zs[0:n, :])], zs, reads=[zs], writes=[S["ZS"]])
                self.norm_to_T(k, B[4], B[4][0:n, 0:384], n, 3, self.gqT[:, :], cqnT, off, B[6], gbuf=self.gqT)
                stt_, rstd = self.rstd_of(k, B[5][0:n, 0:256], n, 256, [B[5]])
                cc = nxt(c32, "c32")
                k.op(k.dve, lambda e: e.scalar_tensor_tensor(out=cc[0:n, :], in0=B[5][0:n, 0:256], scalar=rstd,
                                                             in1=self.gkv[0:n, :], op0=ALU.mult, op1=ALU.mult),
                     reads=[B[5], stt_, self.gkv], writes=[cc])
                k.dma(k.sp, [(O["ckv"][r0:r0 + n, :], cc[0:n, :])], cc, reads=[cc])
                k.op(k.act, lambda e: e.copy(out=cb16[0:n, :], in_=cc[0:n, :]), reads=[cc], writes=[cb16])
                tb = B[6].t[:].bitcast(BF16)
                for kc in range(2):
                    k.op(k.pe, lambda e, kc=kc: e.transpose(out=tb[:, kc * 128:kc * 128 + n],
                                                            in_=cb16[0:n, kc * 128:(kc + 1) * 128],
                                                            identity=self.identb[0:n, 0:n]),
                         reads=[cb16, self.identb], writes=[B[6]], inc=(kc == 1))
                k.op(k.act, lambda e: e.copy(out=cT[:, :, off:off + n],
                                             in_=tb[:, 0:256].rearrange("p (k t) -> p k t", t=128)[:, :, 0:n]),
                     reads=[B[6]], writes=[cT])
                gb = nxt(gbt, "gb")
                k.op(k.dve, lambda e: e.tensor_tensor(out=gb[0:n, 8:12], in0=B[5][0:n, 256:260], in1=self.abc[0:n, 4:8],
                                                      op=ALU.add), reads=[B[5], self.abc], writes=[gb])
                k.op(k.act, lambda e: e.activation(out=gb[0:n, 8:12], in_=gb[0:n, 8:12], func=AF.Exp), reads=[gb], writes=[gb])
                k.op(k.act, lambda e: e.activation(out=gb[0:n, 8:12], in_=gb[0:n, 8:12], func=AF.Ln, bias=1.0),
                     reads=[gb], writes=[gb])
                k.op(k.dve, lambda e: e.tensor_tensor(out=gb[0:n, 0:4], in0=gb[0:n, 8:12], in1=self.abc[0:n, 8:12],
                                                      op=ALU.mult), reads=[gb, self.abc], writes=[gb])
                k.op(k.act, lambda e: e.activation(out=gb[0:n, 12:16], in_=B[5][0:n, 260:264], func=AF.Exp, scale=-1.0),
                     reads=[B[5]], writes=[gb])
                k.op(k.dve, lambda e: e.tensor_scalar(out=gb[0:n, 12:16], in0=gb[0:n, 12:16], scalar1=1.0, scalar2=None,
                                                      op0=ALU.add), reads=[gb], writes=[gb])
                k.op(k.dve, lambda e: e.reciprocal(out=gb[0:n, 4:8], in_=gb[0:n, 12:16]), reads=[gb], writes=[gb])
                k.dma(k.sp, [(S["GBc"][r0:r0 + n, :], gb[0:n, 0:8])], gb, reads=[gb], writes=[S["GBc"]])
                k.op(k.pe, lambda e: e.transpose(out=B[7][0:8, 0:n], in_=gb[0:n, 0:8], identity=self.identf[0:n, 0:n]),
                     reads=[gb, self.identf], writes=[B[7]])
                k.op(k.dve, lambda e: e.tensor_copy(out=gbr[:, off:off + n], in_=B[7][0:8, 0:n]), reads=[B[7]], writes=[gbr])
                for kc in range(2):
                    k.op(k.pe, lambda e, kc=kc: e.matmul(B[3][0:n, :], lhsT=cT[:, kc, off:off + n], rhs=WUV[:, kc, :],
                                                         start=(kc == 0), stop=(kc == 1)),
                         reads=[cT, WUV], writes=[B[3]], inc=(kc == 1))
                v1 = nxt(v1s, "v1")
                k.op(k.act, lambda e: e.copy(out=v1[0:n, :, 0:128], in_=B[3][0:n, :].rearrange("p (h d) -> p h d", d=128)),
                     reads=[B[3]], writes=[v1])
                k.dma(k.sp, [(S["V1"][r0:r0 + n, :, :], v1[0:n, :, :])], v1, reads=[v1], writes=[S["V1"]])
            r00 = blocks[0][0]
            runs = []
            for (r0, n, off) in blocks:
                if runs and runs[-1][2] + runs[-1][1] == r0:
                    runs[-1] = (runs[-1][0], runs[-1][1] + n, runs[-1][2])
                else:
                    runs.append((off, n, r0))

            def store_fm(dst3, stage, rows_=128):
                k.dma(k.sp, [(dst3[:, r:r + n], stage[0:rows_, o:o + n]) for (o, n, r) in runs], stage,
                      reads=[stage], writes=[])
            k.dma(k.sp, [(S["GBr"][:, r:r + n], gbr[:, o:o + n]) for (o, n, r) in runs], gbr, reads=[gbr])

            for si, (o, L, kind, ix) in enumerate(segs):
                xo = o + 3 * si
                if kind == "zero":
                    k.op(k.pool, lambda e, xo=xo: e.memset(rawx[:, :, xo:xo + 3], 0.0), writes=rawx_cc)
                elif kind == "prev":
                    k.op(k.pool, lambda e, xo=xo: e.tensor_copy(out=rawx[:, :, xo:xo + 3], in_=halo[:]),
                         reads=[halo], writes=rawx_cc)
                else:
                    with nc.allow_non_contiguous_dma(reason="3-row conv history, transposed on load"):
                        k.dma(k.sp, [(rawx[:, :, xo + t], I["state_conv"][ix, t, :].rearrange("(cc p) -> p cc", p=128))
                                     for t in range(3)], rawx, writes=rawx_cc)
            def cc_stream(cc_i, sl):
                bank, (cvu, sqsd, un) = B[sl], ccbuf[sl]
                rx = rawx_cc[cc_i]
                h = cc_i % 4
                for kc in range(KC):
                    k.op(k.pe, lambda e, kc=kc: e.matmul(bank[:, 0:NT], lhsT=WF[:, kc, cc_i * 128:(cc_i + 1) * 128],
                                                         rhs=hT[:, kc, 0:NT], start=(kc == 0), stop=(kc == KC - 1)),
                         reads=[WF, hT], writes=[bank], inc=(kc == KC - 1))
                for si, (o, L, kind, ix) in enumerate(segs):
                    xo = o + 3 * si
                    k.op(k.act, lambda e, o=o, L=L, xo=xo: e.copy(out=rawx[:, cc_i, xo + 3:xo + 3 + L], in_=bank[:, o:o + L]),
                         reads=[bank], writes=[rx])
                for si, (o, L, kind, ix) in enumerate(segs):
                    xo = o + 3 * si
                    k.op(k.dve, lambda e, o=o, L=L, xo=xo: e.tensor_scalar(
                        out=cvu[:, o:o + L], in0=rawx[:, cc_i, xo:xo + L], scalar1=self.cwT[:, cc_i, 0:1], scalar2=None,
                        op0=ALU.mult), reads=[rx, self.cwT], writes=[cvu])
                    for j in range(1, 4):
                        k.op(k.dve, lambda e, o=o, L=L, xo=xo, j=j: e.scalar_tensor_tensor(
                            out=cvu[:, o:o + L], in0=rawx[:, cc_i, xo + j:xo + j + L], scalar=self.cwT[:, cc_i, j:j + 1],
                            in1=cvu[:, o:o + L], op0=ALU.mult, op1=ALU.add), reads=[rx, self.cwT, cvu], writes=[cvu])
                k.op(k.act, lambda e: e.activation(out=cvu[:, 0:NT], in_=cvu[:, 0:NT], func=AF.Silu), reads=[cvu], writes=[cvu])
                src = cvu
                if cc_i < 8:
                    k.op(k.act, lambda e: e.activation(out=sqsd[:, 0:NT], in_=cvu[:, 0:NT], func=AF.Square), reads=[cvu], writes=[sqsd])
                    yield
                    ob_ = B[4 + sl % 2]
                    k.op(k.pe, lambda e: e.matmul(ob_[:, 0:NT], lhsT=self.ones[:, :], rhs=sqsd[:, 0:NT], start=True, stop=True),
                         reads=[self.ones, sqsd], writes=[ob_])
                    mul = 128.0 if cc_i < 4 else 1.0
                    k.op(k.act, lambda e: e.activation(out=sqsd[:, 0:NT], in_=ob_[:, 0:NT], func=AF.Sqrt, scale=mul,
                                                       bias=self.l2b[:, (0 if cc_i < 4 else 1):(1 if cc_i < 4 else 2)]),
                         reads=[ob_, self.l2b], writes=[sqsd])
                    k.op(k.dve, lambda e: e.reciprocal(out=sqsd[:, 0:NT], in_=sqsd[:, 0:NT]), reads=[sqsd], writes=[sqsd])
                    k.op(k.dve, lambda e: e.tensor_tensor(out=un[:, 0:NT], in0=cvu[:, 0:NT], in1=sqsd[:, 0:NT], op=ALU.mult),
                         reads=[cvu, sqsd], writes=[un])
                    store_fm(S["GQT" if cc_i < 4 else "GKT"].t[h], un)
                    src = un
                if cc_i >= 4:
                    yield
                    dst = S["GK" if cc_i < 8 else "GV"]
                    tbk = B[6 + cc_i % 2]
                    for bi, (r0, n, off) in enumerate(blocks):
                        k.op(k.pe, lambda e, n=n, off=off, bi=bi: e.transpose(out=tbk[0:n, bi * 128:(bi + 1) * 128], in_=src[:, off:off + n],
                                                                             identity=self.identf[:, :]),
                             reads=[src, self.identf], writes=[tbk], inc=(bi == len(blocks) - 1))
                    st_ = nxt(tm, "tm")
                    nb_ = len(blocks)
                    k.op(k.act, lambda e, st_=st_: e.copy(out=st_[:, 0:nb_ * 128], in_=tbk[:, 0:nb_ * 128]), reads=[tbk], writes=[st_])
                    k.dma(k.sp, [(dst[r0:r0 + n, h, :], st_[0:n, bi * 128:(bi + 1) * 128]) for bi, (r0, n, off) in enumerate(blocks)],
                          st_, reads=[st_], writes=[dst])
            run_streams([(lambda sl, c=c: cc_stream(c, sl)) for c in range(12)], 4)
            for si, (o, L, kind, ix) in enumerate(segs):
                xo = o + 3 * si
                if kind in ("prev", "zero"):
                    k.op(k.pool, lambda e, xo=xo, L=L: e.tensor_copy(out=halo[:], in_=rawx[:, :, xo + L:xo + L + 3]),
                         reads=rawx_cc, writes=[halo])
                else:
                    with nc.allow_non_contiguous_dma(reason="3-row conv state out"):
                        k.dma(k.sp, [(O["sconv"][ix, t, :].rearrange("(cc p) -> p cc", p=128), rawx[:, :, xo + L + t])
                                     for t in range(3)], rawx, reads=rawx_cc)
            if gi == last_prompt_group:
                with nc.allow_non_contiguous_dma(reason="3-row conv state out"):
                    k.dma(k.sp, [(O["pconv"][t, :].rearrange("(cc p) -> p cc", p=128), halo[:, :, t]) for t in range(3)],
                          halo, reads=[halo])

            def rope(bank_a, bank_b, out_t):
                t1 = nxt(tmp, "tmp")
                k.op(k.dve, lambda e: e.tensor_tensor(out=t1[0:64, 0:NT], in0=bank_a[0:64, 0:NT], in1=cos_t[:, 0:NT], op=ALU.mult),
                     reads=[bank_a, cos_t], writes=[t1])
                t2 = nxt(tmp, "tmp")
                k.op(k.dve, lambda e: e.tensor_tensor(out=t2[0:64, 0:NT], in0=bank_b[0:64, 0:NT], in1=sin_t[:, 0:NT], op=ALU.mult),
                     reads=[bank_b, sin_t], writes=[t2])
                k.op(k.pool, lambda e: e.tensor_tensor(out=out_t[0:64, 0:NT], in0=t1[0:64, 0:NT], in1=t2[0:64, 0:NT], op=ALU.add),
                     reads=[t1, t2], writes=[out_t])

            def fm_mm(bank, Wt, nkc, c0, M, rhsT):
                for kc in range(nkc):
                    k.op(k.pe, lambda e, kc=kc: e.matmul(bank[0:M, 0:NT], lhsT=Wt[:, kc, c0:c0 + M], rhs=rhsT[:, kc, 0:NT],
                                                         start=(kc == 0), stop=(kc == nkc - 1)),
                         reads=[Wt, rhsT], writes=[bank], inc=(kc == nkc - 1))
            ba, bb = B[0], B[1]
            fm_mm(ba, WF, KC, 12 * 128, 64, hT)
            fm_mm(bb, WF, KC, 12 * 128 + 64, 64, hT)
            kro = nxt(ft, "ft")
            rope(ba, bb, kro)
            krb = nxt(fb, "fb")
            k.op(k.act, lambda e: e.copy(out=krb[0:64, 0:NT], in_=kro[0:64, 0:NT]), reads=[kro], writes=[krb])
            store_fm(S["KRT"].t, krb, 64)
            krsq = krsq_b
            k.op(k.act, lambda e: e.activation(out=krsq[0:64, 0:NT], in_=kro[0:64, 0:NT], func=AF.Square), reads=[kro], writes=[krsq])
            for bi, (r0, n, off) in enumerate(blocks):
                k.op(k.pe, lambda e, n=n, off=off, bi=bi: e.transpose(out=B[7][0:n, bi * 64:(bi + 1) * 64], in_=kro[0:64, off:off + n],
                                                                     identity=self.identf[0:64, 0:64]),
                     reads=[kro, self.identf], writes=[B[7]], inc=(bi == len(blocks) - 1))
            k.op(k.dve, lambda e: e.tensor_copy(out=krt[:, 0:len(blocks) * 64], in_=B[7][:, 0:len(blocks) * 64]), reads=[B[7]], writes=[krt])
            k.dma(k.sp, [(O["kr"][r0:r0 + n, :], krt[0:n, bi * 64:(bi + 1) * 64]) for bi, (r0, n, off) in enumerate(blocks)],
                  krt, reads=[krt])

            for h in range(4):
                bq = fm_bank()
                fm_mm(bq, WUQ, 3, h * 256, 128, cqnT)
                qb = nxt(fb, "fb")
                k.op(k.act, lambda e: e.copy(out=qb[:, 0:NT], in_=bq[:, 0:NT]), reads=[bq], writes=[qb])
                store_fm(S["QT"].t[h], qb)
                qsq = nxt(tmp, "tmp")
                k.op(k.act, lambda e: e.activation(out=qsq[:, 0:NT], in_=bq[:, 0:NT], func=AF.Square), reads=[bq], writes=[qsq])
                ba, bb = fm_bank(), B[2]
                fm_mm(ba, WUQ, 3, h * 256 + 128, 64, cqnT)
                fm_mm(bb, WUQ, 3, h * 256 + 192, 64, cqnT)
                qr = nxt(tmp, "tmp")
                rope(ba, bb, qr)
                qrb = nxt(fb, "fb")
                k.op(k.act, lambda e: e.copy(out=qrb[0:64, 0:NT], in_=qr[0:64, 0:NT]), reads=[qr], writes=[qrb])
                store_fm(S["QRT"].t[h], qrb, 64)
                qrsq = nxt(tmp, "tmp")
                k.op(k.act, lambda e: e.activation(out=qrsq[0:64, 0:NT], in_=qr[0:64, 0:NT], func=AF.Square), reads=[qr], writes=[qrsq])
                k.op(k.pe, lambda e: e.matmul(B[7][0:1, 0:NT], lhsT=self.ones[:, 0:1], rhs=qsq[:, 0:NT], start=True, stop=False),
                     reads=[self.ones, qsq], writes=[B[7]], inc=False)
                k.op(k.pe, lambda e: e.matmul(B[7][0:1, 0:NT], lhsT=self.ones[0:64, 0:1], rhs=qrsq[0:64, 0:NT], start=False, stop=True),
                     reads=[self.ones, qrsq], writes=[B[7]])
                rw = nxt(rows, "row")
                k.op(k.dve, lambda e: e.tensor_copy(out=rw[:, 0:NT], in_=B[7][0:1, 0:NT]), reads=[B[7]], writes=[rw])
                k.dma(k.sp, [(S["QN2"][h:h + 1, r:r + n], rw[:, o:o + n]) for (o, n, r) in runs], rw, reads=[rw])
                bk = fm_bank()
                fm_mm(bk, WUK, 2, h * 128, 128, cT)
                kb = nxt(fb, "fb")
                k.op(k.act, lambda e: e.copy(out=kb[:, 0:NT], in_=bk[:, 0:NT]), reads=[bk], writes=[kb])
                store_fm(S["KT"].t[h], kb)
                ksq = nxt(tmp, "tmp")
                k.op(k.act, lambda e: e.activation(out=ksq[:, 0:NT], in_=bk[:, 0:NT], func=AF.Square), reads=[bk], writes=[ksq])
                k.op(k.pe, lambda e: e.matmul(B[7][0:1, 0:NT], lhsT=self.ones[:, 0:1], rhs=ksq[:, 0:NT], start=True, stop=False),
                     reads=[self.ones, ksq], writes=[B[7]], inc=False)
                k.op(k.pe, lambda e: e.matmul(B[7][0:1, 0:NT], lhsT=self.ones[0:64, 0:1], rhs=krsq[0:64, 0:NT], start=False, stop=True),
                     reads=[self.ones, krsq], writes=[B[7]])
                rw = nxt(rows, "row")
                k.op(k.dve, lambda e: e.tensor_copy(out=rw[:, 0:NT], in_=B[7][0:1, 0:NT]), reads=[B[7]], writes=[rw])
                k.dma(k.sp, [(S["K2"][h:h + 1, r:r + n], rw[:, o:o + n]) for (o, n, r) in runs], rw, reads=[rw])


    def gdn_consts(self, k):
        def sel(name, src_val, fill, pattern, cm, op):
            t = k.sbuf(name, [128, 128], F32)
            src = self.ones if src_val == 1.0 else self.zeros
            k.op(k.pool, lambda e: e.affine_select(out=t[:], in_=src[:], pattern=pattern, compare_op=op, fill=fill,
                                                   base=0, channel_multiplier=cm), reads=[src], writes=[t])
            return t
        self.zeros = k.sbuf("g_zeros", [128, 128], F32)
        k.op(k.pool, lambda e: e.memset(self.zeros[:], 0.0), writes=[self.zeros])
        BIG = 30000.0
        self.triu = sel("g_triu", 1.0, 0.0, [[1, 128]], -1, ALU.is_ge)
        self.mmin_incl = sel("g_mmin", 0.0, -BIG, [[1, 128]], -1, ALU.is_ge)
        self.strict01 = sel("g_st01", 1.0, 0.0, [[1, 128]], -1, ALU.is_gt)
        self.mmax_strict = sel("g_mmax", 0.0, BIG, [[-1, 128]], 1, ALU.is_gt)
        gg = k.sbuf("g_gain", [128, 128], F32, dma=True)
        k.dma(k.sp, [(gg[:], self.I["gdn_norm"].partition_broadcast(128))], gg, writes=[gg])
        self.ggain = gg

    def phase_gdn(self, k, I, O):
        C, S, B = self.cfg, self.S, self.banks
        self.gdn_consts(k)
        idf = self.identf
        seqs = [("p", 0, [(0, 16)] + [(16 + 128 * i, 128) for i in range(C.SEQ // 128)])]
        for b in range(C.NSB):
            seqs.append(("s", b, [(C.TP + 64 * b, 64)]))
        F = lambda nm, shp, dma=False: k.sbuf(nm, shp, F32, dma=dma)

        def FR(nm, shp):
            return k.sbuf(nm, shp, F32)
        NB = 2
        inp = [dict(qT=F("gi_qT%d" % i, [128, 4, 128], True), kT=F("gi_kT%d" % i, [128, 4, 128], True),
                    ktm=F("gi_ktm%d" % i, [128, 4, 128], True), vtm=F("gi_vtm%d" % i, [128, 4, 128], True),
                    grow=F("gi_grow%d" % i, [128, 4, 128], True), brow=F("gi_brow%d" % i, [128, 4, 128], True),
                    gbc=F("gi_gbc%d" % i, [128, 8], True), zs=F("gi_zs%d" % i, [128, 512], True)) for i in range(NB)]
        hand = [dict(WT=FR("gh_WT%d" % i, [128, 4, 128]), U0=F("gh_U0%d" % i, [128, 4, 128]),
                     QKd=FR("gh_QKd%d" % i, [128, 4, 128]), Kw=FR("gh_Kw%d" % i, [128, 4, 128]),
                     qe=FR("gh_qe%d" % i, [128, 4, 128]), gam=F("gh_gam%d" % i, [128, 4])) for i in range(NB)]
        Gb = F("g_Gb", [128, 4, 128]); gcol = F("g_gcol", [128, 4]); egc = F("g_egc", [128, 4]); bec = F("g_bec", [128, 4])
        wcol = F("g_wcol", [128, 4]); egr = F("g_egr", [128, 4, 128]); kbT = FR("g_kbT", [128, 4, 128])
        Kbe = FR("g_Kbe", [128, 4, 128]); bv = FR("g_bv", [128, 4, 128])
        dtmp = F("g_dtmp", [128, 4, 128]); DTi = F("g_DTi", [128, 4, 128]); DTs = F("g_DTs", [128, 4, 128]); Ds = F("g_Ds", [128, 4, 128])
        Np = [FR("g_N%d" % i, [128, 4, 128]) for i in range(2)]
        Bp = [FR("g_B%d" % i, [128, 4, 128]) for i in range(2)]
        R = FR("g_R", [128, 4, 128])
        kTr = FR("g_kTr", [128, 4, 128]); qTr = FR("g_qTr", [128, 4, 128]); Mr = FR("g_Mr", [128, 4, 128])
        g1_r = None
        NpH = [[k.wrap(Np[i].t, "g_N%d_%d" % (i, hf)) for hf in range(2)] for i in range(2)]
        BpH = [[k.wrap(Bp[i].t, "g_B%d_%d" % (i, hf)) for hf in range(2)] for i in range(2)]
        RH = [k.wrap(R.t, "g_R_%d" % hf) for hf in range(2)]
        M = [F("g_M%d" % i, [128, 4, 128], True) for i in range(2)]
        u = FR("g_u", [128, 4, 128])
        onr = F("g_onr", [128, 8]); og = F("g_og", [128, 512]); ogb = k.sbuf("g_ogb", [128, 512], BF16)
        mixs = [k.sbuf("g_mix%d" % i, [128, 4, 128], BF16, dma=True) for i in range(2)]
        g1_r = [kbT, Kbe, bv, Np[0], Np[1], Bp[0], Bp[1], R, kTr, qTr]

        def setmode(tiles, flag):
            for t_ in tiles:
                t_.rmode = flag
        ones_row = F("g_ones1", [128, 128])
        k.op(k.pool, lambda e: e.memset(ones_row[:], 1.0), writes=[ones_row])
        ci = 0
        mcur = 0
        for (kind, bidx, chunks) in seqs:
            Mc = M[mcur]
            if kind == "p":
                k.op(k.pool, lambda e, Mc=Mc: e.memset(Mc[:], 0.0), writes=[Mc])
            else:
                k.dma(k.sp, [(Mc[:, h, :], I["state_gdn"][bidx, h, :, :]) for h in range(4)], Mc, writes=[Mc])
            k.op(k.act, lambda e, Mc=Mc: e.activation(func=AF.Copy, out=Mr.t[:], in_=Mc.t[:]), reads=[Mc], writes=[Mr])
            pend = None
            def g1_gen(item_, ci_, res_):
                r0, Lr = item_
                X, H = inp[ci_ % NB], hand[ci_ % NB]
                setmode(g1_r + [H["WT"], H["QKd"], H["Kw"], H["qe"]], True)
                L = Lr
                r1 = r0 + L
                if Lr < 128:
                    for nm in ("ktm", "vtm", "gbc"):
                        k.op(k.pool, lambda e, nm=nm: e.memset(X[nm][:], 0.0), writes=[X[nm]])
                k.dma(k.sp, [(X["qT"][:, :, 0:L], S["GQT"].t[:, :, r0:r1].rearrange("h p t -> p h t"))], X["qT"], writes=[X["qT"]])
                k.dma(k.sp, [(X["kT"][:, :, 0:L], S["GKT"].t[:, :, r0:r1].rearrange("h p t -> p h t"))], X["kT"], writes=[X["kT"]])
                k.dma(k.sp, [(X["ktm"][0:L, :, :], S["GK"].t[r0:r1, :, :])], X["ktm"], writes=[X["ktm"]])
                k.dma(k.sp, [(X["vtm"][0:L, :, :], S["GV"].t[r0:r1, :, :])], X["vtm"], writes=[X["vtm"]])
                k.dma(k.sp, [(X["grow"][:, h, 0:L], S["GBr"].t[h, r0:r1].partition_broadcast(128)) for h in range(4)],
                      X["grow"], writes=[X["grow"]])
                k.dma(k.sp, [(X["brow"][:, h, 0:L], S["GBr"].t[4 + h, r0:r1].partition_broadcast(128)) for h in range(4)],
                      X["brow"], writes=[X["brow"]])
                k.dma(k.sp, [(X["gbc"][0:L, :], S["GBc"].t[r0:r1, :])], X["gbc"], writes=[X["gbc"]])
                k.dma(k.sp, [(X["zs"][0:L, :], S["ZS"].t[r0:r1, :])], X["zs"], writes=[X["zs"]])
                if Lr < 128:
                    for nm in ("qT", "kT", "grow", "brow"):
                        k.op(k.pool, lambda e, nm=nm: e.memset(X[nm][:, :, Lr:128], 0.0), writes=[X[nm]])
                L = 128
                qT, kT, ktm, vtm, grow, brow, gbc = (X[n] for n in ("qT", "kT", "ktm", "vtm", "grow", "brow", "gbc"))
                for h in range(4):
                    k.op(k.dve, lambda e, h=h: e.tensor_tensor_scan(out=Gb[:, h, 0:L], data0=ones_row[:, 0:L], data1=grow[:, h, 0:L],
                                                                   initial=0.0, op0=ALU.mult, op1=ALU.add),
                         reads=[ones_row, grow], writes=[Gb])
                k.op(k.pe, lambda e: e.matmul(B[0][0:L, 0:4], lhsT=self.triu[0:L, 0:L], rhs=gbc[0:L, 0:4], start=True, stop=True),
                     reads=[self.triu, gbc], writes=[B[0]])
                k.op(k.dve, lambda e: e.tensor_copy(out=gcol[0:L, :], in_=B[0][0:L, 0:4]), reads=[B[0]], writes=[gcol])
                k.op(k.act, lambda e: e.activation(out=egc[0:L, :], in_=gcol[0:L, :], func=AF.Exp), reads=[gcol], writes=[egc])
                k.op(k.dve, lambda e: e.tensor_tensor(out=bec[0:L, :], in0=egc[0:L, :], in1=gbc[0:L, 4:8], op=ALU.mult),
                     reads=[egc, gbc], writes=[bec])
                k.op(k.act, lambda e: e.activation(out=H["gam"][:, :], in_=Gb[:, :, L - 1], func=AF.Exp), reads=[Gb], writes=[H["gam"]])
                k.op(k.dve, lambda e: e.tensor_tensor(out=wcol[0:L, :], in0=Gb[0:L, :, L - 1], in1=gcol[0:L, :], op=ALU.subtract),
                     reads=[Gb, gcol], writes=[wcol])
                k.op(k.act, lambda e: e.activation(out=wcol[0:L, :], in_=wcol[0:L, :], func=AF.Exp), reads=[wcol], writes=[wcol])
                k.op(k.act, lambda e: e.activation(out=egr[:, :, 0:L], in_=Gb[:, :, 0:L], func=AF.Exp), reads=[Gb], writes=[egr])
                k.op(k.dve, lambda e: e.tensor_tensor(out=H["qe"][:, :, 0:L], in0=qT[:, :, 0:L], in1=egr[:, :, 0:L], op=ALU.mult),
                     reads=[qT, egr], writes=[H["qe"]])
                k.op(k.dve, lambda e: e.tensor_tensor(out=kbT[:, :, 0:L], in0=kT[:, :, 0:L], in1=brow[:, :, 0:L], op=ALU.mult),
                     reads=[kT, brow], writes=[kbT])
                k.op(k.act, lambda e: e.activation(func=AF.Copy, out=kTr[:, :, 0:L], in_=kT[:, :, 0:L]), reads=[kT], writes=[kTr])
                k.op(k.act, lambda e: e.activation(func=AF.Copy, out=qTr[:, :, 0:L], in_=qT[:, :, 0:L]), reads=[qT], writes=[qTr])
                bc3 = lambda col: col.unsqueeze(2).to_broadcast([L, 4, 128])
                k.op(k.dve, lambda e: e.tensor_tensor(out=Kbe[0:L], in0=ktm[0:L], in1=bc3(bec[0:L, :]), op=ALU.mult),
                     reads=[ktm, bec], writes=[Kbe])
                k.op(k.dve, lambda e: e.tensor_tensor(out=H["Kw"][0:L], in0=ktm[0:L], in1=bc3(wcol[0:L, :]), op=ALU.mult),
                     reads=[ktm, wcol], writes=[H["Kw"]])
                k.op(k.dve, lambda e: e.tensor_tensor(out=bv[0:L], in0=vtm[0:L], in1=bc3(gbc[0:L, 4:8]), op=ALU.mult),
                     reads=[vtm, gbc], writes=[bv])
                yield
                for h in range(4):
                    k.op(k.dve, lambda e, h=h: e.scalar_tensor_tensor(out=dtmp[0:L, h, 0:L], in0=Gb[0:L, h, 0:L], scalar=gcol[0:L, h:h + 1],
                                                                     in1=self.mmin_incl[0:L, 0:L], op0=ALU.subtract, op1=ALU.min),
                         reads=[Gb, gcol, self.mmin_incl], writes=[dtmp])
                k.op(k.act, lambda e: e.activation(out=DTi[0:L, :, 0:L], in_=dtmp[0:L, :, 0:L], func=AF.Exp), reads=[dtmp], writes=[DTi])
                k.op(k.pool, lambda e: e.tensor_tensor(out=DTs[0:L, :, 0:L], in0=DTi[0:L, :, 0:L],
                                                       in1=self.strict01[0:L, 0:L].unsqueeze(1).to_broadcast([L, 4, L]), op=ALU.mult),
                     reads=[DTi, self.strict01], writes=[DTs])
                for h in range(4):
                    k.op(k.dve, lambda e, h=h: e.scalar_tensor_tensor(out=dtmp[0:L, h, 0:L], in0=Gb[0:L, h, 0:L], scalar=gcol[0:L, h:h + 1],
                                                                     in1=self.mmax_strict[0:L, 0:L], op0=ALU.subtract, op1=ALU.max),
                         reads=[Gb, gcol, self.mmax_strict], writes=[dtmp])
                k.op(k.act, lambda e: e.activation(out=Ds[0:L, :, 0:L], in_=dtmp[0:L, :, 0:L], func=AF.Exp, scale=-1.0),
                     reads=[dtmp], writes=[Ds])
                yield
                N0, B0 = Np[0], Bp[0]
                for h in range(4):
                    cs = slice(h * 128, h * 128 + L)
                    k.op(k.pe, lambda e, h=h, cs=cs: e.matmul(B[0][0:L, cs], lhsT=kbT[:, h, 0:L], rhs=kTr[:, h, 0:L], start=True, stop=True),
                         reads=[kbT, kTr], writes=[B[0]])
                    k.op(k.pe, lambda e, h=h, cs=cs: e.matmul(B[1][0:L, cs], lhsT=kTr[:, h, 0:L], rhs=kbT[:, h, 0:L], start=True, stop=True),
                         reads=[kbT, kTr], writes=[B[1]])
                    k.op(k.pe, lambda e, h=h, cs=cs: e.matmul(B[2][0:L, cs], lhsT=kTr[:, h, 0:L], rhs=qTr[:, h, 0:L], start=True, stop=True),
                         reads=[kTr, qTr], writes=[B[2]])
                v4 = lambda bank: bank.t[:].rearrange("p (h t) -> p h t", t=128)[0:L, :, 0:L]
                k.op(k.dve, lambda e: e.scalar_tensor_tensor(out=N0[0:L, :, 0:L], in0=v4(B[0]), scalar=-1.0, in1=Ds[0:L, :, 0:L],
                                                             op0=ALU.mult, op1=ALU.mult), reads=[B[0], Ds], writes=[N0])
                k.op(k.dve, lambda e: e.scalar_tensor_tensor(out=B0[0:L, :, 0:L], in0=v4(B[1]), scalar=-1.0, in1=DTs[0:L, :, 0:L],
                                                             op0=ALU.mult, op1=ALU.mult), reads=[B[1], DTs], writes=[B0])
                k.op(k.dve, lambda e: e.tensor_tensor(out=H["QKd"][0:L, :, 0:L], in0=v4(B[2]), in1=DTi[0:L, :, 0:L], op=ALU.mult),
                     reads=[B[2], DTi], writes=[H["QKd"]])
                k.op(k.dve, lambda e: e.tensor_tensor(out=R[0:L, :, 0:L], in0=B0.f32((slice(0, L), slice(None), slice(0, L))),
                                                       in1=idf[0:L, 0:L].unsqueeze(1).to_broadcast([L, 4, L]), op=ALU.add),
                     reads=[B0, idf], writes=[R])
                J = 0
                while (1 << (J + 1)) < L:
                    J += 1
                def sq_stream(hs, sl):
                    sqb, rb = (B[3], B[4])[sl], (B[2], B[1])[sl]
                    hsl = slice(hs[0], hs[-1] + 1)
                    p2 = lambda bank, c0: bank.t[:, c0:c0 + 256].rearrange("p (h t) -> p h t", t=128)[0:L, :, 0:L]
                    for j in range(1, J + 1):
                        Nn, Bn, No, Bo = NpH[j % 2][sl], BpH[j % 2][sl], NpH[(j - 1) % 2][sl], BpH[(j - 1) % 2][sl]
                        Nn_t, Bn_t, No_t, Bo_t = Np[j % 2], Bp[j % 2], Np[(j - 1) % 2], Bp[(j - 1) % 2]
                        for hl, h in enumerate(hs):
                            k.op(k.pe, lambda e, h=h, hl=hl: e.matmul(sqb[0:L, hl * 128:hl * 128 + L], lhsT=Bo_t[0:L, h, 0:L], rhs=No_t[0:L, h, 0:L],
                                                                     start=True, stop=True), reads=[Bo, No], writes=[sqb])
                            if j < J:
                                k.op(k.pe, lambda e, h=h, hl=hl: e.matmul(sqb[0:L, 256 + hl * 128:256 + hl * 128 + L], lhsT=No_t[0:L, h, 0:L],
                                                                         rhs=Bo_t[0:L, h, 0:L], start=True, stop=True), reads=[Bo, No], writes=[sqb])
                        k.op(k.act, lambda e: e.activation(func=AF.Copy, out=Nn_t[0:L, hsl, 0:L], in_=p2(sqb, 0)), reads=[sqb], writes=[Nn])
                        if j < J:
                            k.op(k.act, lambda e: e.activation(func=AF.Copy, out=Bn_t[0:L, hsl, 0:L], in_=p2(sqb, 256)), reads=[sqb], writes=[Bn])
                        yield
                        for hl, h in enumerate(hs):
                            k.op(k.pe, lambda e, h=h, hl=hl: e.matmul(rb[0:L, hl * 128:hl * 128 + L], lhsT=Nn_t[0:L, h, 0:L], rhs=R[0:L, h, 0:L],
                                                                     start=True, stop=True), reads=[Nn, RH[sl]], writes=[rb])
                        k.op(k.dve, lambda e: e.tensor_tensor(out=R[0:L, hsl, 0:L], in0=p2(rb, 0), in1=R.f32((slice(0, L), hsl, slice(0, L))), op=ALU.add),
                             reads=[rb, RH[sl]], writes=[RH[sl]])
                        yield
                pairs_ = ((Np[0], NpH[0]), (Np[1], NpH[1]), (Bp[0], BpH[0]), (Bp[1], BpH[1]), (R, RH))
                for whole, halves in pairs_:
                    k.split(whole, halves)
                alive = [sq_stream(hs, sl) for sl, hs in enumerate(((0, 1), (2, 3)))]
                while alive:
                    for sg in list(alive):
                        try:
                            next(sg)
                        except StopIteration:
                            alive.remove(sg)
                    yield
                for whole, halves in pairs_:
                    k.merge(whole, halves)
                for h in range(4):
                    k.op(k.pe, lambda e, h=h: e.matmul(B[0][:, h * 128:h * 128 + L], lhsT=Kbe[0:L, h, :], rhs=R[0:L, h, 0:L], start=True, stop=True),
                         reads=[Kbe, R], writes=[B[0]])
                    k.op(k.pe, lambda e, h=h: e.matmul(B[1][0:L, h * 128:(h + 1) * 128], lhsT=R[0:L, h, 0:L], rhs=bv[0:L, h, :], start=True, stop=True),
                         reads=[R, bv], writes=[B[1]])
                k.op(k.act, lambda e: e.activation(func=AF.Copy, out=H["WT"][:, :, 0:L], in_=B[0].t[:].rearrange("p (h t) -> p h t", t=128)[:, :, 0:L]),
                     reads=[B[0]], writes=[H["WT"]])
                k.op(k.act, lambda e: e.copy(out=H["U0"][0:L], in_=B[1].t[:].rearrange("p (h t) -> p h t", t=128)[0:L]),
                     reads=[B[1]], writes=[H["U0"]])
                res_[0] = (r0, Lr, X, H)

            def g2_gen(pend_, mc_):
                (q0, Lo, Xq, Hq) = pend_
                Lq = 128
                Mo, Mn = M[mc_], M[1 - mc_]
                setmode([u, Mr, Hq["WT"], Hq["QKd"], Hq["Kw"], Hq["qe"]], True)
                for h in range(4):
                    k.op(k.pe, lambda e, h=h: e.matmul(B[5][0:Lq, h * 128:(h + 1) * 128], lhsT=Hq["WT"][:, h, 0:Lq], rhs=Mr[:, h, :], start=True, stop=True),
                         reads=[Hq["WT"], Mr], writes=[B[5]])
                k.op(k.dve, lambda e: e.scalar_tensor_tensor(out=u[0:Lq], in0=B[5].t[:].rearrange("p (h t) -> p h t", t=128)[0:Lq], scalar=-1.0,
                                                             in1=Hq["U0"][0:Lq], op0=ALU.mult, op1=ALU.add),
                     reads=[B[5], Hq["U0"]], writes=[u])
                yield
                for h in range(4):
                    k.op(k.pe, lambda e, h=h: e.matmul(B[6][:, h * 128:(h + 1) * 128], lhsT=Hq["Kw"][0:Lq, h, :], rhs=u[0:Lq, h, :], start=True, stop=True),
                         reads=[Hq["Kw"], u], writes=[B[6]])
                for h in range(4):
                    k.op(k.pe, lambda e, h=h: e.matmul(B[7][0:Lq, h * 128:(h + 1) * 128], lhsT=Hq["qe"][:, h, 0:Lq], rhs=Mr[:, h, :], start=True, stop=False),
                         reads=[Hq["qe"], Mr], writes=[B[7]], inc=False)
                    k.op(k.pe, lambda e, h=h: e.matmul(B[7][0:Lq, h * 128:(h + 1) * 128], lhsT=Hq["QKd"][0:Lq, h, 0:Lq], rhs=u[0:Lq, h, :], start=False, stop=True),
                         reads=[Hq["QKd"], u], writes=[B[7]])
                for h in range(4):
                    k.op(k.dve, lambda e, h=h: e.scalar_tensor_tensor(out=Mn[:, h, :], in0=Mo[:, h, :], scalar=Hq["gam"][:, h:h + 1],
                                                                     in1=B[6][:, h * 128:(h + 1) * 128], op0=ALU.mult, op1=ALU.add),
                         reads=[Mo, Hq["gam"], B[6]], writes=[Mn])
                k.op(k.act, lambda e: e.activation(func=AF.Copy, out=Mr.t[:], in_=Mn.t[:]), reads=[Mn], writes=[Mr])
                yield
                o3 = B[7].t[:].rearrange("p (h d) -> p h d", d=128)
                for h in range(4):
                    k.op(k.act, lambda e, h=h: e.activation(out=self.junk[0:Lo, 0:128], in_=B[7][0:Lo, h * 128:(h + 1) * 128], func=AF.Square,
                                                           accum_out=onr[0:Lo, h:h + 1]), reads=[B[7]], writes=[self.junk, onr])
                k.op(k.act, lambda e: e.activation(out=onr[0:Lo, 4:8], in_=onr[0:Lo, 0:4], func=AF.Sqrt, bias=self.epsb[0:Lo, :], scale=1.0 / 128),
                     reads=[onr, self.epsb], writes=[onr])
                k.op(k.dve, lambda e: e.reciprocal(out=onr[0:Lo, 4:8], in_=onr[0:Lo, 4:8]), reads=[onr], writes=[onr])
                og3 = og.t[:].rearrange("p (h d) -> p h d", d=128)
                k.op(k.dve, lambda e: e.tensor_tensor(out=og3[0:Lo], in0=o3[0:Lo], in1=onr[0:Lo, 4:8].unsqueeze(2).to_broadcast([Lo, 4, 128]), op=ALU.mult),
                     reads=[B[7], onr], writes=[og])
                k.op(k.pool, lambda e: e.tensor_tensor(out=og3[0:Lo], in0=og3[0:Lo], in1=self.ggain[0:Lo, :].unsqueeze(1).to_broadcast([Lo, 4, 128]), op=ALU.mult),
                     reads=[og, self.ggain], writes=[og])
                k.op(k.pool, lambda e: e.tensor_tensor(out=ogb[0:Lo, :], in0=og[0:Lo, :], in1=Xq["zs"][0:Lo, :], op=ALU.mult),
                     reads=[og, Xq["zs"]], writes=[ogb])
                yield
                tb = B[5].t[:].bitcast(BF16)
                for h in range(4):
                    k.op(k.pe, lambda e, h=h: e.transpose(out=tb[:, h * 128:h * 128 + Lo], in_=ogb[0:Lo, h * 128:(h + 1) * 128], identity=self.identb[0:Lo, 0:Lo]),
                         reads=[ogb, self.identb], writes=[B[5]], inc=(h == 3))
                mx = mixs[mc_]
                k.op(k.act, lambda e: e.copy(out=mx[:, :, 0:Lo], in_=tb[:, 0:512].rearrange("p (h t) -> p h t", t=128)[:, :, 0:Lo]),
                     reads=[B[5]], writes=[mx])
                k.dma(k.sp, [(self.MIXT.t[0:4, :, q0:q0 + Lo].rearrange("h p t -> p h t"), mx[:, :, 0:Lo])], mx, reads=[mx], writes=[self.MIXT])

            for item in chunks + [None]:
                res = [None]
                gens = []
                if item is not None:
                    gens.append(g1_gen(item, ci, res))
                    ci += 1
                if pend is not None:
                    gens.append(g2_gen(pend, mcur))
                    mcur = 1 - mcur
                while gens:
                    for gg in list(gens):
                        try:
                            next(gg)
                        except StopIteration:
                            gens.remove(gg)
                pend = res[0]
            Mf = M[mcur]
            dst = O["pgdn"] if kind == "p" else O["sgdn"][bidx]
            k.dma(k.sp, [(dst[h, :, :], Mf[:, h, :]) for h in range(4)], Mf, reads=[Mf])
            mcur = 1 - mcur


    SM = (128 + 64) ** -0.5

    def attn_finish(self, k, accb, nq, h, rows0, ob, rec):
        B = self.banks
        k.op(k.dve, lambda e: e.reciprocal(out=rec[0:nq, :], in_=accb[0:nq, 128:129]), reads=[accb], writes=[rec])
        k.op(k.act, lambda e: e.activation(out=ob[0:nq, :], in_=accb[0:nq, 0:128], func=AF.Copy, scale=rec[0:nq, :]),
             reads=[accb, rec], writes=[ob])
        tb = B[6].t[:].bitcast(BF16)
        k.op(k.pe, lambda e: e.transpose(out=tb[:, 0:nq], in_=ob[0:nq, :], identity=self.identb[0:nq, 0:nq]),
             reads=[ob, self.identb], writes=[B[6]])
        st = self.a_st[self.a_cnt % 2]
        self.a_cnt += 1
        k.op(k.dve, lambda e: e.tensor_copy(out=st[:, 0:nq], in_=tb[:, 0:nq]), reads=[B[6]], writes=[st])
        k.dma(k.sp, [(self.MIXT.t[4 + h, :, rows0:rows0 + nq], st[:, 0:nq])], st, reads=[st], writes=[self.MIXT])

    def stab_row(self, k, QR, ncol, qn2_src, k2max, tmp65):
        for c0 in range(0, ncol, 2048):
            w = min(2048, ncol - c0)
            k.dma(k.sp, [(tmp65[64:65, 0:w], qn2_src[:, c0:c0 + w])], tmp65, writes=[tmp65])
            k.op(k.dve, lambda e: e.tensor_scalar(out=tmp65[64:65, 0:w], in0=tmp65[64:65, 0:w], scalar1=k2max[64:65, 0:1],
                                                  scalar2=None, op0=ALU.mult), reads=[tmp65, k2max], writes=[tmp65])
            k.op(k.act, lambda e: e.activation(out=tmp65[64:65, 0:w], in_=tmp65[64:65, 0:w], func=AF.Sqrt),
                 reads=[tmp65], writes=[tmp65])
            k.op(k.dve, lambda e, c0=c0: e.tensor_scalar(out=QR[64:65, c0:c0 + w], in0=tmp65[64:65, 0:w], scalar1=-1.0, scalar2=None,
                                                         op0=ALU.mult), reads=[tmp65], writes=[QR])

    def row_max(self, k, src_row, ncol, k2m, tmp65, src_buf=None):
        for ci, c0 in enumerate(range(0, ncol, 2048)):
            w = min(2048, ncol - c0)
            k.dma(k.sp, [(tmp65[64:65, 0:w], src_row[:, c0:c0 + w])], tmp65, reads=([src_buf] if src_buf else []), writes=[tmp65])
            dst = k2m[64:65, 0:1] if ci == 0 else k2m[64:65, 1:2]
            k.op(k.dve, lambda e, dst=dst: e.reduce_max(out=dst, in_=tmp65[64:65, 0:w], axis=mybir.AxisListType.X),
                 reads=[tmp65], writes=[k2m])
            if ci > 0:
                k.op(k.dve, lambda e: e.tensor_tensor(out=k2m[64:65, 0:1], in0=k2m[64:65, 0:1], in1=k2m[64:65, 1:2], op=ALU.max),
                     reads=[k2m], writes=[k2m])

    def attn_bufs(self, k, TWk, TWq, nvt):
        A = dict(KT=k.sbuf("a_KT", [128, TWk], BF16, dma=True), KR=k.sbuf("a_KR", [65, TWk], BF16, dma=True),
                 QT=k.sbuf("a_QT", [128, TWq], BF16, dma=True), QR=k.sbuf("a_QR", [65, TWq], BF16, dma=True),
                 V=k.sbuf("a_V", [128, nvt, 129], BF16, dma=True), t65=k.sbuf("a_t65", [65, 2048], F32, dma=True),
                 k2m=k.sbuf("a_k2m", [65, 2], F32), PT=[k.sbuf("a_PT%d" % i, [128, 512], BF16) for i in range(3)],
                 ob=k.sbuf("a_ob", [128, 128], BF16), rec=k.sbuf("a_rec", [128, 1], F32))
        self.a_st = [k.sbuf("a_st%d" % i, [128, 128], BF16, dma=True) for i in range(2)]
        self.a_cnt = 0
        k.op(k.pool, lambda e: e.memset(A["KR"][64:65, :], 1.0), writes=[A["KR"]])
        return A

    def make_attend(self, k, A):
        B, SM = self.banks, self.SM
        KTt, KRt, QTt, QRt, Vt, PT = A["KT"], A["KR"], A["QT"], A["QR"], A["V"], A["PT"]
        pcnt = [0]

        def attend(keytiles, qcol0, nq, accs, last_tile_of, diag=None, pre_pv=None):
            n = len(keytiles)
            st = {}

            def scores(ti):
                c0, nk, vt = keytiles[ti]
                vis = [bi for bi in range(len(accs)) if last_tile_of[bi] >= ti]
                q0 = accs[vis[0]][1]
                sb = B[pcnt[0] % 2]
                pt = PT[pcnt[0] % 3]
                pcnt[0] += 1
                k.op(k.pe, lambda e: e.matmul(sb[0:nk, q0:nq], lhsT=KTt[:, c0:c0 + nk], rhs=QTt[:, qcol0 + q0:qcol0 + nq], start=True, stop=False),
                     reads=[KTt, QTt], writes=[sb], inc=False)
                k.op(k.pe, lambda e: e.matmul(sb[0:nk, q0:nq], lhsT=KRt[0:65, c0:c0 + nk], rhs=QRt[0:65, qcol0 + q0:qcol0 + nq], start=False, stop=True),
                     reads=[KRt, QRt], writes=[sb])
                k.op(k.act, lambda e: e.activation(out=pt[0:nk, q0:nq], in_=sb[0:nk, q0:nq], func=AF.Exp, scale=SM), reads=[sb], writes=[pt])
                for bi in vis:
                    if diag is not None and diag[bi] == ti:
                        qo = accs[bi][1]
                        k.op(k.pool, lambda e, qo=qo: e.memset(pt[64:128, qo:qo + 64], 0.0), writes=[pt])
                st[ti] = (pt, vis)

            DEPTH = 2
            for t_ in range(min(DEPTH, n)):
                scores(t_)
            if pre_pv is not None:
                pre_pv()
            for ti, (c0, nk, vt) in enumerate(keytiles):
                if ti + DEPTH < n:
                    scores(ti + DEPTH)
                pt, vis = st.pop(ti)
                for bi in vis:
                    bank, qo, nqb = accs[bi]
                    k.op(k.pe, lambda e, bank=bank, qo=qo, nqb=nqb: e.matmul(bank[0:nqb, 0:129], lhsT=pt[0:nk, qo:qo + nqb], rhs=Vt[0:nk, vt, :],
                                                                            start=(ti == 0), stop=(ti == last_tile_of[bi])),
                         reads=[pt, Vt], writes=[bank], inc=(ti == last_tile_of[bi] or bi == vis[-1]))
        return attend

    def phase_mla_prompt(self, k, I, O):
        C, S, B = self.cfg, self.S, self.banks
        TP = C.TP
        NT_ = 1 + C.SEQ // 128
        A = self.attn_bufs(k, TP, TP, NT_)
        attend = self.make_attend(k, A)
        KTt, KRt, QTt, QRt, Vt = A["KT"], A["KR"], A["QT"], A["QR"], A["V"]
        k.dma(k.sp, [(KRt[0:64, 0:TP], S["KRT"].t[:, 0:TP])], KRt, writes=[KRt])
        for h in range(4):
            k.dma(k.sp, [(KTt[:, 0:TP], S["KT"].t[h, :, 0:TP])], KTt, writes=[KTt])
            k.dma(k.sp, [(QTt[:, 0:TP], S["QT"].t[h, :, 0:TP])], QTt, writes=[QTt])
            k.dma(k.sp, [(QRt[0:64, 0:TP], S["QRT"].t[h, :, 0:TP])], QRt, writes=[QRt])
            vsrc = S["V1"].t[16:TP, h, :].rearrange("(i p) c -> p i c", p=128)
            k.dma(k.sp, [(Vt[0:16, 0, :], S["V1"].t[0:16, h, :])] +
                  [(Vt[:, 1 + i0:1 + min(i0 + 16, NT_ - 1), :], vsrc[:, i0:min(i0 + 16, NT_ - 1), :]) for i0 in range(0, NT_ - 1, 16)],
                  Vt, writes=[Vt])
            self.row_max(k, S["K2"].t[h:h + 1, 0:TP], TP, A["k2m"], A["t65"])
            self.stab_row(k, QRt, TP, S["QN2"].t[h:h + 1, 0:TP], A["k2m"], A["t65"])
            pending_fin = None
            attend([(0, 16, 0)], 0, 16, [(B[2], 0, 16)], [0])
            self.attn_finish(k, B[2], 16, h, 0, A["ob"], A["rec"])
            nfb = C.SEQ // 128
            for g0 in range(0, nfb, 4):
                nb = min(4, nfb - g0)
                tiles = [(0, 16, 0)] + [(16 + 128 * i, 128, 1 + i) for i in range(g0 + nb)]
                accs = [(B[2 + bi], bi * 128, 128) for bi in range(nb)]
                last = [1 + g0 + bi for bi in range(nb)]
                attend(tiles, 16 + 128 * g0, nb * 128, accs, last, diag=last, pre_pv=pending_fin)

                def pending_fin(nb=nb, g0=g0, h=h):
                    for bi in range(nb):
                        self.attn_finish(k, B[2 + bi], 128, h, 16 + 128 * (g0 + bi), A["ob"], A["rec"])
            if pending_fin is not None:
                pending_fin()
            pending_fin = None

    def phase_mla_sample(self, k, I, O):
        C, S, B = self.cfg, self.S, self.banks
        TP, CA = C.TP, C.CACHE
        ctiles = [(c0, min(128, CA - c0)) for c0 in range(0, CA, 128)]
        nct = len(ctiles)
        A = self.attn_bufs(k, CA + 64, 64, nct + 1)
        attend = self.make_attend(k, A)
        KTt, KRt, QTt, QRt, Vt = A["KT"], A["KR"], A["QT"], A["QR"], A["V"]
        WUK = self.load_resident(k, "aWUK", I["wuk"], [128, 2, 512])
        WUV = self.load_resident(k, "aWUV", I["wuv"], [128, 2, 512])
        cin = [k.sbuf("a_cin%d" % i, [128, 256], F32, dma=True) for i in range(2)]
        kin = [k.sbuf("a_kin%d" % i, [128, 64], F32, dma=True) for i in range(2)]
        cb = k.sbuf("a_cb", [128, 256], BF16)
        cTc = k.sbuf("a_cTc", [128, 2, CA], BF16)
        Vall = k.sbuf("a_Vall", [128, nct + 1, 4, 129], BF16, dma=True)
        k.op(k.pool, lambda e: e.memset(Vall[:, :, :, 128:129], 1.0), writes=[Vall])
        ksq = k.sbuf("a_ksq", [128, 512], F32, dma=True)
        krsq = k.sbuf("a_krsq", [64, CA], F32)
        K2c = k.dram("K2c", [1, CA + 64], F32)
        for b in range(C.NSB):
            rq = TP + 64 * b
            for ti, (c0, nk) in enumerate(ctiles):
                ci_, ki_ = cin[ti % 2], kin[ti % 2]
                k.dma(k.sp, [(ci_[0:nk, :], I["cache_ckv"][b, c0:c0 + nk, :])], ci_, writes=[ci_])
                k.dma(k.sp, [(ki_[0:nk, :], I["cache_kr"][b, c0:c0 + nk, :])], ki_, writes=[ki_])
                k.op(k.act, lambda e: e.copy(out=cb[0:nk, :], in_=ci_[0:nk, :]), reads=[ci_], writes=[cb])
                tb = B[6].t[:].bitcast(BF16)
                for kc in range(2):
                    k.op(k.pe, lambda e, kc=kc: e.transpose(out=tb[:, kc * 128:kc * 128 + nk], in_=cb[0:nk, kc * 128:(kc + 1) * 128],
                                                            identity=self.identb[0:nk, 0:nk]), reads=[cb, self.identb], writes=[B[6]], inc=(kc == 1))
                k.op(k.dve, lambda e: e.tensor_copy(out=cTc[:, :, c0:c0 + nk], in_=tb[:, 0:256].rearrange("p (k t) -> p k t", t=128)[:, :, 0:nk]),
                     reads=[B[6]], writes=[cTc])
                k.op(k.pe, lambda e: e.transpose(out=B[7][0:64, 0:nk], in_=ki_[0:nk, :], identity=self.identf[0:nk, 0:nk]),
                     reads=[ki_, self.identf], writes=[B[7]])
                k.op(k.act, lambda e: e.copy(out=KRt[0:64, c0:c0 + nk], in_=B[7][0:64, 0:nk]), reads=[B[7]], writes=[KRt])
                k.op(k.act, lambda e: e.activation(out=krsq[0:64, c0:c0 + nk], in_=B[7][0:64, 0:nk], func=AF.Square), reads=[B[7]], writes=[krsq])
                for kc in range(2):
                    k.op(k.pe, lambda e, kc=kc: e.matmul(B[5][0:nk, :], lhsT=cTc[:, kc, c0:c0 + nk], rhs=WUV[:, kc, :], start=(kc == 0), stop=(kc == 1)),
                         reads=[cTc, WUV], writes=[B[5]], inc=(kc == 1))
                k.op(k.dve, lambda e: e.tensor_copy(out=Vall[0:nk, ti, :, 0:128], in_=B[5][0:nk, :].rearrange("p (h d) -> p h d", d=128)),
                     reads=[B[5]], writes=[Vall])
            k.dma(k.sp, [(KRt[0:64, CA:CA + 64], S["KRT"].t[:, rq:rq + 64])], KRt, writes=[KRt])
            k.dma(k.sp, [(Vall[0:64, nct, :, :], S["V1"].t[rq:rq + 64, :, :])], Vall, writes=[Vall])
            for h in range(4):
                for c0 in range(0, CA, 512):
                    w = min(512, CA - c0)
                    for kc in range(2):
                        k.op(k.pe, lambda e, kc=kc: e.matmul(B[5][:, 0:w], lhsT=WUK[:, kc, h * 128:(h + 1) * 128], rhs=cTc[:, kc, c0:c0 + w],
                                                             start=(kc == 0), stop=(kc == 1)), reads=[WUK, cTc], writes=[B[5]], inc=(kc == 1))
                    k.op(k.act, lambda e: e.copy(out=KTt[:, c0:c0 + w], in_=B[5][:, 0:w]), reads=[B[5]], writes=[KTt])
                    k.op(k.act, lambda e: e.activation(out=ksq[:, 0:w], in_=B[5][:, 0:w], func=AF.Square), reads=[B[5]], writes=[ksq])
                    k.op(k.pe, lambda e: e.matmul(B[7][0:1, 0:w], lhsT=self.ones[:, 0:1], rhs=ksq[:, 0:w], start=True, stop=False),
                         reads=[self.ones, ksq], writes=[B[7]], inc=False)
                    k.op(k.pe, lambda e: e.matmul(B[7][0:1, 0:w], lhsT=self.ones[0:64, 0:1], rhs=krsq[0:64, c0:c0 + w], start=False, stop=True),
                         reads=[self.ones, krsq], writes=[B[7]])
                    k.op(k.dve, lambda e: e.tensor_copy(out=ksq[0:1, 0:w], in_=B[7][0:1, 0:w]), reads=[B[7]], writes=[ksq])
                    k.dma(k.sp, [(K2c[:, c0:c0 + w], ksq[0:1, 0:w])], ksq, reads=[ksq], writes=[K2c])
                k.dma(k.sp, [(K2c[:, CA:CA + 64], S["K2"].t[h:h + 1, rq:rq + 64])], K2c, writes=[K2c])
                k.dma(k.sp, [(KTt[:, CA:CA + 64], S["KT"].t[h, :, rq:rq + 64])], KTt, writes=[KTt])
                self.row_max_buf(k, K2c, CA + 64, A["k2m"], A["t65"])
                k.dma(k.sp, [(QTt[:, 0:64], S["QT"].t[h, :, rq:rq + 64])], QTt, writes=[QTt])
                k.dma(k.sp, [(QRt[0:64, 0:64], S["QRT"].t[h, :, rq:rq + 64])], QRt, writes=[QRt])
                self.stab_row(k, QRt, 64, S["QN2"].t[h:h + 1, rq:rq + 64], A["k2m"], A["t65"])
                k.op(k.pool, lambda e: e.tensor_copy(out=Vt[:, 0:nct + 1, :], in_=Vall[:, :, h, :]), reads=[Vall], writes=[Vt])
                tiles = [(c0, nk, ti) for ti, (c0, nk) in enumerate(ctiles)] + [(CA, 64, nct)]
                attend(tiles, 0, 64, [(B[2], 0, 64)], [len(tiles) - 1])
                self.attn_finish(k, B[2], 64, h, rq, A["ob"], A["rec"])

    def row_max_buf(self, k, buf, ncol, k2m, tmp65):
        self.row_max(k, buf.t[:, 0:ncol], ncol, k2m, tmp65, src_buf=buf)

    def phase_out(self, k, I, O):
        C, B = self.cfg, self.banks
        WO = self.load_resident(k, "WO", I["wout"], [128, KC, D])
        mixT = k.sbuf("o_mixT", [128, 8, 512], BF16, dma=True)
        for gi, grp in enumerate(C.groups):
            blocks, NT = group_layout(grp)
            xt = self.xt[gi % 2]
            runs = []
            for (r0, n, off) in blocks:
                if runs and runs[-1][2] + runs[-1][1] == r0:
                    runs[-1] = (runs[-1][0], runs[-1][1] + n, runs[-1][2])
                else:
                    runs.append((off, n, r0))
            k.dma(k.sp, [(mixT[:, :, o:o + n], self.MIXT.t[:, :, r:r + n].rearrange("c p t -> p c t")) for (o, n, r) in runs],
                  mixT, reads=[self.MIXT], writes=[mixT])
            for bi, (r0, n, off) in enumerate(blocks):
                k.dma(k.sp, [(xt[bi][0:n, :], self.X1[r0:r0 + n, :])], xt[bi], reads=[self.X1], writes=[xt[bi]])
            for bi, (r0, n, off) in enumerate(blocks):
                for half in range(2):
                    cs = slice(half * 512, (half + 1) * 512)
                    bank = B[4 + (2 * bi + half) % 4]
                    for kc in range(8):
                        k.op(k.pe, lambda e, kc=kc: e.matmul(bank[0:n, :], lhsT=mixT[:, kc, off:off + n], rhs=WO[:, kc, cs],
                                                             start=(kc == 0), stop=(kc == 7)), reads=[mixT, WO], writes=[bank], inc=(kc == 7))
                    k.op(k.dve, lambda e: e.tensor_tensor(out=xt[bi][0:n, cs], in0=bank[0:n, :], in1=xt[bi][0:n, cs], op=ALU.add),
                         reads=[bank, xt[bi]], writes=[xt[bi]])
            self.norm_group(k, blocks, xt, self.gT[:, 2, :], B[4:8])

            def epi(half, blocks=blocks, xt=xt):
                for bi, (r0, n, off) in enumerate(blocks):
                    cs = slice(half * 512, (half + 1) * 512)
                    xo = self.xo[bi]
                    k.op(k.dve, lambda e, bi=bi, n=n, cs=cs, xo=xo: e.scalar_tensor_tensor(
                        out=xo[0:n, cs], in0=B[4 + bi][0:n, 0:512], scalar=0.5, in1=xt[bi][0:n, cs], op0=ALU.mult, op1=ALU.add),
                         reads=[B[4 + bi], xt[bi]], writes=[xo])
                    if half == 1:
                        stt_, rstd = self.rstd_of(k, xo[0:n, :], n, D, [xo])
                        k.op(k.dve, lambda e, n=n, xo=xo, rstd=rstd: e.scalar_tensor_tensor(
                            out=xo[0:n, :], in0=xo[0:n, :], scalar=rstd, in1=self.gfin[0:n, :], op0=ALU.mult, op1=ALU.mult),
                             reads=[xo, stt_, self.gfin], writes=[xo])
                        k.dma(k.sp, [(O["y"][r0:r0 + n, :], xo[0:n, :])], xo, reads=[xo])
            self.ffn_core(k, blocks, NT, self.wgu2_s, self.wd2_s, epi)

def lay_wgu(wg, wu):
    a = np.stack([wg, wu], 0).reshape(2, KC, 128, NFC, 128)
    a = a.transpose(3, 2, 0, 1, 4)
    return np.ascontiguousarray(a).reshape(NFC * 128, 2 * KC * 128)


def lay_wd(wd):
    a = wd.reshape(NFC, 128, D).transpose(1, 0, 2)
    return np.ascontiguousarray(a).reshape(128 * NFC, D)


def _kcp(w):
    K_, Cc = w.shape
    return np.ascontiguousarray(w.reshape(K_ // 128, 128, Cc).transpose(1, 0, 2))


def _swap(w):
    return np.concatenate([w[:, 32:64], w[:, 0:32]], axis=1)


def lay_win(w_in):
    kr = w_in[:, 2696:2760]
    wf = np.concatenate([w_in[:, 0:1536], kr, _swap(kr)], axis=1)
    wt = np.concatenate([w_in[:, 1536:2048], w_in[:, 2056:2440], w_in[:, 2440:2696], w_in[:, 2048:2056]], axis=1)
    return _kcp(wf), _kcp(wt)


def lay_wuq(w_uq):
    cols = []
    for h in range(4):
        qn = w_uq[:, h * 192:h * 192 + 128]
        qr = w_uq[:, h * 192 + 128:h * 192 + 192]
        cols += [qn, qr, _swap(qr)]
    return _kcp(np.concatenate(cols, axis=1))


def lay_wukv(w_ukv):
    kn = np.concatenate([w_ukv[:, h * 256:h * 256 + 128] for h in range(4)], axis=1)
    v = np.concatenate([w_ukv[:, h * 256 + 128:h * 256 + 256] for h in range(4)], axis=1)
    return _kcp(kn), _kcp(v)


_PROG = {}


def _program():
    if "nc" not in _PROG:
        cfg = Cfg(SEQ=8192, NSB=4, PAST=2048)
        p = Prog(cfg)
        _PROG["nc"] = p.build(upto=5)
        _PROG["cfg"] = cfg
    return _PROG["nc"], _PROG["cfg"]


def kernel(x_prompt, x_sample, cache_mla_ckv, cache_mla_krope, state_gdn, state_conv, meta,
           ffn1_norm, ffn1_wg, ffn1_wu, ffn1_wd, mix_norm, w_in, conv_w, a_log, dt_bias, gdn_norm,
           q_norm, kv_norm, w_uq, w_ukv, w_out, ffn2_norm, ffn2_wg, ffn2_wu, ffn2_wd, final_norm):
    f = lambda a: np.ascontiguousarray(np.asarray(a, dtype=np.float32))
    nc, cfg = _program()
    NB, NSB, TP = 4, cfg.NSB, cfg.TP
    wf, wt = lay_win(f(w_in)[0])
    wuk, wuv = lay_wukv(f(w_ukv)[0])
    shared = dict(
        norms=np.stack([f(ffn1_norm)[0], f(mix_norm)[0], f(ffn2_norm)[0], f(final_norm)]),
        q_norm=f(q_norm)[0], kv_norm=f(kv_norm)[0], gdn_norm=f(gdn_norm)[0], a_log=f(a_log)[0], dt_bias=f(dt_bias)[0],
        conv_w=f(conv_w)[0],
        wgu1=lay_wgu(f(ffn1_wg)[0], f(ffn1_wu)[0]), wd1=lay_wd(f(ffn1_wd)[0]),
        wgu2=lay_wgu(f(ffn2_wg)[0], f(ffn2_wu)[0]), wd2=lay_wd(f(ffn2_wd)[0]),
        wf=wf, wt=wt, wuq=lay_wuq(f(w_uq)[0]), wuk=wuk, wuv=wuv, wout=_kcp(f(w_out)[0]))
    xp, xs, mt = f(x_prompt), f(x_sample), f(meta)
    in_maps = []
    for c in range(8):
        sb = slice(NSB * c, NSB * (c + 1))
        m = dict(shared)
        m["xin"] = np.concatenate([mt, xp[c % NB], xs[sb].reshape(-1, D)], axis=0)
        m["state_conv"] = f(state_conv)[0, sb]
        m["state_gdn"] = f(state_gdn)[0, sb]
        m["cache_ckv"] = f(cache_mla_ckv)[0, sb]
        m["cache_kr"] = f(cache_mla_krope)[0, sb]
        in_maps.append(m)
    res = run_bass_kernel_spmd(nc, in_maps, core_ids=list(range(8))).results
    y_p = np.stack([res[b]["y"][16:TP] for b in range(NB)])
    y_s = np.concatenate([res[c]["y"][TP:].reshape(NSB, 64, D) for c in range(8)])
    p_ckv = np.stack([res[b]["ckv"][0:TP] for b in range(NB)])[None]
    p_kr = np.stack([res[b]["kr"][0:TP] for b in range(NB)])[None]
    p_gdn = np.stack([res[b]["pgdn"] for b in range(NB)])[None]
    p_conv = np.stack([res[b]["pconv"] for b in range(NB)])[None]
    s_ckv = np.concatenate([res[c]["ckv"][TP:].reshape(NSB, 64, 256) for c in range(8)])[None]
    s_kr = np.concatenate([res[c]["kr"][TP:].reshape(NSB, 64, 64) for c in range(8)])[None]
    s_gdn = np.concatenate([res[c]["sgdn"] for c in range(8)])[None]
    s_conv = np.concatenate([res[c]["sconv"] for c in range(8)])[None]
    return tuple(np.ascontiguousarray(a, dtype=np.float32) for a in
                 (y_p, y_s, p_ckv, p_kr, p_gdn, p_conv, s_ckv, s_kr, s_gdn, s_conv))
```

```python
import contextlib
import numpy as np
import concourse.bass as bass
import concourse.mybir as mybir
from concourse.bass_utils import run_bass_kernel_spmd

F32 = mybir.dt.float32
BF16 = mybir.dt.bfloat16
F32R = mybir.dt.float32r
AF = mybir.ActivationFunctionType
ALU = mybir.AluOpType

D = 1024
DFF = 2816
NFC = DFF // 128
KC = D // 128
EPS = 1e-6


class Eng:
    def __init__(self, name, e, sem):
        self.name, self.e, self.sem = name, e, sem
        self.tick = 0
        self.seen = {}
        self.nwait = 0
        self.nins = 0


class DSem:
    def __init__(self, h):
        self.h = h
        self.count = 0


class Buf:
    def __init__(self, t, name, dsem=None):
        self.t = t
        self.name = name
        self.last_w = None
        self.extra_w = []
        self.readers = {}
        self.ds = dsem
        self.native_r = False
        self.rmode = False

    def __getitem__(self, k):
        ap = self.t[k]
        if self.native_r and not self.rmode:
            return ap.bitcast(F32)
        return ap

    def f32(self, k):
        ap = self.t[k]
        return ap.bitcast(F32) if self.native_r else ap


class K:
    def __init__(self, nc, stack):
        self.nc, self.stack = nc, stack
        self.engs = {}
        for name, e in (("pe", nc.tensor), ("act", nc.scalar), ("dve", nc.vector),
                        ("pool", nc.gpsimd), ("sp", nc.sync)):
            sem = stack.enter_context(nc.semaphore("sem_" + name))
            self.engs[name] = Eng(name, e, sem)
        self.pe, self.act, self.dve, self.pool, self.sp = (
            self.engs[n] for n in ("pe", "act", "dve", "pool", "sp"))
        self.nsem = 5
        self.all_ds = []
        self.free_ds = []
        self.scopes = [(stack, [])]
        self.names = {}

    def new_sem(self, name, sw=False):
        if not sw and self.free_ds:
            return self.free_ds.pop()
        self.nsem += 1
        assert self.nsem <= 100, "out of semaphores"
        d = DSem(self.stack.enter_context(self.nc.semaphore("ds%d" % self.nsem)))
        d.sw = sw
        self.all_ds.append(d)
        return d

    def _reg(self, b):
        if b.ds is not None and not getattr(b.ds, "sw", False):
            self.scopes[-1][1].append(b.ds)
        return b

    @contextlib.contextmanager
    def scope(self):
        with contextlib.ExitStack() as st:
            self.scopes.append((st, []))
            try:
                yield
            finally:
                self.barrier()
                _, dss = self.scopes.pop()
                self.free_ds.extend(dss)

    def _uniq(self, name):
        n = self.names.get(name, 0)
        self.names[name] = n + 1
        return name if n == 0 else "%s__%d" % (name, n)

    def sbuf(self, name, shape, dtype, dma=False):
        name = self._uniq(name)
        t = self.scopes[-1][0].enter_context(self.nc.sbuf_tensor(name, list(shape), dtype))
        return self._reg(Buf(t, name, self.new_sem(name, sw=(dma == "sw")) if dma else None))

    def psum(self, name, shape, dtype):
        name = self._uniq(name)
        t = self.scopes[-1][0].enter_context(self.nc.psum_tensor(name, list(shape), dtype))
        return Buf(t, name)

    def dram(self, name, shape, dtype, dma=True):
        name = self._uniq(name)
        t = self.nc.dram_tensor(name, list(shape), dtype, kind="Internal").ap()
        return self._reg(Buf(t, name, self.new_sem(name, sw=(dma == "sw")) if dma else None))

    def wrap(self, t, name, dma=False):
        return self._reg(Buf(t, name, self.new_sem(name, sw=(dma == "sw")) if dma else None))

    @staticmethod
    def _max_per_sem(toks):
        best = {}
        for tok in toks:
            if tok is None:
                continue
            old = best.get(tok[0])
            if old is None or old[2] < tok[2]:
                best[tok[0]] = tok
        return best

    def split(self, whole, parts):
        for p in parts:
            p.last_w = whole.last_w
            p.extra_w = list(whole.extra_w)
            p.readers = dict(whole.readers)

    def merge(self, whole, parts):
        w = self._max_per_sem([whole.last_w] + list(whole.extra_w) +
                              [t for p in parts for t in ([p.last_w] + list(p.extra_w))])
        toks = list(w.values())
        whole.last_w = toks[0] if toks else None
        whole.extra_w = toks[1:]
        whole.readers = self._max_per_sem(list(whole.readers.values()) + [t for p in parts for t in p.readers.values()])

    def _wait(self, eng, tok):
        if tok is None:
            return
        key, sem, val = tok
        if eng.seen.get(key, 0) >= val:
            return
        if key == id(eng.sem) and eng.name == "pe":
            return
        eng.e.wait_ge(sem, val)
        eng.seen[key] = val
        eng.nwait += 1

    def _deps(self, eng, reads, writes):
        for b in reads:
            self._wait(eng, b.last_w)
            for tok in b.extra_w:
                self._wait(eng, tok)
        for b in writes:
            self._wait(eng, b.last_w)
            for tok in b.extra_w:
                self._wait(eng, tok)
            for tok in b.readers.values():
                self._wait(eng, tok)

    def op(self, eng, fn, reads=(), writes=(), inc=True):
        self._deps(eng, reads, writes)
        ins = fn(eng.e)
        eng.nins += 1
        if inc:
            ins.then_inc(eng.sem, 1)
            eng.tick += 1
            t = eng.tick
        else:
            t = eng.tick + 1
        tok = (id(eng.sem), eng.sem, t)
        for b in reads:
            old = b.readers.get(tok[0])
            if old is None or old[2] < t:
                b.readers[tok[0]] = tok
        for b in writes:
            b.last_w = tok
            b.extra_w = []
            b.readers = {}
        return ins

    def dma(self, q, pairs, slot, reads=(), writes=(), **kw):
        self._deps(q, reads, writes)
        ds = slot.ds
        for (o, i) in pairs:
            q.e.dma_start(out=o, in_=i, **kw).then_inc(ds.h, 16)
            ds.count += 1
            q.nins += 1
        tok = (id(ds.h), ds.h, 16 * ds.count)
        for b in reads:
            b.readers[tok[0]] = tok
        for b in writes:
            b.last_w = tok
            b.extra_w = []
            b.readers = {}
        return tok

    def barrier(self, engines=None):
        toks = []
        for e in self.engs.values():
            if e.tick > 0:
                toks.append((id(e.sem), e.sem, e.tick))
        for d in self.all_ds:
            if d.count > 0:
                toks.append((id(d.h), d.h, 16 * d.count))
        for e in (engines or self.engs.values()):
            for tok in toks:
                if tok[0] == id(e.sem):
                    continue
                self._wait(e, tok)

    def finish(self):
        self.barrier(engines=[self.sp])


class Cfg:
    def __init__(self, SEQ=8192, NSB=4, PAST=2048):
        self.SEQ, self.NSB, self.PAST = SEQ, NSB, PAST
        self.NMETA = 16
        self.TP = 16 + SEQ
        self.NS = NSB * 64
        self.NTOK = self.TP + self.NS
        self.CACHE = 16 + PAST
        self.blocks = [(0, 16)]
        self.blocks += [(16 + 128 * i, 128) for i in range(SEQ // 128)]
        sb = []
        r = self.TP
        while r < self.NTOK:
            n = min(128, self.NTOK - r)
            sb.append((r, n))
            r += n
        self.samp_blocks = sb
        g0 = [self.blocks[0]] + sb
        self.groups = []
        cur, tot = [], 0
        for b in g0:
            if tot + b[1] > 512:
                self.groups.append(cur)
                cur, tot = [], 0
            cur.append(b)
            tot += b[1]
        if cur:
            self.groups.append(cur)
        fb = self.blocks[1:]
        for i in range(0, len(fb), 4):
            self.groups.append(fb[i:i + 4])


def run_streams(makers, W):
    pending = list(makers)
    active, free = {}, list(range(W))
    while pending or active:
        while pending and free:
            sl = free.pop(0)
            active[sl] = pending.pop(0)(sl)
        for sl in sorted(active):
            try:
                next(active[sl])
            except StopIteration:
                del active[sl]
                free.append(sl)


def group_layout(grp):
    out, off = [], 0
    for (r, n) in grp:
        out.append((r, n, off))
        off += n
    return out, off


class Prog:
    def __init__(self, cfg, debug=()):
        self.cfg = cfg
        self.debug = set(debug)
        self.nc = bass.Bass("TRN2", target_bir_lowering=False)
        self.ins = {}
        self.outs = {}

    def inp(self, name, shape, dtype=F32):
        self.ins[name] = self.nc.dram_tensor(name, list(shape), dtype, kind="ExternalInput").ap()
        return self.ins[name]

    def out(self, name, shape, dtype=F32):
        self.outs[name] = self.nc.dram_tensor(name, list(shape), dtype, kind="ExternalOutput").ap()
        return self.outs[name]

    def build(self, upto=5, dump=()):
        cfg, nc = self.cfg, self.nc
        C = cfg
        I = {}
        for nm, shp in (("xin", [C.NTOK, D]), ("norms", [4, D]), ("q_norm", [384]), ("kv_norm", [256]),
                        ("gdn_norm", [128]), ("a_log", [4]), ("dt_bias", [4]), ("conv_w", [4, 1536]),
                        ("wgu1", [NFC * 128, 2 * KC * 128]), ("wd1", [128 * NFC, D]),
                        ("wgu2", [NFC * 128, 2 * KC * 128]), ("wd2", [128 * NFC, D]),
                        ("wf", [128, KC, 13 * 128]), ("wt", [128, KC, 1160]), ("wuq", [128, 3, 1024]),
                        ("wuk", [128, 2, 512]), ("wuv", [128, 2, 512]), ("wout", [128, KC, D]),
                        ("state_conv", [C.NSB, 3, 1536]), ("state_gdn", [C.NSB, 4, 128, 128]),
                        ("cache_ckv", [C.NSB, C.CACHE, 256]), ("cache_kr", [C.NSB, C.CACHE, 64])):
            I[nm] = self.inp(nm, shp)
        O = {}
        for nm, shp in (("y", [C.NTOK, D]), ("ckv", [C.NTOK, 256]), ("kr", [C.NTOK, 64]), ("pgdn", [4, 128, 128]),
                        ("pconv", [3, 1536]), ("sgdn", [C.NSB, 4, 128, 128]), ("sconv", [C.NSB, 3, 1536])):
            O[nm] = self.out(nm, shp)
        self.I, self.O = I, O
        with contextlib.ExitStack() as st:
            k = K(nc, st)
            self.k = k
            self.consts(k, I["norms"])
            self.proj_consts(k, I)
            self.wgu1_s = self.cast_weight(k, "wgu1_s", I["wgu1"], part_rows=[2 * 128, 6 * 128, 14 * 128])
            self.wd1_s = self.cast_weight(k, "wd1_s", I["wd1"])
            X1 = k.dram("X1", [C.NTOK, D], F32, dma=False)
            self.X1 = X1
            self.banks = [k.psum("bank%d" % i, [128, 512], F32) for i in range(8)]
            self.hT = k.sbuf("hT", [128, KC, 512], BF16)
            self.xnb = [k.sbuf("xnb%d" % i, [128, D], BF16) for i in range(4)]
            self.stat = [k.sbuf("stat%d" % i, [128, 4], F32) for i in range(4)]
            self.junk = k.sbuf("junk", [128, D], BF16)
            self.cnt = {"wgu": 0, "wd": 0, "xnb": 0, "stat": 0, "sg": 0}
            self.rope_tables(k)
            self.proj_scratch(k)
            with k.scope():
                self.ffn_bufs(k)
                self.phase_ffn1(k, I["xin"], X1)
            if upto >= 2:
                with k.scope():
                    self.phase_proj(k, I, O)
                    self.sbuf_left = nc.sbuf_bytes_remaining
            if upto >= 3:
                with k.scope():
                    self.phase_gdn(k, I, O)
                    self.sbuf_left = nc.sbuf_bytes_remaining
            if upto >= 4:
                with k.scope():
                    self.phase_mla_prompt(k, I, O)
                    self.sbuf_left4 = nc.sbuf_bytes_remaining
                with k.scope():
                    self.phase_mla_sample(k, I, O)
            if upto >= 5:
                with k.scope():
                    self.ffn_bufs(k)
                    self.phase_out(k, I, O)
            k.barrier()
            allscr = dict(self.S)
            allscr.update({"X1": X1, "COS2": self.COS2, "SIN2S": self.SIN2S})
            dsl = k.wrap(None, "dbgslot", dma=True)
            for nm in dump:
                src = allscr[nm]
                o = self.out("dbg_" + nm, list(src.t.shape), src.t.dtype)
                k.dma(k.sp, [(o, src.t)], dsl)
            k.finish()
            self.stats = {n: (e.nins, e.nwait) for n, e in k.engs.items()}
        return nc

    def consts(self, k, norms):
        nc = self.nc
        ones = k.sbuf("c_ones", [128, 128], F32)
        k.op(k.pool, lambda e: e.memset(ones[:], 1.0), writes=[ones])
        self.ones = ones
        identf = k.sbuf("c_identf", [128, 128], F32)
        k.op(k.pool, lambda e: e.affine_select(out=identf[:], in_=ones[:], pattern=[[1, 128]],
                                               compare_op=ALU.is_equal, fill=0.0, base=0,
                                               channel_multiplier=-1), reads=[ones], writes=[identf])
        identb = k.sbuf("c_identb", [128, 128], BF16)
        k.op(k.dve, lambda e: e.tensor_copy(out=identb[:], in_=identf[:]), reads=[identf], writes=[identb])
        self.identf, self.identb = identf, identb
        gT = k.sbuf("c_gT", [128, 4, KC], F32, dma=True)
        with nc.allow_non_contiguous_dma(reason="tiny one-time gain transpose load"):
            k.dma(k.sp, [(gT[:, r, :], norms[r, :].rearrange("(kc p) -> p kc", p=128)) for r in range(4)],
                  gT, writes=[gT])
        self.gT = gT
        gfin = k.sbuf("c_gfin", [128, D], F32, dma=True)
        k.dma(k.sp, [(gfin[:], norms[3:4, :].partition_broadcast(128))], gfin, writes=[gfin])
        self.gfin = gfin
        epsb = k.sbuf("c_eps", [128, 1], F32)
        k.op(k.pool, lambda e: e.memset(epsb[:], EPS), writes=[epsb])
        self.epsb = epsb
        l2b = k.sbuf("c_l2b", [128, 2], F32)
        k.op(k.pool, lambda e: e.memset(l2b[:, 0:1], 128.0 * 1e-6), writes=[l2b])
        k.op(k.pool, lambda e: e.memset(l2b[:, 1:2], 1e-6), writes=[l2b])
        self.l2b = l2b

    def cast_weight(self, k, name, src, part_rows=None):
        R, Cc = src.shape
        scr = k.dram(name, [R, Cc], BF16, dma="sw")
        bounds = [0] + list(part_rows or []) + [R]
        parts = []
        for a, b in zip(bounds[:-1], bounds[1:]):
            pb = scr if len(bounds) == 2 else k.wrap(scr.t, "%s_p%d" % (name, a), dma="sw")
            pairs = [(scr[r0:min(b, r0 + 256), :], src[r0:min(b, r0 + 256), :]) for r0 in range(a, b, 256)]
            k.dma(k.pool, pairs, pb, writes=[pb], max_dma_last_dim=2048 * 4)
            parts.append((a, b, pb))
        scr.parts = parts
        return scr

    def ffn_bufs(self, k):
        self.xt = [[k.sbuf("xt%d_%d" % (s, b), [128, D], F32, dma=True) for b in range(4)] for s in range(2)]
        self.xo = [k.sbuf("xo%d" % b, [128, D], F32, dma=True) for b in range(4)]
        self.actT = k.sbuf("actT", [128, NFC, 512], BF16)
        self.wgu_sl = [k.sbuf("wgu_sl%d" % i, [128, 2, KC, 128], BF16, dma=True) for i in range(3)]
        self.wd_sl = [k.sbuf("wd_sl%d" % i, [128, 11, 512], BF16, dma=True) for i in range(2)]
        self.sg = [k.sbuf("sg%d" % i, [128, 512], F32) for i in range(2)]

    def rstd_of(self, k, x_ap, n, Dn, reads):
        stt = self.stat[self.cnt["stat"] % 4]
        self.cnt["stat"] += 1
        junk = self.junk
        k.op(k.act, lambda e: e.activation(out=junk[0:n, 0:Dn], in_=x_ap, func=AF.Square,
                                           accum_out=stt[0:n, 0:1]), reads=reads, writes=[junk, stt])
        k.op(k.act, lambda e: e.activation(out=stt[0:n, 1:2], in_=stt[0:n, 0:1], func=AF.Sqrt,
                                           bias=self.epsb[0:n, :], scale=1.0 / Dn),
             reads=[stt, self.epsb], writes=[stt])
        k.op(k.dve, lambda e: e.reciprocal(out=stt[0:n, 2:3], in_=stt[0:n, 1:2]), reads=[stt], writes=[stt])
        return stt, stt[0:n, 2:3]

    def norm_A(self, k, xbuf, x_ap, n, nkc):
        Dn = nkc * 128
        stt, rstd = self.rstd_of(k, x_ap, n, Dn, [xbuf])
        xnb = self.xnb[self.cnt["xnb"] % len(self.xnb)]
        self.cnt["xnb"] += 1
        k.op(k.act, lambda e: e.activation(out=xnb[0:n, 0:Dn], in_=x_ap, func=AF.Copy, scale=rstd),
             reads=[xbuf, stt], writes=[xnb])
        return xnb

    def norm_B(self, k, xnb, n, nkc, g_ap, dstT, off, tbank, gbuf=None):
        tb = tbank.t[:].bitcast(BF16)
        for kc in range(nkc):
            k.op(k.pe, lambda e, kc=kc: e.transpose(out=tb[:, kc * 128:kc * 128 + n],
                                                    in_=xnb[0:n, kc * 128:(kc + 1) * 128],
                                                    identity=self.identb[0:n, 0:n]),
                 reads=[xnb, self.identb], writes=[tbank], inc=(kc == nkc - 1))
        src = tb[:, 0:nkc * 128].rearrange("p (k t) -> p k t", t=128)[:, :, 0:n]
        k.op(k.dve, lambda e: e.tensor_tensor(out=dstT[:, 0:nkc, off:off + n], in0=src,
                                              in1=g_ap.unsqueeze(2).to_broadcast([128, nkc, n]), op=ALU.mult),
             reads=[tbank, gbuf or self.gT], writes=[dstT])

    def norm_to_T(self, k, xbuf, x_ap, n, nkc, g_ap, dstT, off, tbank, gbuf=None):
        xnb = self.norm_A(k, xbuf, x_ap, n, nkc)
        self.norm_B(k, xnb, n, nkc, g_ap, dstT, off, tbank, gbuf)

    def norm_group(self, k, blocks, xt, g_ap, tbanks):
        xn = [self.norm_A(k, xt[bi], xt[bi][0:n, :], n, KC) for bi, (r0, n, off) in enumerate(blocks)]
        for bi, (r0, n, off) in enumerate(blocks):
            self.norm_B(k, xn[bi], n, KC, g_ap, self.hT, off, tbanks[bi])

    @staticmethod
    def part_of(scr, row):
        for (a, b, pb) in scr.parts:
            if a <= row < b:
                return pb
        raise AssertionError(row)

    def ffn_core(self, k, blocks, NT, wgu_s, wd_s, epilogue):
        hT, actT = self.hT, self.actT
        wg_v = wgu_s.t.rearrange("(f p) (g k j) -> f p g k j", p=128, g=2, k=KC)
        wd_v = wd_s.t.rearrange("(p f) c -> p f c", f=NFC)
        for fc in range(NFC):
            sl = self.wgu_sl[self.cnt["wgu"] % 3]
            self.cnt["wgu"] += 1
            k.dma(k.sp, [(sl[:], wg_v[fc])], sl, reads=[self.part_of(wgu_s, fc * 128)], writes=[sl])
            pg = self.banks[(2 * fc) % 4]
            pu = self.banks[(2 * fc + 1) % 4]
            for gu, pb in ((0, pg), (1, pu)):
                for kc in range(KC):
                    k.op(k.pe, lambda e, gu=gu, kc=kc, pb=pb: e.matmul(
                        pb[:, 0:NT], lhsT=sl[:, gu, kc, :], rhs=hT[:, kc, 0:NT],
                        start=(kc == 0), stop=(kc == KC - 1)),
                         reads=[sl, hT], writes=[pb], inc=(kc == KC - 1))
            sg = self.sg[self.cnt["sg"] % 2]
            self.cnt["sg"] += 1
            k.op(k.act, lambda e: e.activation(out=sg[:, 0:NT], in_=pg[:, 0:NT], func=AF.Silu),
                 reads=[pg], writes=[sg])
            k.op(k.dve, lambda e: e.tensor_tensor(out=actT[:, fc, 0:NT], in0=sg[:, 0:NT], in1=pu[:, 0:NT],
                                                  op=ALU.mult), reads=[sg, pu], writes=[actT])
        for half in range(2):
            for q in range(2):
                ws = self.wd_sl[self.cnt["wd"] % 2]
                self.cnt["wd"] += 1
                k.dma(k.sp, [(ws[:], wd_v[:, q * 11:(q + 1) * 11, half * 512:(half + 1) * 512])], ws,
                      reads=[wd_s], writes=[ws])
                for fi in range(11):
                    fc = q * 11 + fi
                    for bi, (r0, n, off) in enumerate(blocks):
                        last = (fi == 10 and bi == len(blocks) - 1) or fc == NFC - 1
                        k.op(k.pe, lambda e, fc=fc, fi=fi, bi=bi, n=n, off=off: e.matmul(
                            self.banks[4 + bi][0:n, 0:512], lhsT=actT[:, fc, off:off + n], rhs=ws[:, fi, :],
                            start=(fc == 0), stop=(fc == NFC - 1)),
                             reads=[actT, ws], writes=[self.banks[4 + bi]], inc=last)
            epilogue(half)

    def phase_ffn1(self, k, xin, X1):
        C = self.cfg
        def load_group(gj):
            blocks_, _ = group_layout(C.groups[gj])
            xt_ = self.xt[gj % 2]
            for bi, (r0, n, off) in enumerate(blocks_):
                k.dma(k.sp, [(xt_[bi][0:n, :], xin[r0:r0 + n, :])], xt_[bi], writes=[xt_[bi]])
        load_group(0)
        for gi, grp in enumerate(C.groups):
            blocks, NT = group_layout(grp)
            xt = self.xt[gi % 2]
            if gi + 1 < len(C.groups):
                load_group(gi + 1)
            self.norm_group(k, blocks, xt, self.gT[:, 0, :], self.banks[4:8])

            def epi(half, blocks=blocks, xt=xt):
                for bi, (r0, n, off) in enumerate(blocks):
                    cs = slice(half * 512, (half + 1) * 512)
                    k.op(k.dve, lambda e, bi=bi, n=n, cs=cs: e.scalar_tensor_tensor(
                        out=self.xo[bi][0:n, cs], in0=self.banks[4 + bi][0:n, 0:512], scalar=0.5,
                        in1=xt[bi][0:n, cs], op0=ALU.mult, op1=ALU.add),
                         reads=[self.banks[4 + bi], xt[bi]], writes=[self.xo[bi]])
                    if half == 1:
                        k.dma(k.sp, [(X1[r0:r0 + n, :], self.xo[bi][0:n, :])], self.xo[bi],
                              reads=[self.xo[bi]], writes=[X1])
            self.ffn_core(k, blocks, NT, self.wgu1_s, self.wd1_s, epi)
            if gi == 0:
                self.wgu2_s = self.cast_weight(k, "wgu2_s", self.I["wgu2"])
                self.wd2_s = self.cast_weight(k, "wd2_s", self.I["wd2"])


    def proj_consts(self, k, I):
        nc, C = self.nc, self.cfg
        gq = k.sbuf("c_gqT", [128, 3], F32, dma=True)
        cw = k.sbuf("c_cwT", [128, 12, 4], F32, dma=True)
        with nc.allow_non_contiguous_dma(reason="tiny one-time constant transposes"):
            k.dma(k.sp, [(gq[:], I["q_norm"].rearrange("(kc p) -> p kc", p=128))], gq, writes=[gq])
            k.dma(k.sp, [(cw[:, :, j], I["conv_w"][j, :].rearrange("(cc p) -> p cc", p=128)) for j in range(4)],
                  cw, writes=[cw])
        self.gqT, self.cwT = gq, cw
        gkv = k.sbuf("c_gkv", [128, 256], F32, dma=True)
        k.dma(k.sp, [(gkv[:], I["kv_norm"].partition_broadcast(128))], gkv, writes=[gkv])
        self.gkv = gkv
        ab = k.sbuf("c_ab", [128, 16], F32, dma=True)
        k.dma(k.sp, [(ab[:, 0:4], I["a_log"].partition_broadcast(128)),
                     (ab[:, 4:8], I["dt_bias"].partition_broadcast(128))], ab, writes=[ab])
        k.op(k.act, lambda e: e.activation(out=ab[:, 8:12], in_=ab[:, 0:4], func=AF.Exp), reads=[ab], writes=[ab])
        k.op(k.dve, lambda e: e.tensor_scalar(out=ab[:, 8:12], in0=ab[:, 8:12], scalar1=-1.0, scalar2=None,
                                              op0=ALU.mult), reads=[ab], writes=[ab])
        k.op(k.pool, lambda e: e.memset(ab[:, 12:16], 0.0), writes=[ab])
        self.abc = ab

    def load_resident(self, k, name, src, shape):
        t = k.sbuf(name, shape, BF16, dma="sw")
        k.dma(k.pool, [(t[:, a, :], src[:, a, :]) for a in range(shape[1])], t, writes=[t],
              max_dma_last_dim=8192)
        return t

    def rope_tables(self, k):
        C = self.cfg
        NPOS = C.TP + 64
        self.COS2 = k.dram("COS2", [64, NPOS], F32, dma=False)
        self.SIN2S = k.dram("SIN2S", [64, NPOS], F32, dma=False)
        I32 = mybir.dt.int32
        W = 2048
        with k.scope():
            pidx = k.sbuf("r_pidx", [64, 1], F32)
            for h in range(2):
                k.op(k.pool, lambda e, h=h: e.iota(pidx[32 * h:32 * h + 32, :], pattern=[[0, 1]], base=0,
                                                   channel_multiplier=1, allow_small_or_imprecise_dtypes=True),
                     writes=[pidx])
            inv = k.sbuf("r_inv", [64, 1], F32)
            k.op(k.act, lambda e: e.activation(out=inv[:], in_=pidx[:], func=AF.Exp,
                                               scale=-float(np.log(10000.0) / 32)), reads=[pidx], writes=[inv])
            sgn = k.sbuf("r_sgn", [64, 1], F32)
            k.op(k.pool, lambda e: e.memset(sgn[0:32, :], -1.0), writes=[sgn])
            k.op(k.pool, lambda e: e.memset(sgn[32:64, :], 1.0), writes=[sgn])
            T = {n: k.sbuf("r_" + n, [64, W], F32, dma=(n in ("co", "si"))) for n in
                 ("pos", "th", "t", "nf", "u", "w", "p", "su", "cu", "co", "si")}
            ni = k.sbuf("r_ni", [64, W], I32)
            chunks = [(c0, min(W, C.TP - c0), c0) for c0 in range(0, C.TP, W)] + [(C.TP, 64, C.CACHE)]
            HI = 6.28125
            LO = float(2 * np.pi - HI)
            sc = [1.0 / 362880, -1.0 / 5040, 1.0 / 120, -1.0 / 6]
            cc_ = [-1.0 / 3628800, 1.0 / 40320, -1.0 / 720, 1.0 / 24, -0.5]
            for (c0, w, p0) in chunks:
                def ts(out, in0, s1, s2=None, o0=ALU.mult, o1=None, rd=()):
                    kw = dict(out=out.t[:, 0:w], in0=in0.t[:, 0:w], scalar1=s1, scalar2=s2, op0=o0)
                    if o1 is not None:
                        kw["op1"] = o1
                    k.op(k.dve, lambda e: e.tensor_scalar(**kw), reads=[in0] + list(rd), writes=[out])

                def stt(out, in0, sca, in1, o0, o1):
                    k.op(k.dve, lambda e: e.scalar_tensor_tensor(out=out.t[:, 0:w], in0=in0.t[:, 0:w], scalar=sca,
                                                                 in1=in1.t[:, 0:w], op0=o0, op1=o1),
                         reads=[in0, in1], writes=[out])
                k.op(k.pool, lambda e: e.iota(T["pos"].t[:, 0:w], pattern=[[1, w]], base=p0, channel_multiplier=0,
                                              allow_small_or_imprecise_dtypes=True), writes=[T["pos"]])
                ts(T["th"], T["pos"], inv[:], rd=[inv])
                ts(T["t"], T["th"], float(1.0 / (2 * np.pi)))
                k.op(k.dve, lambda e: e.tensor_copy(out=ni[:, 0:w], in_=T["t"].t[:, 0:w]), reads=[T["t"]], writes=[ni])
                k.op(k.dve, lambda e: e.tensor_copy(out=T["nf"].t[:, 0:w], in_=ni[:, 0:w]), reads=[ni], writes=[T["nf"]])
                stt(T["u"], T["nf"], -HI, T["th"], ALU.mult, ALU.add)
                stt(T["u"], T["nf"], -LO, T["u"], ALU.mult, ALU.add)
                ts(T["u"], T["u"], 0.5)
                k.op(k.dve, lambda e: e.tensor_tensor(out=T["w"].t[:, 0:w], in0=T["u"].t[:, 0:w], in1=T["u"].t[:, 0:w],
                                                      op=ALU.mult), reads=[T["u"]], writes=[T["w"]])
                ts(T["p"], T["w"], sc[0])
                for c in sc[1:]:
                    stt(T["p"], T["p"], c, T["w"], ALU.add, ALU.mult)
                stt(T["su"], T["p"], 1.0, T["u"], ALU.add, ALU.mult)
                ts(T["p"], T["w"], cc_[0])
                for c in cc_[1:]:
                    stt(T["p"], T["p"], c, T["w"], ALU.add, ALU.mult)
                ts(T["cu"], T["p"], 1.0, o0=ALU.add)
                stt(T["si"], T["su"], 2.0, T["cu"], ALU.mult, ALU.mult)
                ts(T["si"], T["si"], sgn[:], rd=[sgn])
                k.op(k.dve, lambda e: e.tensor_tensor(out=T["p"].t[:, 0:w], in0=T["su"].t[:, 0:w], in1=T["su"].t[:, 0:w],
                                                      op=ALU.mult), reads=[T["su"]], writes=[T["p"]])
                ts(T["co"], T["p"], -2.0, 1.0, ALU.mult, ALU.add)
                k.dma(k.sp, [(self.COS2[:, c0:c0 + w], T["co"].t[:, 0:w])], T["co"], reads=[T["co"]], writes=[self.COS2])
                k.dma(k.sp, [(self.SIN2S[:, c0:c0 + w], T["si"].t[:, 0:w])], T["si"], reads=[T["si"]], writes=[self.SIN2S])

    def proj_scratch(self, k):
        C = self.cfg
        N = C.NTOK
        S = {}
        for nm, shp, dt in (("GQT", [4, 128, N], F32), ("GKT", [4, 128, N], F32), ("GK", [N, 4, 128], F32),
                            ("GV", [N, 4, 128], F32), ("GBc", [N, 8], F32), ("GBr", [8, N], F32),
                            ("ZS", [N, 512], F32), ("QT", [4, 128, N], BF16), ("QRT", [4, 64, N], BF16),
                            ("QN2", [4, N], F32), ("KT", [4, 128, N], BF16), ("KRT", [64, N], BF16),
                            ("K2", [4, N], F32), ("V1", [N, 4, 129], BF16)):
            S[nm] = k.dram(nm, shp, dt, dma=False)
        self.S = S
        self.MIXT = k.dram("MIXT", [8, 128, N], BF16, dma=False)
        S["MIXT"] = self.MIXT

    def group_segments(self, gi, blocks):
        C = self.cfg
        segs = []
        for (r0, n, off) in blocks:
            if r0 == 0:
                segs.append((off, n, "zero", 0))
            elif r0 < C.TP:
                if segs and segs[-1][2] == "prev":
                    o, L, kd, ix = segs[-1]
                    segs[-1] = (o, L + n, kd, ix)
                else:
                    segs.append((off, n, "prev", 0))
            else:
                for j in range(0, n, 64):
                    segs.append((off + j, 64, "state", (r0 - C.TP + j) // 64))
        return segs

    def phase_proj(self, k, I, O):
        C, S, nc = self.cfg, self.S, self.nc
        B = self.banks
        hT = self.hT
        WF = self.load_resident(k, "WF", I["wf"], [128, KC, 13 * 128])
        WT = self.load_resident(k, "WT", I["wt"], [128, KC, 1160])
        WUQ = self.load_resident(k, "WUQ", I["wuq"], [128, 3, 1024])
        WUK = self.load_resident(k, "WUK", I["wuk"], [128, 2, 512])
        WUV = self.load_resident(k, "WUV", I["wuv"], [128, 2, 512])
        xt = [k.sbuf("p_xt%d" % b, [128, D], F32, dma=True) for b in range(4)]
        rawx = k.sbuf("p_rawx", [128, 12, 528], F32, dma=True)
        rawx_cc = [k.wrap(rawx.t, "p_rawx_cc%d" % c) for c in range(12)]
        ccbuf = [(k.sbuf("p_cvu%d" % i, [128, 512], F32), k.sbuf("p_sqsd%d" % i, [128, 512], F32),
                  k.sbuf("p_un%d" % i, [128, 512], F32, dma=True)) for i in range(4)]
        halo = k.sbuf("p_halo", [128, 12, 3], F32, dma=True)
        k.op(k.pool, lambda e: e.memset(halo[:], 0.0), writes=[halo])
        cqnT = k.sbuf("p_cqnT", [128, 3, 512], BF16)
        cT = k.sbuf("p_cT", [128, 2, 512], BF16)
        cos_t = k.sbuf("p_cos", [64, 512], F32, dma=True)
        sin_t = k.sbuf("p_sin", [64, 512], F32, dma=True)
        ft = [k.sbuf("p_ft%d" % i, [128, 512], F32, dma=True) for i in range(2)]
        fb = [k.sbuf("p_fb%d" % i, [128, 512], BF16, dma=True) for i in range(4)]
        tmp = [k.sbuf("p_tmp%d" % i, [128, 512], F32) for i in range(7)]
        krsq_b = k.sbuf("p_krsq", [64, 512], F32)
        tm = [k.sbuf("p_tm%d" % i, [128, 512], F32, dma=True) for i in range(4)]
        c32 = [k.sbuf("p_c32_%d" % i, [128, 256], F32, dma=True) for i in range(2)]
        cb16 = k.sbuf("p_cb16", [128, 256], BF16)
        gbt = [k.sbuf("p_gb%d" % i, [128, 16], F32, dma=True) for i in range(2)]
        gbr = k.sbuf("p_gbr", [8, 512], F32, dma=True)
        krt = k.sbuf("p_krt", [128, 256], F32, dma=True)
        v1s = [k.sbuf("p_v1_%d" % i, [128, 4, 129], BF16, dma=True) for i in range(2)]
        for v in v1s:
            k.op(k.pool, lambda e, v=v: e.memset(v[:, :, 128:129], 1.0), writes=[v])
        rows = [k.sbuf("p_row%d" % i, [1, 512], F32, dma=True) for i in range(4)]
        rr = {"ft": 0, "fb": 0, "tmp": 0, "tm": 0, "c32": 0, "gb": 0, "v1": 0, "row": 0, "fm": 0}

        def nxt(lst, key):
            b = lst[rr[key] % len(lst)]
            rr[key] += 1
            return b

        def fm_bank():
            return nxt(B[0:2], "fm")

        last_prompt_group = max(gi for gi, g in enumerate(C.groups) if any(r0 < C.TP for (r0, n) in g))
        def load_x(gj):
            for bi, (r0, n) in enumerate(C.groups[gj]):
                k.dma(k.sp, [(xt[bi][0:n, :], self.X1[r0:r0 + n, :])], xt[bi], reads=[self.X1], writes=[xt[bi]])
        load_x(0)
        for gi, grp in enumerate(C.groups):
            blocks, NT = group_layout(grp)
            segs = self.group_segments(gi, blocks)
            self.norm_group(k, blocks, xt, self.gT[:, 1, :], [B[6], B[7], B[6], B[7]])
            if gi + 1 < len(C.groups):
                load_x(gi + 1)
            pr = []
            for (r0, n, off) in blocks:
                if r0 < C.TP:
                    pr.append((off, n, r0))
                else:
                    for j in range(0, n, 64):
                        pr.append((off + j, 64, C.TP))
            k.dma(k.sp, [(cos_t[:, o:o + n], self.COS2[:, c:c + n]) for (o, n, c) in pr], cos_t,
                  reads=[self.COS2], writes=[cos_t])
            k.dma(k.sp, [(sin_t[:, o:o + n], self.SIN2S[:, c:c + n]) for (o, n, c) in pr], sin_t,
                  reads=[self.SIN2S], writes=[sin_t])

            for bi, (r0, n, off) in enumerate(blocks):
                def tok_mm(bank, c0, c1):
                    for kc in range(KC):
                        k.op(k.pe, lambda e, kc=kc: e.matmul(bank[0:n, 0:c1 - c0], lhsT=hT[:, kc, off:off + n],
                                                             rhs=WT[:, kc, c0:c1], start=(kc == 0), stop=(kc == KC - 1)),
                             reads=[hT, WT], writes=[bank], inc=(kc == KC - 1))
                tok_mm(B[3], 0, 512)
                tok_mm(B[4], 512, 896)
                tok_mm(B[5], 896, 1160)
                zs = nxt(tm, "tm")
                k.op(k.act, lambda e: e.activation(out=zs[0:n, :], in_=B[3][0:n, :], func=AF.Silu), reads=[B[3]], writes=[zs])
                k.dma(k.sp, [(S["ZS"][r0:r0 + n, :], zs[0:n, :])], zs, reads=[zs], writes=[S["ZS"]])
                self.norm_to_T(k, B[4], B[4][0:n, 0:384], n, 3, self.gqT[:, :], cqnT, off, B[6], gbuf=self.gqT)
                stt_, rstd = self.rstd_of(k, B[5][0:n, 0:256], n, 256, [B[5]])
                cc = nxt(c32, "c32")
                k.op(k.dve, lambda e: e.scalar_tensor_tensor(out=cc[0:n, :], in0=B[5][0:n, 0:256], scalar=rstd,
                                                             in1=self.gkv[0:n, :], op0=ALU.mult, op1=ALU.mult),
                     reads=[B[5], stt_, self.gkv], writes=[cc])
                k.dma(k.sp, [(O["ckv"][r0:r0 + n, :], cc[0:n, :])], cc, reads=[cc])
                k.op(k.act, lambda e: e.copy(out=cb16[0:n, :], in_=cc[0:n, :]), reads=[cc], writes=[cb16])
                tb = B[6].t[:].bitcast(BF16)
                for kc in range(2):
                    k.op(k.pe, lambda e, kc=kc: e.transpose(out=tb[:, kc * 128:kc * 128 + n],
                                                            in_=cb16[0:n, kc * 128:(kc + 1) * 128],
                                                            identity=self.identb[0:n, 0:n]),
                         reads=[cb16, self.identb], writes=[B[6]], inc=(kc == 1))
                k.op(k.act, lambda e: e.copy(out=cT[:, :, off:off + n],
                                             in_=tb[:, 0:256].rearrange("p (k t) -> p k t", t=128)[:, :, 0:n]),
                     reads=[B[6]], writes=[cT])
                gb = nxt(gbt, "gb")
                k.op(k.dve, lambda e: e.tensor_tensor(out=gb[0:n, 8:12], in0=B[5][0:n, 256:260], in1=self.abc[0:n, 4:8],
                                                      op=ALU.add), reads=[B[5], self.abc], writes=[gb])
                k.op(k.act, lambda e: e.activation(out=gb[0:n, 8:12], in_=gb[0:n, 8:12], func=AF.Exp), reads=[gb], writes=[gb])
                k.op(k.act, lambda e: e.activation(out=gb[0:n, 8:12], in_=gb[0:n, 8:12], func=AF.Ln, bias=1.0),
                     reads=[gb], writes=[gb])
                k.op(k.dve, lambda e: e.tensor_tensor(out=gb[0:n, 0:4], in0=gb[0:n, 8:12], in1=self.abc[0:n, 8:12],
                                                      op=ALU.mult), reads=[gb, self.abc], writes=[gb])
                k.op(k.act, lambda e: e.activation(out=gb[0:n, 12:16], in_=B[5][0:n, 260:264], func=AF.Exp, scale=-1.0),
                     reads=[B[5]], writes=[gb])
                k.op(k.dve, lambda e: e.tensor_scalar(out=gb[0:n, 12:16], in0=gb[0:n, 12:16], scalar1=1.0, scalar2=None,
                                                      op0=ALU.add), reads=[gb], writes=[gb])
                k.op(k.dve, lambda e: e.reciprocal(out=gb[0:n, 4:8], in_=gb[0:n, 12:16]), reads=[gb], writes=[gb])
                k.dma(k.sp, [(S["GBc"][r0:r0 + n, :], gb[0:n, 0:8])], gb, reads=[gb], writes=[S["GBc"]])
                k.op(k.pe, lambda e: e.transpose(out=B[7][0:8, 0:n], in_=gb[0:n, 0:8], identity=self.identf[0:n, 0:n]),
                     reads=[gb, self.identf], writes=[B[7]])
                k.op(k.dve, lambda e: e.tensor_copy(out=gbr[:, off:off + n], in_=B[7][0:8, 0:n]), reads=[B[7]], writes=[gbr])
                for kc in range(2):
                    k.op(k.pe, lambda e, kc=kc: e.matmul(B[3][0:n, :], lhsT=cT[:, kc, off:off + n], rhs=WUV[:, kc, :],
                                                         start=(kc == 0), stop=(kc == 1)),
                         reads=[cT, WUV], writes=[B[3]], inc=(kc == 1))
                v1 = nxt(v1s, "v1")
                k.op(k.act, lambda e: e.copy(out=v1[0:n, :, 0:128], in_=B[3][0:n, :].rearrange("p (h d) -> p h d", d=128)),
                     reads=[B[3]], writes=[v1])
                k.dma(k.sp, [(S["V1"][r0:r0 + n, :, :], v1[0:n, :, :])], v1, reads=[v1], writes=[S["V1"]])
            r00 = blocks[0][0]
            runs = []
            for (r0, n, off) in blocks:
                if runs and runs[-1][2] + runs[-1][1] == r0:
                    runs[-1] = (runs[-1][0], runs[-1][1] + n, runs[-1][2])
                else:
                    runs.append((off, n, r0))

            def store_fm(dst3, stage, rows_=128):
                k.dma(k.sp, [(dst3[:, r:r + n], stage[0:rows_, o:o + n]) for (o, n, r) in runs], stage,
                      reads=[stage], writes=[])
            k.dma(k.sp, [(S["GBr"][:, r:r + n], gbr[:, o:o + n]) for (o, n, r) in runs], gbr, reads=[gbr])

            for si, (o, L, kind, ix) in enumerate(segs):
                xo = o + 3 * si
                if kind == "zero":
                    k.op(k.pool, lambda e, xo=xo: e.memset(rawx[:, :, xo:xo + 3], 0.0), writes=rawx_cc)
                elif kind == "prev":
                    k.op(k.pool, lambda e, xo=xo: e.tensor_copy(out=rawx[:, :, xo:xo + 3], in_=halo[:]),
                         reads=[halo], writes=rawx_cc)
                else:
                    with nc.allow_non_contiguous_dma(reason="3-row conv history, transposed on load"):
                        k.dma(k.sp, [(rawx[:, :, xo + t], I["state_conv"][ix, t, :].rearrange("(cc p) -> p cc", p=128))
                                     for t in range(3)], rawx, writes=rawx_cc)
            def cc_stream(cc_i, sl):
                bank, (cvu, sqsd, un) = B[sl], ccbuf[sl]
                rx = rawx_cc[cc_i]
                h = cc_i % 4
                for kc in range(KC):
                    k.op(k.pe, lambda e, kc=kc: e.matmul(bank[:, 0:NT], lhsT=WF[:, kc, cc_i * 128:(cc_i + 1) * 128],
                                                         rhs=hT[:, kc, 0:NT], start=(kc == 0), stop=(kc == KC - 1)),
                         reads=[WF, hT], writes=[bank], inc=(kc == KC - 1))
                for si, (o, L, kind, ix) in enumerate(segs):
                    xo = o + 3 * si
                    k.op(k.act, lambda e, o=o, L=L, xo=xo: e.copy(out=rawx[:, cc_i, xo + 3:xo + 3 + L], in_=bank[:, o:o + L]),
                         reads=[bank], writes=[rx])
                for si, (o, L, kind, ix) in enumerate(segs):
                    xo = o + 3 * si
                    k.op(k.dve, lambda e, o=o, L=L, xo=xo: e.tensor_scalar(
                        out=cvu[:, o:o + L], in0=rawx[:, cc_i, xo:xo + L], scalar1=self.cwT[:, cc_i, 0:1], scalar2=None,
                        op0=ALU.mult), reads=[rx, self.cwT], writes=[cvu])
                    for j in range(1, 4):
                        k.op(k.dve, lambda e, o=o, L=L, xo=xo, j=j: e.scalar_tensor_tensor(
                            out=cvu[:, o:o + L], in0=rawx[:, cc_i, xo + j:xo + j + L], scalar=self.cwT[:, cc_i, j:j + 1],
                            in1=cvu[:, o:o + L], op0=ALU.mult, op1=ALU.add), reads=[rx, self.cwT, cvu], writes=[cvu])
                k.op(k.act, lambda e: e.activation(out=cvu[:, 0:NT], in_=cvu[:, 0:NT], func=AF.Silu), reads=[cvu], writes=[cvu])
                src = cvu
                if cc_i < 8:
                    k.op(k.act, lambda e: e.activation(out=sqsd[:, 0:NT], in_=cvu[:, 0:NT], func=AF.Square), reads=[cvu], writes=[sqsd])
                    yield
                    ob_ = B[4 + sl % 2]
                    k.op(k.pe, lambda e: e.matmul(ob_[:, 0:NT], lhsT=self.ones[:, :], rhs=sqsd[:, 0:NT], start=True, stop=True),
                         reads=[self.ones, sqsd], writes=[ob_])
                    mul = 128.0 if cc_i < 4 else 1.0
                    k.op(k.act, lambda e: e.activation(out=sqsd[:, 0:NT], in_=ob_[:, 0:NT], func=AF.Sqrt, scale=mul,
                                                       bias=self.l2b[:, (0 if cc_i < 4 else 1):(1 if cc_i < 4 else 2)]),
                         reads=[ob_, self.l2b], writes=[sqsd])
                    k.op(k.dve, lambda e: e.reciprocal(out=sqsd[:, 0:NT], in_=sqsd[:, 0:NT]), reads=[sqsd], writes=[sqsd])
                    k.op(k.dve, lambda e: e.tensor_tensor(out=un[:, 0:NT], in0=cvu[:, 0:NT], in1=sqsd[:, 0:NT], op=ALU.mult),
                         reads=[cvu, sqsd], writes=[un])
                    store_fm(S["GQT" if cc_i < 4 else "GKT"].t[h], un)
                    src = un
                if cc_i >= 4:
                    yield
                    dst = S["GK" if cc_i < 8 else "GV"]
                    tbk = B[6 + cc_i % 2]
                    for bi, (r0, n, off) in enumerate(blocks):
                        k.op(k.pe, lambda e, n=n, off=off, bi=bi: e.transpose(out=tbk[0:n, bi * 128:(bi + 1) * 128], in_=src[:, off:off + n],
                                                                             identity=self.identf[:, :]),
                             reads=[src, self.identf], writes=[tbk], inc=(bi == len(blocks) - 1))
                    st_ = nxt(tm, "tm")
                    nb_ = len(blocks)
                    k.op(k.act, lambda e, st_=st_: e.copy(out=st_[:, 0:nb_ * 128], in_=tbk[:, 0:nb_ * 128]), reads=[tbk], writes=[st_])
                    k.dma(k.sp, [(dst[r0:r0 + n, h, :], st_[0:n, bi * 128:(bi + 1) * 128]) for bi, (r0, n, off) in enumerate(blocks)],
                          st_, reads=[st_], writes=[dst])
            run_streams([(lambda sl, c=c: cc_stream(c, sl)) for c in range(12)], 4)
            for si, (o, L, kind, ix) in enumerate(segs):
                xo = o + 3 * si
                if kind in ("prev", "zero"):
                    k.op(k.pool, lambda e, xo=xo, L=L: e.tensor_copy(out=halo[:], in_=rawx[:, :, xo + L:xo + L + 3]),
                         reads=rawx_cc, writes=[halo])
                else:
                    with nc.allow_non_contiguous_dma(reason="3-row conv state out"):
                        k.dma(k.sp, [(O["sconv"][ix, t, :].rearrange("(cc p) -> p cc", p=128), rawx[:, :, xo + L + t])
                                     for t in range(3)], rawx, reads=rawx_cc)
            if gi == last_prompt_group:
                with nc.allow_non_contiguous_dma(reason="3-row conv state out"):
                    k.dma(k.sp, [(O["pconv"][t, :].rearrange("(cc p) -> p cc", p=128), halo[:, :, t]) for t in range(3)],
                          halo, reads=[halo])

            def rope(bank_a, bank_b, out_t):
                t1 = nxt(tmp, "tmp")
                k.op(k.dve, lambda e: e.tensor_tensor(out=t1[0:64, 0:NT], in0=bank_a[0:64, 0:NT], in1=cos_t[:, 0:NT], op=ALU.mult),
                     reads=[bank_a, cos_t], writes=[t1])
                t2 = nxt(tmp, "tmp")
                k.op(k.dve, lambda e: e.tensor_tensor(out=t2[0:64, 0:NT], in0=bank_b[0:64, 0:NT], in1=sin_t[:, 0:NT], op=ALU.mult),
                     reads=[bank_b, sin_t], writes=[t2])
                k.op(k.pool, lambda e: e.tensor_tensor(out=out_t[0:64, 0:NT], in0=t1[0:64, 0:NT], in1=t2[0:64, 0:NT], op=ALU.add),
                     reads=[t1, t2], writes=[out_t])

            def fm_mm(bank, Wt, nkc, c0, M, rhsT):
                for kc in range(nkc):
                    k.op(k.pe, lambda e, kc=kc: e.matmul(bank[0:M, 0:NT], lhsT=Wt[:, kc, c0:c0 + M], rhs=rhsT[:, kc, 0:NT],
                                                         start=(kc == 0), stop=(kc == nkc - 1)),
                         reads=[Wt, rhsT], writes=[bank], inc=(kc == nkc - 1))
            ba, bb = B[0], B[1]
            fm_mm(ba, WF, KC, 12 * 128, 64, hT)
            fm_mm(bb, WF, KC, 12 * 128 + 64, 64, hT)
            kro = nxt(ft, "ft")
            rope(ba, bb, kro)
            krb = nxt(fb, "fb")
            k.op(k.act, lambda e: e.copy(out=krb[0:64, 0:NT], in_=kro[0:64, 0:NT]), reads=[kro], writes=[krb])
            store_fm(S["KRT"].t, krb, 64)
            krsq = krsq_b
            k.op(k.act, lambda e: e.activation(out=krsq[0:64, 0:NT], in_=kro[0:64, 0:NT], func=AF.Square), reads=[kro], writes=[krsq])
            for bi, (r0, n, off) in enumerate(blocks):
                k.op(k.pe, lambda e, n=n, off=off, bi=bi: e.transpose(out=B[7][0:n, bi * 64:(bi + 1) * 64], in_=kro[0:64, off:off + n],
                                                                     identity=self.identf[0:64, 0:64]),
                     reads=[kro, self.identf], writes=[B[7]], inc=(bi == len(blocks) - 1))
            k.op(k.dve, lambda e: e.tensor_copy(out=krt[:, 0:len(blocks) * 64], in_=B[7][:, 0:len(blocks) * 64]), reads=[B[7]], writes=[krt])
            k.dma(k.sp, [(O["kr"][r0:r0 + n, :], krt[0:n, bi * 64:(bi + 1) * 64]) for bi, (r0, n, off) in enumerate(blocks)],
                  krt, reads=[krt])

            for h in range(4):
                bq = fm_bank()
                fm_mm(bq, WUQ, 3, h * 256, 128, cqnT)
                qb = nxt(fb, "fb")
                k.op(k.act, lambda e: e.copy(out=qb[:, 0:NT], in_=bq[:, 0:NT]), reads=[bq], writes=[qb])
                store_fm(S["QT"].t[h], qb)
                qsq = nxt(tmp, "tmp")
                k.op(k.act, lambda e: e.activation(out=qsq[:, 0:NT], in_=bq[:, 0:NT], func=AF.Square), reads=[bq], writes=[qsq])
                ba, bb = fm_bank(), B[2]
                fm_mm(ba, WUQ, 3, h * 256 + 128, 64, cqnT)
                fm_mm(bb, WUQ, 3, h * 256 + 192, 64, cqnT)
                qr = nxt(tmp, "tmp")
                rope(ba, bb, qr)
                qrb = nxt(fb, "fb")
                k.op(k.act, lambda e: e.copy(out=qrb[0:64, 0:NT], in_=qr[0:64, 0:NT]), reads=[qr], writes=[qrb])
                store_fm(S["QRT"].t[h], qrb, 64)
                qrsq = nxt(tmp, "tmp")
                k.op(k.act, lambda e: e.activation(out=qrsq[0:64, 0:NT], in_=qr[0:64, 0:NT], func=AF.Square), reads=[qr], writes=[qrsq])
                k.op(k.pe, lambda e: e.matmul(B[7][0:1, 0:NT], lhsT=self.ones[:, 0:1], rhs=qsq[:, 0:NT], start=True, stop=False),
                     reads=[self.ones, qsq], writes=[B[7]], inc=False)
                k.op(k.pe, lambda e: e.matmul(B[7][0:1, 0:NT], lhsT=self.ones[0:64, 0:1], rhs=qrsq[0:64, 0:NT], start=False, stop=True),
                     reads=[self.ones, qrsq], writes=[B[7]])
                rw = nxt(rows, "row")
                k.op(k.dve, lambda e: e.tensor_copy(out=rw[:, 0:NT], in_=B[7][0:1, 0:NT]), reads=[B[7]], writes=[rw])
                k.dma(k.sp, [(S["QN2"][h:h + 1, r:r + n], rw[:, o:o + n]) for (o, n, r) in runs], rw, reads=[rw])
                bk = fm_bank()
                fm_mm(bk, WUK, 2, h * 128, 128, cT)
                kb = nxt(fb, "fb")
                k.op(k.act, lambda e: e.copy(out=kb[:, 0:NT], in_=bk[:, 0:NT]), reads=[bk], writes=[kb])
                store_fm(S["KT"].t[h], kb)
                ksq = nxt(tmp, "tmp")
                k.op(k.act, lambda e: e.activation(out=ksq[:, 0:NT], in_=bk[:, 0:NT], func=AF.Square), reads=[bk], writes=[ksq])
                k.op(k.pe, lambda e: e.matmul(B[7][0:1, 0:NT], lhsT=self.ones[:, 0:1], rhs=ksq[:, 0:NT], start=True, stop=False),
                     reads=[self.ones, ksq], writes=[B[7]], inc=False)
                k.op(k.pe, lambda e: e.matmul(B[7][0:1, 0:NT], lhsT=self.ones[0:64, 0:1], rhs=krsq[0:64, 0:NT], start=False, stop=True),
                     reads=[self.ones, krsq], writes=[B[7]])
                rw = nxt(rows, "row")
                k.op(k.dve, lambda e: e.tensor_copy(out=rw[:, 0:NT], in_=B[7][0:1, 0:NT]), reads=[B[7]], writes=[rw])
                k.dma(k.sp, [(S["K2"][h:h + 1, r:r + n], rw[:, o:o + n]) for (o, n, r) in runs], rw, reads=[rw])


    def gdn_consts(self, k):
        def sel(name, src_val, fill, pattern, cm, op):
            t = k.sbuf(name, [128, 128], F32)
            src = self.ones if src_val == 1.0 else self.zeros
            k.op(k.pool, lambda e: e.affine_select(out=t[:], in_=src[:], pattern=pattern, compare_op=op, fill=fill,
                                                   base=0, channel_multiplier=cm), reads=[src], writes=[t])
            return t
        self.zeros = k.sbuf("g_zeros", [128, 128], F32)
        k.op(k.pool, lambda e: e.memset(self.zeros[:], 0.0), writes=[self.zeros])
        BIG = 30000.0
        self.triu = sel("g_triu", 1.0, 0.0, [[1, 128]], -1, ALU.is_ge)
        self.mmin_incl = sel("g_mmin", 0.0, -BIG, [[1, 128]], -1, ALU.is_ge)
        self.strict01 = sel("g_st01", 1.0, 0.0, [[1, 128]], -1, ALU.is_gt)
        self.mmax_strict = sel("g_mmax", 0.0, BIG, [[-1, 128]], 1, ALU.is_gt)
        gg = k.sbuf("g_gain", [128, 128], F32, dma=True)
        k.dma(k.sp, [(gg[:], self.I["gdn_norm"].partition_broadcast(128))], gg, writes=[gg])
        self.ggain = gg

    def phase_gdn(self, k, I, O):
        C, S, B = self.cfg, self.S, self.banks
        self.gdn_consts(k)
        idf = self.identf
        seqs = [("p", 0, [(0, 16)] + [(16 + 128 * i, 128) for i in range(C.SEQ // 128)])]
        for b in range(C.NSB):
            seqs.append(("s", b, [(C.TP + 64 * b, 64)]))
        F = lambda nm, shp, dma=False: k.sbuf(nm, shp, F32, dma=dma)

        def FR(nm, shp):
            return k.sbuf(nm, shp, F32)
        NB = 2
        inp = [dict(qT=F("gi_qT%d" % i, [128, 4, 128], True), kT=F("gi_kT%d" % i, [128, 4, 128], True),
                    ktm=F("gi_ktm%d" % i, [128, 4, 128], True), vtm=F("gi_vtm%d" % i, [128, 4, 128], True),
                    grow=F("gi_grow%d" % i, [128, 4, 128], True), brow=F("gi_brow%d" % i, [128, 4, 128], True),
                    gbc=F("gi_gbc%d" % i, [128, 8], True), zs=F("gi_zs%d" % i, [128, 512], True)) for i in range(NB)]
        hand = [dict(WT=FR("gh_WT%d" % i, [128, 4, 128]), U0=F("gh_U0%d" % i, [128, 4, 128]),
                     QKd=FR("gh_QKd%d" % i, [128, 4, 128]), Kw=FR("gh_Kw%d" % i, [128, 4, 128]),
                     qe=FR("gh_qe%d" % i, [128, 4, 128]), gam=F("gh_gam%d" % i, [128, 4])) for i in range(NB)]
        Gb = F("g_Gb", [128, 4, 128]); gcol = F("g_gcol", [128, 4]); egc = F("g_egc", [128, 4]); bec = F("g_bec", [128, 4])
        wcol = F("g_wcol", [128, 4]); egr = F("g_egr", [128, 4, 128]); kbT = FR("g_kbT", [128, 4, 128])
        Kbe = FR("g_Kbe", [128, 4, 128]); bv = FR("g_bv", [128, 4, 128])
        dtmp = F("g_dtmp", [128, 4, 128]); DTi = F("g_DTi", [128, 4, 128]); DTs = F("g_DTs", [128, 4, 128]); Ds = F("g_Ds", [128, 4, 128])
        Np = [FR("g_N%d" % i, [128, 4, 128]) for i in range(2)]
        Bp = [FR("g_B%d" % i, [128, 4, 128]) for i in range(2)]
        R = FR("g_R", [128, 4, 128])
        kTr = FR("g_kTr", [128, 4, 128]); qTr = FR("g_qTr", [128, 4, 128]); Mr = FR("g_Mr", [128, 4, 128])
        g1_r = None
        NpH = [[k.wrap(Np[i].t, "g_N%d_%d" % (i, hf)) for hf in range(2)] for i in range(2)]
        BpH = [[k.wrap(Bp[i].t, "g_B%d_%d" % (i, hf)) for hf in range(2)] for i in range(2)]
        RH = [k.wrap(R.t, "g_R_%d" % hf) for hf in range(2)]
        M = [F("g_M%d" % i, [128, 4, 128], True) for i in range(2)]
        u = FR("g_u", [128, 4, 128])
        onr = F("g_onr", [128, 8]); og = F("g_og", [128, 512]); ogb = k.sbuf("g_ogb", [128, 512], BF16)
        mixs = [k.sbuf("g_mix%d" % i, [128, 4, 128], BF16, dma=True) for i in range(2)]
        g1_r = [kbT, Kbe, bv, Np[0], Np[1], Bp[0], Bp[1], R, kTr, qTr]

        def setmode(tiles, flag):
            for t_ in tiles:
                t_.rmode = flag
        ones_row = F("g_ones1", [128, 128])
        k.op(k.pool, lambda e: e.memset(ones_row[:], 1.0), writes=[ones_row])
        ci = 0
        mcur = 0
        for (kind, bidx, chunks) in seqs:
            Mc = M[mcur]
            if kind == "p":
                k.op(k.pool, lambda e, Mc=Mc: e.memset(Mc[:], 0.0), writes=[Mc])
            else:
                k.dma(k.sp, [(Mc[:, h, :], I["state_gdn"][bidx, h, :, :]) for h in range(4)], Mc, writes=[Mc])
            k.op(k.act, lambda e, Mc=Mc: e.activation(func=AF.Copy, out=Mr.t[:], in_=Mc.t[:]), reads=[Mc], writes=[Mr])
            pend = None
            def g1_gen(item_, ci_, res_):
                r0, Lr = item_
                X, H = inp[ci_ % NB], hand[ci_ % NB]
                setmode(g1_r + [H["WT"], H["QKd"], H["Kw"], H["qe"]], True)
                L = Lr
                r1 = r0 + L
                if Lr < 128:
                    for nm in ("ktm", "vtm", "gbc"):
                        k.op(k.pool, lambda e, nm=nm: e.memset(X[nm][:], 0.0), writes=[X[nm]])
                k.dma(k.sp, [(X["qT"][:, :, 0:L], S["GQT"].t[:, :, r0:r1].rearrange("h p t -> p h t"))], X["qT"], writes=[X["qT"]])
                k.dma(k.sp, [(X["kT"][:, :, 0:L], S["GKT"].t[:, :, r0:r1].rearrange("h p t -> p h t"))], X["kT"], writes=[X["kT"]])
                k.dma(k.sp, [(X["ktm"][0:L, :, :], S["GK"].t[r0:r1, :, :])], X["ktm"], writes=[X["ktm"]])
                k.dma(k.sp, [(X["vtm"][0:L, :, :], S["GV"].t[r0:r1, :, :])], X["vtm"], writes=[X["vtm"]])
                k.dma(k.sp, [(X["grow"][:, h, 0:L], S["GBr"].t[h, r0:r1].partition_broadcast(128)) for h in range(4)],
                      X["grow"], writes=[X["grow"]])
                k.dma(k.sp, [(X["brow"][:, h, 0:L], S["GBr"].t[4 + h, r0:r1].partition_broadcast(128)) for h in range(4)],
                      X["brow"], writes=[X["brow"]])
                k.dma(k.sp, [(X["gbc"][0:L, :], S["GBc"].t[r0:r1, :])], X["gbc"], writes=[X["gbc"]])
                k.dma(k.sp, [(X["zs"][0:L, :], S["ZS"].t[r0:r1, :])], X["zs"], writes=[X["zs"]])
                if Lr < 128:
                    for nm in ("qT", "kT", "grow", "brow"):
                        k.op(k.pool, lambda e, nm=nm: e.memset(X[nm][:, :, Lr:128], 0.0), writes=[X[nm]])
                L = 128
                qT, kT, ktm, vtm, grow, brow, gbc = (X[n] for n in ("qT", "kT", "ktm", "vtm", "grow", "brow", "gbc"))
                for h in range(4):
                    k.op(k.dve, lambda e, h=h: e.tensor_tensor_scan(out=Gb[:, h, 0:L], data0=ones_row[:, 0:L], data1=grow[:, h, 0:L],
                                                                   initial=0.0, op0=ALU.mult, op1=ALU.add),
                         reads=[ones_row, grow], writes=[Gb])
                k.op(k.pe, lambda e: e.matmul(B[0][0:L, 0:4], lhsT=self.triu[0:L, 0:L], rhs=gbc[0:L, 0:4], start=True, stop=True),
                     reads=[self.triu, gbc], writes=[B[0]])
                k.op(k.dve, lambda e: e.tensor_copy(out=gcol[0:L, :], in_=B[0][0:L, 0:4]), reads=[B[0]], writes=[gcol])
                k.op(k.act, lambda e: e.activation(out=egc[0:L, :], in_=gcol[0:L, :], func=AF.Exp), reads=[gcol], writes=[egc])
                k.op(k.dve, lambda e: e.tensor_tensor(out=bec[0:L, :], in0=egc[0:L, :], in1=gbc[0:L, 4:8], op=ALU.mult),
                     reads=[egc, gbc], writes=[bec])
                k.op(k.act, lambda e: e.activation(out=H["gam"][:, :], in_=Gb[:, :, L - 1], func=AF.Exp), reads=[Gb], writes=[H["gam"]])
                k.op(k.dve, lambda e: e.tensor_tensor(out=wcol[0:L, :], in0=Gb[0:L, :, L - 1], in1=gcol[0:L, :], op=ALU.subtract),
                     reads=[Gb, gcol], writes=[wcol])
                k.op(k.act, lambda e: e.activation(out=wcol[0:L, :], in_=wcol[0:L, :], func=AF.Exp), reads=[wcol], writes=[wcol])
                k.op(k.act, lambda e: e.activation(out=egr[:, :, 0:L], in_=Gb[:, :, 0:L], func=AF.Exp), reads=[Gb], writes=[egr])
                k.op(k.dve, lambda e: e.tensor_tensor(out=H["qe"][:, :, 0:L], in0=qT[:, :, 0:L], in1=egr[:, :, 0:L], op=ALU.mult),
                     reads=[qT, egr], writes=[H["qe"]])
                k.op(k.dve, lambda e: e.tensor_tensor(out=kbT[:, :, 0:L], in0=kT[:, :, 0:L], in1=brow[:, :, 0:L], op=ALU.mult),
                     reads=[kT, brow], writes=[kbT])
                k.op(k.act, lambda e: e.activation(func=AF.Copy, out=kTr[:, :, 0:L], in_=kT[:, :, 0:L]), reads=[kT], writes=[kTr])
                k.op(k.act, lambda e: e.activation(func=AF.Copy, out=qTr[:, :, 0:L], in_=qT[:, :, 0:L]), reads=[qT], writes=[qTr])
                bc3 = lambda col: col.unsqueeze(2).to_broadcast([L, 4, 128])
                k.op(k.dve, lambda e: e.tensor_tensor(out=Kbe[0:L], in0=ktm[0:L], in1=bc3(bec[0:L, :]), op=ALU.mult),
                     reads=[ktm, bec], writes=[Kbe])
                k.op(k.dve, lambda e: e.tensor_tensor(out=H["Kw"][0:L], in0=ktm[0:L], in1=bc3(wcol[0:L, :]), op=ALU.mult),
                     reads=[ktm, wcol], writes=[H["Kw"]])
                k.op(k.dve, lambda e: e.tensor_tensor(out=bv[0:L], in0=vtm[0:L], in1=bc3(gbc[0:L, 4:8]), op=ALU.mult),
                     reads=[vtm, gbc], writes=[bv])
                yield
                for h in range(4):
                    k.op(k.dve, lambda e, h=h: e.scalar_tensor_tensor(out=dtmp[0:L, h, 0:L], in0=Gb[0:L, h, 0:L], scalar=gcol[0:L, h:h + 1],
                                                                     in1=self.mmin_incl[0:L, 0:L], op0=ALU.subtract, op1=ALU.min),
                         reads=[Gb, gcol, self.mmin_incl], writes=[dtmp])
                k.op(k.act, lambda e: e.activation(out=DTi[0:L, :, 0:L], in_=dtmp[0:L, :, 0:L], func=AF.Exp), reads=[dtmp], writes=[DTi])
                k.op(k.pool, lambda e: e.tensor_tensor(out=DTs[0:L, :, 0:L], in0=DTi[0:L, :, 0:L],
                                                       in1=self.strict01[0:L, 0:L].unsqueeze(1).to_broadcast([L, 4, L]), op=ALU.mult),
                     reads=[DTi, self.strict01], writes=[DTs])
                for h in range(4):
                    k.op(k.dve, lambda e, h=h: e.scalar_tensor_tensor(out=dtmp[0:L, h, 0:L], in0=Gb[0:L, h, 0:L], scalar=gcol[0:L, h:h + 1],
                                                                     in1=self.mmax_strict[0:L, 0:L], op0=ALU.subtract, op1=ALU.max),
                         reads=[Gb, gcol, self.mmax_strict], writes=[dtmp])
                k.op(k.act, lambda e: e.activation(out=Ds[0:L, :, 0:L], in_=dtmp[0:L, :, 0:L], func=AF.Exp, scale=-1.0),
                     reads=[dtmp], writes=[Ds])
                yield
                N0, B0 = Np[0], Bp[0]
                for h in range(4):
                    cs = slice(h * 128, h * 128 + L)
                    k.op(k.pe, lambda e, h=h, cs=cs: e.matmul(B[0][0:L, cs], lhsT=kbT[:, h, 0:L], rhs=kTr[:, h, 0:L], start=True, stop=True),
                         reads=[kbT, kTr], writes=[B[0]])
                    k.op(k.pe, lambda e, h=h, cs=cs: e.matmul(B[1][0:L, cs], lhsT=kTr[:, h, 0:L], rhs=kbT[:, h, 0:L], start=True, stop=True),
                         reads=[kbT, kTr], writes=[B[1]])
                    k.op(k.pe, lambda e, h=h, cs=cs: e.matmul(B[2][0:L, cs], lhsT=kTr[:, h, 0:L], rhs=qTr[:, h, 0:L], start=True, stop=True),
                         reads=[kTr, qTr], writes=[B[2]])
                v4 = lambda bank: bank.t[:].rearrange("p (h t) -> p h t", t=128)[0:L, :, 0:L]
                k.op(k.dve, lambda e: e.scalar_tensor_tensor(out=N0[0:L, :, 0:L], in0=v4(B[0]), scalar=-1.0, in1=Ds[0:L, :, 0:L],
                                                             op0=ALU.mult, op1=ALU.mult), reads=[B[0], Ds], writes=[N0])
                k.op(k.dve, lambda e: e.scalar_tensor_tensor(out=B0[0:L, :, 0:L], in0=v4(B[1]), scalar=-1.0, in1=DTs[0:L, :, 0:L],
                                                             op0=ALU.mult, op1=ALU.mult), reads=[B[1], DTs], writes=[B0])
                k.op(k.dve, lambda e: e.tensor_tensor(out=H["QKd"][0:L, :, 0:L], in0=v4(B[2]), in1=DTi[0:L, :, 0:L], op=ALU.mult),
                     reads=[B[2], DTi], writes=[H["QKd"]])
                k.op(k.dve, lambda e: e.tensor_tensor(out=R[0:L, :, 0:L], in0=B0.f32((slice(0, L), slice(None), slice(0, L))),
                                                       in1=idf[0:L, 0:L].unsqueeze(1).to_broadcast([L, 4, L]), op=ALU.add),
                     reads=[B0, idf], writes=[R])
                J = 0
                while (1 << (J + 1)) < L:
                    J += 1
                def sq_stream(hs, sl):
                    sqb, rb = (B[3], B[4])[sl], (B[2], B[1])[sl]
                    hsl = slice(hs[0], hs[-1] + 1)
                    p2 = lambda bank, c0: bank.t[:, c0:c0 + 256].rearrange("p (h t) -> p h t", t=128)[0:L, :, 0:L]
                    for j in range(1, J + 1):
                        Nn, Bn, No, Bo = NpH[j % 2][sl], BpH[j % 2][sl], NpH[(j - 1) % 2][sl], BpH[(j - 1) % 2][sl]
                        Nn_t, Bn_t, No_t, Bo_t = Np[j % 2], Bp[j % 2], Np[(j - 1) % 2], Bp[(j - 1) % 2]
                        for hl, h in enumerate(hs):
                            k.op(k.pe, lambda e, h=h, hl=hl: e.matmul(sqb[0:L, hl * 128:hl * 128 + L], lhsT=Bo_t[0:L, h, 0:L], rhs=No_t[0:L, h, 0:L],
                                                                     start=True, stop=True), reads=[Bo, No], writes=[sqb])
                            if j < J:
                                k.op(k.pe, lambda e, h=h, hl=hl: e.matmul(sqb[0:L, 256 + hl * 128:256 + hl * 128 + L], lhsT=No_t[0:L, h, 0:L],
                                                                         rhs=Bo_t[0:L, h, 0:L], start=True, stop=True), reads=[Bo, No], writes=[sqb])
                        k.op(k.act, lambda e: e.activation(func=AF.Copy, out=Nn_t[0:L, hsl, 0:L], in_=p2(sqb, 0)), reads=[sqb], writes=[Nn])
                        if j < J:
                            k.op(k.act, lambda e: e.activation(func=AF.Copy, out=Bn_t[0:L, hsl, 0:L], in_=p2(sqb, 256)), reads=[sqb], writes=[Bn])
                        yield
                        for hl, h in enumerate(hs):
                            k.op(k.pe, lambda e, h=h, hl=hl: e.matmul(rb[0:L, hl * 128:hl * 128 + L], lhsT=Nn_t[0:L, h, 0:L], rhs=R[0:L, h, 0:L],
                                                                     start=True, stop=True), reads=[Nn, RH[sl]], writes=[rb])
                        k.op(k.dve, lambda e: e.tensor_tensor(out=R[0:L, hsl, 0:L], in0=p2(rb, 0), in1=R.f32((slice(0, L), hsl, slice(0, L))), op=ALU.add),
                             reads=[rb, RH[sl]], writes=[RH[sl]])
                        yield
                pairs_ = ((Np[0], NpH[0]), (Np[1], NpH[1]), (Bp[0], BpH[0]), (Bp[1], BpH[1]), (R, RH))
                for whole, halves in pairs_:
                    k.split(whole, halves)
                alive = [sq_stream(hs, sl) for sl, hs in enumerate(((0, 1), (2, 3)))]
                while alive:
                    for sg in list(alive):
                        try:
                            next(sg)
                        except StopIteration:
                            alive.remove(sg)
                    yield
                for whole, halves in pairs_:
                    k.merge(whole, halves)
                for h in range(4):
                    k.op(k.pe, lambda e, h=h: e.matmul(B[0][:, h * 128:h * 128 + L], lhsT=Kbe[0:L, h, :], rhs=R[0:L, h, 0:L], start=True, stop=True),
                         reads=[Kbe, R], writes=[B[0]])
                    k.op(k.pe, lambda e, h=h: e.matmul(B[1][0:L, h * 128:(h + 1) * 128], lhsT=R[0:L, h, 0:L], rhs=bv[0:L, h, :], start=True, stop=True),
                         reads=[R, bv], writes=[B[1]])
                k.op(k.act, lambda e: e.activation(func=AF.Copy, out=H["WT"][:, :, 0:L], in_=B[0].t[:].rearrange("p (h t) -> p h t", t=128)[:, :, 0:L]),
                     reads=[B[0]], writes=[H["WT"]])
                k.op(k.act, lambda e: e.copy(out=H["U0"][0:L], in_=B[1].t[:].rearrange("p (h t) -> p h t", t=128)[0:L]),
                     reads=[B[1]], writes=[H["U0"]])
                res_[0] = (r0, Lr, X, H)

            def g2_gen(pend_, mc_):
                (q0, Lo, Xq, Hq) = pend_
                Lq = 128
                Mo, Mn = M[mc_], M[1 - mc_]
                setmode([u, Mr, Hq["WT"], Hq["QKd"], Hq["Kw"], Hq["qe"]], True)
                for h in range(4):
                    k.op(k.pe, lambda e, h=h: e.matmul(B[5][0:Lq, h * 128:(h + 1) * 128], lhsT=Hq["WT"][:, h, 0:Lq], rhs=Mr[:, h, :], start=True, stop=True),
                         reads=[Hq["WT"], Mr], writes=[B[5]])
                k.op(k.dve, lambda e: e.scalar_tensor_tensor(out=u[0:Lq], in0=B[5].t[:].rearrange("p (h t) -> p h t", t=128)[0:Lq], scalar=-1.0,
                                                             in1=Hq["U0"][0:Lq], op0=ALU.mult, op1=ALU.add),
                     reads=[B[5], Hq["U0"]], writes=[u])
                yield
                for h in range(4):
                    k.op(k.pe, lambda e, h=h: e.matmul(B[6][:, h * 128:(h + 1) * 128], lhsT=Hq["Kw"][0:Lq, h, :], rhs=u[0:Lq, h, :], start=True, stop=True),
                         reads=[Hq["Kw"], u], writes=[B[6]])
                for h in range(4):
                    k.op(k.pe, lambda e, h=h: e.matmul(B[7][0:Lq, h * 128:(h + 1) * 128], lhsT=Hq["qe"][:, h, 0:Lq], rhs=Mr[:, h, :], start=True, stop=False),
                         reads=[Hq["qe"], Mr], writes=[B[7]], inc=False)
                    k.op(k.pe, lambda e, h=h: e.matmul(B[7][0:Lq, h * 128:(h + 1) * 128], lhsT=Hq["QKd"][0:Lq, h, 0:Lq], rhs=u[0:Lq, h, :], start=False, stop=True),
                         reads=[Hq["QKd"], u], writes=[B[7]])
                for h in range(4):
                    k.op(k.dve, lambda e, h=h: e.scalar_tensor_tensor(out=Mn[:, h, :], in0=Mo[:, h, :], scalar=Hq["gam"][:, h:h + 1],
                                                                     in1=B[6][:, h * 128:(h + 1) * 128], op0=ALU.mult, op1=ALU.add),
                         reads=[Mo, Hq["gam"], B[6]], writes=[Mn])
                k.op(k.act, lambda e: e.activation(func=AF.Copy, out=Mr.t[:], in_=Mn.t[:]), reads=[Mn], writes=[Mr])
                yield
                o3 = B[7].t[:].rearrange("p (h d) -> p h d", d=128)
                for h in range(4):
                    k.op(k.act, lambda e, h=h: e.activation(out=self.junk[0:Lo, 0:128], in_=B[7][0:Lo, h * 128:(h + 1) * 128], func=AF.Square,
                                                           accum_out=onr[0:Lo, h:h + 1]), reads=[B[7]], writes=[self.junk, onr])
                k.op(k.act, lambda e: e.activation(out=onr[0:Lo, 4:8], in_=onr[0:Lo, 0:4], func=AF.Sqrt, bias=self.epsb[0:Lo, :], scale=1.0 / 128),
                     reads=[onr, self.epsb], writes=[onr])
                k.op(k.dve, lambda e: e.reciprocal(out=onr[0:Lo, 4:8], in_=onr[0:Lo, 4:8]), reads=[onr], writes=[onr])
                og3 = og.t[:].rearrange("p (h d) -> p h d", d=128)
                k.op(k.dve, lambda e: e.tensor_tensor(out=og3[0:Lo], in0=o3[0:Lo], in1=onr[0:Lo, 4:8].unsqueeze(2).to_broadcast([Lo, 4, 128]), op=ALU.mult),
                     reads=[B[7], onr], writes=[og])
                k.op(k.pool, lambda e: e.tensor_tensor(out=og3[0:Lo], in0=og3[0:Lo], in1=self.ggain[0:Lo, :].unsqueeze(1).to_broadcast([Lo, 4, 128]), op=ALU.mult),
                     reads=[og, self.ggain], writes=[og])
                k.op(k.pool, lambda e: e.tensor_tensor(out=ogb[0:Lo, :], in0=og[0:Lo, :], in1=Xq["zs"][0:Lo, :], op=ALU.mult),
                     reads=[og, Xq["zs"]], writes=[ogb])
                yield
                tb = B[5].t[:].bitcast(BF16)
                for h in range(4):
                    k.op(k.pe, lambda e, h=h: e.transpose(out=tb[:, h * 128:h * 128 + Lo], in_=ogb[0:Lo, h * 128:(h + 1) * 128], identity=self.identb[0:Lo, 0:Lo]),
                         reads=[ogb, self.identb], writes=[B[5]], inc=(h == 3))
                mx = mixs[mc_]
                k.op(k.act, lambda e: e.copy(out=mx[:, :, 0:Lo], in_=tb[:, 0:512].rearrange("p (h t) -> p h t", t=128)[:, :, 0:Lo]),
                     reads=[B[5]], writes=[mx])
                k.dma(k.sp, [(self.MIXT.t[0:4, :, q0:q0 + Lo].rearrange("h p t -> p h t"), mx[:, :, 0:Lo])], mx, reads=[mx], writes=[self.MIXT])

            for item in chunks + [None]:
                res = [None]
                gens = []
                if item is not None:
                    gens.append(g1_gen(item, ci, res))
                    ci += 1
                if pend is not None:
                    gens.append(g2_gen(pend, mcur))
                    mcur = 1 - mcur
                while gens:
                    for gg in list(gens):
                        try:
                            next(gg)
                        except StopIteration:
                            gens.remove(gg)
                pend = res[0]
            Mf = M[mcur]
            dst = O["pgdn"] if kind == "p" else O["sgdn"][bidx]
            k.dma(k.sp, [(dst[h, :, :], Mf[:, h, :]) for h in range(4)], Mf, reads=[Mf])
            mcur = 1 - mcur


    SM = (128 + 64) ** -0.5

    def attn_finish(self, k, accb, nq, h, rows0, ob, rec):
        B = self.banks
        k.op(k.dve, lambda e: e.reciprocal(out=rec[0:nq, :], in_=accb[0:nq, 128:129]), reads=[accb], writes=[rec])
        k.op(k.act, lambda e: e.activation(out=ob[0:nq, :], in_=accb[0:nq, 0:128], func=AF.Copy, scale=rec[0:nq, :]),
             reads=[accb, rec], writes=[ob])
        tb = B[6].t[:].bitcast(BF16)
        k.op(k.pe, lambda e: e.transpose(out=tb[:, 0:nq], in_=ob[0:nq, :], identity=self.identb[0:nq, 0:nq]),
             reads=[ob, self.identb], writes=[B[6]])
        st = self.a_st[self.a_cnt % 2]
        self.a_cnt += 1
        k.op(k.dve, lambda e: e.tensor_copy(out=st[:, 0:nq], in_=tb[:, 0:nq]), reads=[B[6]], writes=[st])
        k.dma(k.sp, [(self.MIXT.t[4 + h, :, rows0:rows0 + nq], st[:, 0:nq])], st, reads=[st], writes=[self.MIXT])

    def stab_row(self, k, QR, ncol, qn2_src, k2max, tmp65):
        for c0 in range(0, ncol, 2048):
            w = min(2048, ncol - c0)
            k.dma(k.sp, [(tmp65[64:65, 0:w], qn2_src[:, c0:c0 + w])], tmp65, writes=[tmp65])
            k.op(k.dve, lambda e: e.tensor_scalar(out=tmp65[64:65, 0:w], in0=tmp65[64:65, 0:w], scalar1=k2max[64:65, 0:1],
                                                  scalar2=None, op0=ALU.mult), reads=[tmp65, k2max], writes=[tmp65])
            k.op(k.act, lambda e: e.activation(out=tmp65[64:65, 0:w], in_=tmp65[64:65, 0:w], func=AF.Sqrt),
                 reads=[tmp65], writes=[tmp65])
            k.op(k.dve, lambda e, c0=c0: e.tensor_scalar(out=QR[64:65, c0:c0 + w], in0=tmp65[64:65, 0:w], scalar1=-1.0, scalar2=None,
                                                         op0=ALU.mult), reads=[tmp65], writes=[QR])

    def row_max(self, k, src_row, ncol, k2m, tmp65, src_buf=None):
        for ci, c0 in enumerate(range(0, ncol, 2048)):
            w = min(2048, ncol - c0)
            k.dma(k.sp, [(tmp65[64:65, 0:w], src_row[:, c0:c0 + w])], tmp65, reads=([src_buf] if src_buf else []), writes=[tmp65])
            dst = k2m[64:65, 0:1] if ci == 0 else k2m[64:65, 1:2]
            k.op(k.dve, lambda e, dst=dst: e.reduce_max(out=dst, in_=tmp65[64:65, 0:w], axis=mybir.AxisListType.X),
                 reads=[tmp65], writes=[k2m])
            if ci > 0:
                k.op(k.dve, lambda e: e.tensor_tensor(out=k2m[64:65, 0:1], in0=k2m[64:65, 0:1], in1=k2m[64:65, 1:2], op=ALU.max),
                     reads=[k2m], writes=[k2m])

    def attn_bufs(self, k, TWk, TWq, nvt):
        A = dict(KT=k.sbuf("a_KT", [128, TWk], BF16, dma=True), KR=k.sbuf("a_KR", [65, TWk], BF16, dma=True),
                 QT=k.sbuf("a_QT", [128, TWq], BF16, dma=True), QR=k.sbuf("a_QR", [65, TWq], BF16, dma=True),
                 V=k.sbuf("a_V", [128, nvt, 129], BF16, dma=True), t65=k.sbuf("a_t65", [65, 2048], F32, dma=True),
                 k2m=k.sbuf("a_k2m", [65, 2], F32), PT=[k.sbuf("a_PT%d" % i, [128, 512], BF16) for i in range(3)],
                 ob=k.sbuf("a_ob", [128, 128], BF16), rec=k.sbuf("a_rec", [128, 1], F32))
        self.a_st = [k.sbuf("a_st%d" % i, [128, 128], BF16, dma=True) for i in range(2)]
        self.a_cnt = 0
        k.op(k.pool, lambda e: e.memset(A["KR"][64:65, :], 1.0), writes=[A["KR"]])
        return A

    def make_attend(self, k, A):
        B, SM = self.banks, self.SM
        KTt, KRt, QTt, QRt, Vt, PT = A["KT"], A["KR"], A["QT"], A["QR"], A["V"], A["PT"]
        pcnt = [0]

        def attend(keytiles, qcol0, nq, accs, last_tile_of, diag=None, pre_pv=None):
            n = len(keytiles)
            st = {}

            def scores(ti):
                c0, nk, vt = keytiles[ti]
                vis = [bi for bi in range(len(accs)) if last_tile_of[bi] >= ti]
                q0 = accs[vis[0]][1]
                sb = B[pcnt[0] % 2]
                pt = PT[pcnt[0] % 3]
                pcnt[0] += 1
                k.op(k.pe, lambda e: e.matmul(sb[0:nk, q0:nq], lhsT=KTt[:, c0:c0 + nk], rhs=QTt[:, qcol0 + q0:qcol0 + nq], start=True, stop=False),
                     reads=[KTt, QTt], writes=[sb], inc=False)
                k.op(k.pe, lambda e: e.matmul(sb[0:nk, q0:nq], lhsT=KRt[0:65, c0:c0 + nk], rhs=QRt[0:65, qcol0 + q0:qcol0 + nq], start=False, stop=True),
                     reads=[KRt, QRt], writes=[sb])
                k.op(k.act, lambda e: e.activation(out=pt[0:nk, q0:nq], in_=sb[0:nk, q0:nq], func=AF.Exp, scale=SM), reads=[sb], writes=[pt])
                for bi in vis:
                    if diag is not None and diag[bi] == ti:
                        qo = accs[bi][1]
                        k.op(k.pool, lambda e, qo=qo: e.memset(pt[64:128, qo:qo + 64], 0.0), writes=[pt])
                st[ti] = (pt, vis)

            DEPTH = 2
            for t_ in range(min(DEPTH, n)):
                scores(t_)
            if pre_pv is not None:
                pre_pv()
            for ti, (c0, nk, vt) in enumerate(keytiles):
                if ti + DEPTH < n:
                    scores(ti + DEPTH)
                pt, vis = st.pop(ti)
                for bi in vis:
                    bank, qo, nqb = accs[bi]
                    k.op(k.pe, lambda e, bank=bank, qo=qo, nqb=nqb: e.matmul(bank[0:nqb, 0:129], lhsT=pt[0:nk, qo:qo + nqb], rhs=Vt[0:nk, vt, :],
                                                                            start=(ti == 0), stop=(ti == last_tile_of[bi])),
                         reads=[pt, Vt], writes=[bank], inc=(ti == last_tile_of[bi] or bi == vis[-1]))
        return attend

    def phase_mla_prompt(self, k, I, O):
        C, S, B = self.cfg, self.S, self.banks
        TP = C.TP
        NT_ = 1 + C.SEQ // 128
        A0 = self.attn_bufs(k, TP, TP, NT_)
        A1 = dict(A0)
        A1.update(KT=k.sbuf("a_KT_b", [128, TP], BF16, dma=True), QT=k.sbuf("a_QT_b", [128, TP], BF16, dma=True),
                  QR=k.sbuf("a_QR_b", [65, TP], BF16, dma=True), V=k.sbuf("a_V_b", [128, NT_, 129], BF16, dma=True))
        sets = [A0, A1]
        attends = [self.make_attend(k, A0), self.make_attend(k, A1)]
        KRt = A0["KR"]
        k.dma(k.sp, [(KRt[0:64, 0:TP], S["KRT"].t[:, 0:TP])], KRt, writes=[KRt])

        def load_head(h):
            A = sets[h % 2]
            KTt, QTt, QRt, Vt = A["KT"], A["QT"], A["QR"], A["V"]
            k.dma(k.sp, [(KTt[:, 0:TP], S["KT"].t[h, :, 0:TP])], KTt, writes=[KTt])
            k.dma(k.sp, [(QTt[:, 0:TP], S["QT"].t[h, :, 0:TP])], QTt, writes=[QTt])
            k.dma(k.sp, [(QRt[0:64, 0:TP], S["QRT"].t[h, :, 0:TP])], QRt, writes=[QRt])
            vsrc = S["V1"].t[16:TP, h, :].rearrange("(i p) c -> p i c", p=128)
            k.dma(k.sp, [(Vt[0:16, 0, :], S["V1"].t[0:16, h, :])] +
                  [(Vt[:, 1 + i0:1 + min(i0 + 16, NT_ - 1), :], vsrc[:, i0:min(i0 + 16, NT_ - 1), :]) for i0 in range(0, NT_ - 1, 16)],
                  Vt, writes=[Vt])
            self.row_max(k, S["K2"].t[h:h + 1, 0:TP], TP, A["k2m"], A["t65"])
            self.stab_row(k, QRt, TP, S["QN2"].t[h:h + 1, 0:TP], A["k2m"], A["t65"])
        load_head(0)
        for h in range(4):
            A, attend = sets[h % 2], attends[h % 2]
            if h + 1 < 4:
                load_head(h + 1)
            pending_fin = None
            attend([(0, 16, 0)], 0, 16, [(B[2], 0, 16)], [0])
            self.attn_finish(k, B[2], 16, h, 0, A["ob"], A["rec"])
            nfb = C.SEQ // 128
            for g0 in range(0, nfb, 4):
                nb = min(4, nfb - g0)
                tiles = [(0, 16, 0)] + [(16 + 128 * i, 128, 1 + i) for i in range(g0 + nb)]
                accs = [(B[2 + bi], bi * 128, 128) for bi in range(nb)]
                last = [1 + g0 + bi for bi in range(nb)]
                attend(tiles, 16 + 128 * g0, nb * 128, accs, last, diag=last, pre_pv=pending_fin)

                def pending_fin(nb=nb, g0=g0, h=h, A=A):
                    for bi in range(nb):
                        self.attn_finish(k, B[2 + bi], 128, h, 16 + 128 * (g0 + bi), A["ob"], A["rec"])
            if pending_fin is not None:
                pending_fin()
            pending_fin = None

    def phase_mla_sample(self, k, I, O):
        C, S, B = self.cfg, self.S, self.banks
        TP, CA = C.TP, C.CACHE
        ctiles = [(c0, min(128, CA - c0)) for c0 in range(0, CA, 128)]
        nct = len(ctiles)
        A = self.attn_bufs(k, CA + 64, 64, nct + 1)
        attend = self.make_attend(k, A)
        KTt, KRt, QTt, QRt, Vt = A["KT"], A["KR"], A["QT"], A["QR"], A["V"]
        WUK = self.load_resident(k, "aWUK", I["wuk"], [128, 2, 512])
        WUV = self.load_resident(k, "aWUV", I["wuv"], [128, 2, 512])
        cin = [k.sbuf("a_cin%d" % i, [128, 256], F32, dma=True) for i in range(2)]
        kin = [k.sbuf("a_kin%d" % i, [128, 64], F32, dma=True) for i in range(2)]
        cb = k.sbuf("a_cb", [128, 256], BF16)
        cTc = k.sbuf("a_cTc", [128, 2, CA], BF16)
        Vall = k.sbuf("a_Vall", [128, nct + 1, 4, 129], BF16, dma=True)
        k.op(k.pool, lambda e: e.memset(Vall[:, :, :, 128:129], 1.0), writes=[Vall])
        ksq = k.sbuf("a_ksq", [128, 512], F32, dma=True)
        krsq = k.sbuf("a_krsq", [64, CA], F32)
        K2c = k.dram("K2c", [1, CA + 64], F32)
        for b in range(C.NSB):
            rq = TP + 64 * b
            for ti, (c0, nk) in enumerate(ctiles):
                ci_, ki_ = cin[ti % 2], kin[ti % 2]
                k.dma(k.sp, [(ci_[0:nk, :], I["cache_ckv"][b, c0:c0 + nk, :])], ci_, writes=[ci_])
                k.dma(k.sp, [(ki_[0:nk, :], I["cache_kr"][b, c0:c0 + nk, :])], ki_, writes=[ki_])
                k.op(k.act, lambda e: e.copy(out=cb[0:nk, :], in_=ci_[0:nk, :]), reads=[ci_], writes=[cb])
                tb = B[6].t[:].bitcast(BF16)
                for kc in range(2):
                    k.op(k.pe, lambda e, kc=kc: e.transpose(out=tb[:, kc * 128:kc * 128 + nk], in_=cb[0:nk, kc * 128:(kc + 1) * 128],
                                                            identity=self.identb[0:nk, 0:nk]), reads=[cb, self.identb], writes=[B[6]], inc=(kc == 1))
                k.op(k.dve, lambda e: e.tensor_copy(out=cTc[:, :, c0:c0 + nk], in_=tb[:, 0:256].rearrange("p (k t) -> p k t", t=128)[:, :, 0:nk]),
                     reads=[B[6]], writes=[cTc])
                k.op(k.pe, lambda e: e.transpose(out=B[7][0:64, 0:nk], in_=ki_[0:nk, :], identity=self.identf[0:nk, 0:nk]),
                     reads=[ki_, self.identf], writes=[B[7]])
                k.op(k.act, lambda e: e.copy(out=KRt[0:64, c0:c0 + nk], in_=B[7][0:64, 0:nk]), reads=[B[7]], writes=[KRt])
                k.op(k.act, lambda e: e.activation(out=krsq[0:64, c0:c0 + nk], in_=B[7][0:64, 0:nk], func=AF.Square), reads=[B[7]], writes=[krsq])
                for kc in range(2):
                    k.op(k.pe, lambda e, kc=kc: e.matmul(B[5][0:nk, :], lhsT=cTc[:, kc, c0:c0 + nk], rhs=WUV[:, kc, :], start=(kc == 0), stop=(kc == 1)),
                         reads=[cTc, WUV], writes=[B[5]], inc=(kc == 1))
                k.op(k.dve, lambda e: e.tensor_copy(out=Vall[0:nk, ti, :, 0:128], in_=B[5][0:nk, :].rearrange("p (h d) -> p h d", d=128)),
                     reads=[B[5]], writes=[Vall])
            k.dma(k.sp, [(KRt[0:64, CA:CA + 64], S["KRT"].t[:, rq:rq + 64])], KRt, writes=[KRt])
            k.dma(k.sp, [(Vall[0:64, nct, :, :], S["V1"].t[rq:rq + 64, :, :])], Vall, writes=[Vall])
            for h in range(4):
                for c0 in range(0, CA, 512):
                    w = min(512, CA - c0)
                    for kc in range(2):
                        k.op(k.pe, lambda e, kc=kc: e.matmul(B[5][:, 0:w], lhsT=WUK[:, kc, h * 128:(h + 1) * 128], rhs=cTc[:, kc, c0:c0 + w],
                                                             start=(kc == 0), stop=(kc == 1)), reads=[WUK, cTc], writes=[B[5]], inc=(kc == 1))
                    k.op(k.act, lambda e: e.copy(out=KTt[:, c0:c0 + w], in_=B[5][:, 0:w]), reads=[B[5]], writes=[KTt])
                    k.op(k.act, lambda e: e.activation(out=ksq[:, 0:w], in_=B[5][:, 0:w], func=AF.Square), reads=[B[5]], writes=[ksq])
                    k.op(k.pe, lambda e: e.matmul(B[7][0:1, 0:w], lhsT=self.ones[:, 0:1], rhs=ksq[:, 0:w], start=True, stop=False),
                         reads=[self.ones, ksq], writes=[B[7]], inc=False)
                    k.op(k.pe, lambda e: e.matmul(B[7][0:1, 0:w], lhsT=self.ones[0:64, 0:1], rhs=krsq[0:64, c0:c0 + w], start=False, stop=True),
                         reads=[self.ones, krsq], writes=[B[7]])
                    k.op(k.dve, lambda e: e.tensor_copy(out=ksq[0:1, 0:w], in_=B[7][0:1, 0:w]), reads=[B[7]], writes=[ksq])
                    k.dma(k.sp, [(K2c[:, c0:c0 + w], ksq[0:1, 0:w])], ksq, reads=[ksq], writes=[K2c])
                k.dma(k.sp, [(K2c[:, CA:CA + 64], S["K2"].t[h:h + 1, rq:rq + 64])], K2c, writes=[K2c])
                k.dma(k.sp, [(KTt[:, CA:CA + 64], S["KT"].t[h, :, rq:rq + 64])], KTt, writes=[KTt])
                self.row_max_buf(k, K2c, CA + 64, A["k2m"], A["t65"])
                k.dma(k.sp, [(QTt[:, 0:64], S["QT"].t[h, :, rq:rq + 64])], QTt, writes=[QTt])
                k.dma(k.sp, [(QRt[0:64, 0:64], S["QRT"].t[h, :, rq:rq + 64])], QRt, writes=[QRt])
                self.stab_row(k, QRt, 64, S["QN2"].t[h:h + 1, rq:rq + 64], A["k2m"], A["t65"])
                k.op(k.pool, lambda e: e.tensor_copy(out=Vt[:, 0:nct + 1, :], in_=Vall[:, :, h, :]), reads=[Vall], writes=[Vt])
                tiles = [(c0, nk, ti) for ti, (c0, nk) in enumerate(ctiles)] + [(CA, 64, nct)]
                attend(tiles, 0, 64, [(B[2], 0, 64)], [len(tiles) - 1])
                self.attn_finish(k, B[2], 64, h, rq, A["ob"], A["rec"])

    def row_max_buf(self, k, buf, ncol, k2m, tmp65):
        self.row_max(k, buf.t[:, 0:ncol], ncol, k2m, tmp65, src_buf=buf)

    def phase_out(self, k, I, O):
        C, B = self.cfg, self.banks
        WO = self.load_resident(k, "WO", I["wout"], [128, KC, D])
        mixT2 = [k.sbuf("o_mixT%d" % i, [128, 8, 512], BF16, dma=True) for i in range(2)]

        def load_group(gj):
            blocks_, _ = group_layout(C.groups[gj])
            xt_, mx_ = self.xt[gj % 2], mixT2[gj % 2]
            runs = []
            for (r0, n, off) in blocks_:
                if runs and runs[-1][2] + runs[-1][1] == r0:
                    runs[-1] = (runs[-1][0], runs[-1][1] + n, runs[-1][2])
                else:
                    runs.append((off, n, r0))
            k.dma(k.sp, [(mx_[:, :, o:o + n], self.MIXT.t[:, :, r:r + n].rearrange("c p t -> p c t")) for (o, n, r) in runs],
                  mx_, reads=[self.MIXT], writes=[mx_])
            for bi, (r0, n, off) in enumerate(blocks_):
                k.dma(k.sp, [(xt_[bi][0:n, :], self.X1[r0:r0 + n, :])], xt_[bi], reads=[self.X1], writes=[xt_[bi]])
        load_group(0)
        for gi, grp in enumerate(C.groups):
            blocks, NT = group_layout(grp)
            xt, mixT = self.xt[gi % 2], mixT2[gi % 2]
            if gi + 1 < len(C.groups):
                load_group(gi + 1)
            for bi, (r0, n, off) in enumerate(blocks):
                for half in range(2):
                    cs = slice(half * 512, (half + 1) * 512)
                    bank = B[4 + (2 * bi + half) % 4]
                    for kc in range(8):
                        k.op(k.pe, lambda e, kc=kc: e.matmul(bank[0:n, :], lhsT=mixT[:, kc, off:off + n], rhs=WO[:, kc, cs],
                                                             start=(kc == 0), stop=(kc == 7)), reads=[mixT, WO], writes=[bank], inc=(kc == 7))
                    k.op(k.dve, lambda e: e.tensor_tensor(out=xt[bi][0:n, cs], in0=bank[0:n, :], in1=xt[bi][0:n, cs], op=ALU.add),
                         reads=[bank, xt[bi]], writes=[xt[bi]])
            self.norm_group(k, blocks, xt, self.gT[:, 2, :], B[4:8])

            def epi(half, blocks=blocks, xt=xt):
                for bi, (r0, n, off) in enumerate(blocks):
                    cs = slice(half * 512, (half + 1) * 512)
                    xo = self.xo[bi]
                    k.op(k.dve, lambda e, bi=bi, n=n, cs=cs, xo=xo: e.scalar_tensor_tensor(
                        out=xo[0:n, cs], in0=B[4 + bi][0:n, 0:512], scalar=0.5, in1=xt[bi][0:n, cs], op0=ALU.mult, op1=ALU.add),
                         reads=[B[4 + bi], xt[bi]], writes=[xo])
                    if half == 1:
                        stt_, rstd = self.rstd_of(k, xo[0:n, :], n, D, [xo])
                        k.op(k.dve, lambda e, n=n, xo=xo, rstd=rstd: e.scalar_tensor_tensor(
                            out=xo[0:n, :], in0=xo[0:n, :], scalar=rstd, in1=self.gfin[0:n, :], op0=ALU.mult, op1=ALU.mult),
                             reads=[xo, stt_, self.gfin], writes=[xo])
                        k.dma(k.sp, [(O["y"][r0:r0 + n, :], xo[0:n, :])], xo, reads=[xo])
            self.ffn_core(k, blocks, NT, self.wgu2_s, self.wd2_s, epi)

def lay_wgu(wg, wu):
    a = np.stack([wg, wu], 0).reshape(2, KC, 128, NFC, 128)
    a = a.transpose(3, 2, 0, 1, 4)
    return np.ascontiguousarray(a).reshape(NFC * 128, 2 * KC * 128)


def lay_wd(wd):
    a = wd.reshape(NFC, 128, D).transpose(1, 0, 2)
    return np.ascontiguousarray(a).reshape(128 * NFC, D)


def _kcp(w):
    K_, Cc = w.shape
    return np.ascontiguousarray(w.reshape(K_ // 128, 128, Cc).transpose(1, 0, 2))


def _swap(w):
    return np.concatenate([w[:, 32:64], w[:, 0:32]], axis=1)


def lay_win(w_in):
    kr = w_in[:, 2696:2760]
    wf = np.concatenate([w_in[:, 0:1536], kr, _swap(kr)], axis=1)
    wt = np.concatenate([w_in[:, 1536:2048], w_in[:, 2056:2440], w_in[:, 2440:2696], w_in[:, 2048:2056]], axis=1)
    return _kcp(wf), _kcp(wt)


def lay_wuq(w_uq):
    cols = []
    for h in range(4):
        qn = w_uq[:, h * 192:h * 192 + 128]
        qr = w_uq[:, h * 192 + 128:h * 192 + 192]
        cols += [qn, qr, _swap(qr)]
    return _kcp(np.concatenate(cols, axis=1))


def lay_wukv(w_ukv):
    kn = np.concatenate([w_ukv[:, h * 256:h * 256 + 128] for h in range(4)], axis=1)
    v = np.concatenate([w_ukv[:, h * 256 + 128:h * 256 + 256] for h in range(4)], axis=1)
    return _kcp(kn), _kcp(v)


_PROG = {}


def _program():
    if "nc" not in _PROG:
        cfg = Cfg(SEQ=8192, NSB=4, PAST=2048)
        p = Prog(cfg)
        _PROG["nc"] = p.build(upto=5)
        _PROG["cfg"] = cfg
    return _PROG["nc"], _PROG["cfg"]


def kernel(x_prompt, x_sample, cache_mla_ckv, cache_mla_krope, state_gdn, state_conv, meta,
           ffn1_norm, ffn1_wg, ffn1_wu, ffn1_wd, mix_norm, w_in, conv_w, a_log, dt_bias, gdn_norm,
           q_norm, kv_norm, w_uq, w_ukv, w_out, ffn2_norm, ffn2_wg, ffn2_wu, ffn2_wd, final_norm):
    f = lambda a: np.ascontiguousarray(np.asarray(a, dtype=np.float32))
    nc, cfg = _program()
    NB, NSB, TP = 4, cfg.NSB, cfg.TP
    wf, wt = lay_win(f(w_in)[0])
    wuk, wuv = lay_wukv(f(w_ukv)[0])
    shared = dict(
        norms=np.stack([f(ffn1_norm)[0], f(mix_norm)[0], f(ffn2_norm)[0], f(final_norm)]),
        q_norm=f(q_norm)[0], kv_norm=f(kv_norm)[0], gdn_norm=f(gdn_norm)[0], a_log=f(a_log)[0], dt_bias=f(dt_bias)[0],
        conv_w=f(conv_w)[0],
        wgu1=lay_wgu(f(ffn1_wg)[0], f(ffn1_wu)[0]), wd1=lay_wd(f(ffn1_wd)[0]),
        wgu2=lay_wgu(f(ffn2_wg)[0], f(ffn2_wu)[0]), wd2=lay_wd(f(ffn2_wd)[0]),
        wf=wf, wt=wt, wuq=lay_wuq(f(w_uq)[0]), wuk=wuk, wuv=wuv, wout=_kcp(f(w_out)[0]))
    xp, xs, mt = f(x_prompt), f(x_sample), f(meta)
    in_maps = []
    for c in range(8):
        sb = slice(NSB * c, NSB * (c + 1))
        m = dict(shared)
        m["xin"] = np.concatenate([mt, xp[c % NB], xs[sb].reshape(-1, D)], axis=0)
        m["state_conv"] = f(state_conv)[0, sb]
        m["state_gdn"] = f(state_gdn)[0, sb]
        m["cache_ckv"] = f(cache_mla_ckv)[0, sb]
        m["cache_kr"] = f(cache_mla_krope)[0, sb]
        in_maps.append(m)
    res = run_bass_kernel_spmd(nc, in_maps, core_ids=list(range(8))).results
    y_p = np.stack([res[b]["y"][16:TP] for b in range(NB)])
    y_s = np.concatenate([res[c]["y"][TP:].reshape(NSB, 64, D) for c in range(8)])
    p_ckv = np.stack([res[b]["ckv"][0:TP] for b in range(NB)])[None]
    p_kr = np.stack([res[b]["kr"][0:TP] for b in range(NB)])[None]
    p_gdn = np.stack([res[b]["pgdn"] for b in range(NB)])[None]
    p_conv = np.stack([res[b]["pconv"] for b in range(NB)])[None]
    s_ckv = np.concatenate([res[c]["ckv"][TP:].reshape(NSB, 64, 256) for c in range(8)])[None]
    s_kr = np.concatenate([res[c]["kr"][TP:].reshape(NSB, 64, 64) for c in range(8)])[None]
    s_gdn = np.concatenate([res[c]["sgdn"] for c in range(8)])[None]
    s_conv = np.concatenate([res[c]["sconv"] for c in range(8)])[None]
    return tuple(np.ascontiguousarray(a, dtype=np.float32) for a in
                 (y_p, y_s, p_ckv, p_kr, p_gdn, p_conv, s_ckv, s_kr, s_gdn, s_conv))
```

```python
import contextlib
import numpy as np
import concourse.bass as bass
import concourse.mybir as mybir
from concourse.bass_utils import run_bass_kernel_spmd

F32 = mybir.dt.float32
BF16 = mybir.dt.bfloat16
F32R = mybir.dt.float32r
AF = mybir.ActivationFunctionType
ALU = mybir.AluOpType

D = 1024
DFF = 2816
NFC = DFF // 128
KC = D // 128
EPS = 1e-6


class Eng:
    def __init__(self, name, e, sem):
        self.name, self.e, self.sem = name, e, sem
        self.tick = 0
        self.seen = {}
        self.nwait = 0
        self.nins = 0


class DSem:
    def __init__(self, h):
        self.h = h
        self.count = 0


class Buf:
    def __init__(self, t, name, dsem=None):
        self.t = t
        self.name = name
        self.last_w = None
        self.extra_w = []
        self.readers = {}
        self.ds = dsem
        self.native_r = False
        self.rmode = False

    def __getitem__(self, k):
        ap = self.t[k]
        if self.native_r and not self.rmode:
            return ap.bitcast(F32)
        return ap

    def f32(self, k):
        ap = self.t[k]
        return ap.bitcast(F32) if self.native_r else ap


class K:
    def __init__(self, nc, stack):
        self.nc, self.stack = nc, stack
        self.engs = {}
        for name, e in (("pe", nc.tensor), ("act", nc.scalar), ("dve", nc.vector),
                        ("pool", nc.gpsimd), ("sp", nc.sync)):
            sem = stack.enter_context(nc.semaphore("sem_" + name))
            self.engs[name] = Eng(name, e, sem)
        self.pe, self.act, self.dve, self.pool, self.sp = (
            self.engs[n] for n in ("pe", "act", "dve", "pool", "sp"))
        self.nsem = 5
        self.all_ds = []
        self.free_ds = []
        self.scopes = [(stack, [])]
        self.names = {}

    def new_sem(self, name, sw=False):
        if not sw and self.free_ds:
            return self.free_ds.pop()
        self.nsem += 1
        assert self.nsem <= 100, "out of semaphores"
        d = DSem(self.stack.enter_context(self.nc.semaphore("ds%d" % self.nsem)))
        d.sw = sw
        self.all_ds.append(d)
        return d

    def _reg(self, b):
        if b.ds is not None and not getattr(b.ds, "sw", False):
            self.scopes[-1][1].append(b.ds)
        return b

    @contextlib.contextmanager
    def scope(self):
        with contextlib.ExitStack() as st:
            self.scopes.append((st, []))
            try:
                yield
            finally:
                self.barrier()
                _, dss = self.scopes.pop()
                self.free_ds.extend(dss)

    def _uniq(self, name):
        n = self.names.get(name, 0)
        self.names[name] = n + 1
        return name if n == 0 else "%s__%d" % (name, n)

    def sbuf(self, name, shape, dtype, dma=False):
        name = self._uniq(name)
        t = self.scopes[-1][0].enter_context(self.nc.sbuf_tensor(name, list(shape), dtype))
        return self._reg(Buf(t, name, self.new_sem(name, sw=(dma == "sw")) if dma else None))

    def psum(self, name, shape, dtype):
        name = self._uniq(name)
        t = self.scopes[-1][0].enter_context(self.nc.psum_tensor(name, list(shape), dtype))
        return Buf(t, name)

    def dram(self, name, shape, dtype, dma=True):
        name = self._uniq(name)
        t = self.nc.dram_tensor(name, list(shape), dtype, kind="Internal").ap()
        return self._reg(Buf(t, name, self.new_sem(name, sw=(dma == "sw")) if dma else None))

    def wrap(self, t, name, dma=False):
        return self._reg(Buf(t, name, self.new_sem(name, sw=(dma == "sw")) if dma else None))

    @staticmethod
    def _max_per_sem(toks):
        best = {}
        for tok in toks:
            if tok is None:
                continue
            old = best.get(tok[0])
            if old is None or old[2] < tok[2]:
                best[tok[0]] = tok
        return best

    def split(self, whole, parts):
        for p in parts:
            p.last_w = whole.last_w
            p.extra_w = list(whole.extra_w)
            p.readers = dict(whole.readers)

    def merge(self, whole, parts):
        w = self._max_per_sem([whole.last_w] + list(whole.extra_w) +
                              [t for p in parts for t in ([p.last_w] + list(p.extra_w))])
        toks = list(w.values())
        whole.last_w = toks[0] if toks else None
        whole.extra_w = toks[1:]
        whole.readers = self._max_per_sem(list(whole.readers.values()) + [t for p in parts for t in p.readers.values()])

    def _wait(self, eng, tok):
        if tok is None:
            return
        key, sem, val = tok
        if eng.seen.get(key, 0) >= val:
            return
        if key == id(eng.sem) and eng.name == "pe":
            return
        eng.e.wait_ge(sem, val)
        eng.seen[key] = val
        eng.nwait += 1

    def _deps(self, eng, reads, writes):
        for b in reads:
            self._wait(eng, b.last_w)
            for tok in b.extra_w:
                self._wait(eng, tok)
        for b in writes:
            self._wait(eng, b.last_w)
            for tok in b.extra_w:
                self._wait(eng, tok)
            for tok in b.readers.values():
                self._wait(eng, tok)

    def op(self, eng, fn, reads=(), writes=(), inc=True):
        self._deps(eng, reads, writes)
        ins = fn(eng.e)
        eng.nins += 1
        if inc:
            ins.then_inc(eng.sem, 1)
            eng.tick += 1
            t = eng.tick
        else:
            t = eng.tick + 1
        tok = (id(eng.sem), eng.sem, t)
        for b in reads:
            old = b.readers.get(tok[0])
            if old is None or old[2] < t:
                b.readers[tok[0]] = tok
        for b in writes:
            b.last_w = tok
            b.extra_w = []
            b.readers = {}
        return ins

    def dma(self, q, pairs, slot, reads=(), writes=(), **kw):
        self._deps(q, reads, writes)
        ds = slot.ds
        for (o, i) in pairs:
            q.e.dma_start(out=o, in_=i, **kw).then_inc(ds.h, 16)
            ds.count += 1
            q.nins += 1
        tok = (id(ds.h), ds.h, 16 * ds.count)
        for b in reads:
            b.readers[tok[0]] = tok
        for b in writes:
            b.last_w = tok
            b.extra_w = []
            b.readers = {}
        return tok

    def barrier(self, engines=None):
        toks = []
        for e in self.engs.values():
            if e.tick > 0:
                toks.append((id(e.sem), e.sem, e.tick))
        for d in self.all_ds:
            if d.count > 0:
                toks.append((id(d.h), d.h, 16 * d.count))
        for e in (engines or self.engs.values()):
            for tok in toks:
                if tok[0] == id(e.sem):
                    continue
                self._wait(e, tok)

    def finish(self):
        self.barrier(engines=[self.sp])


class Cfg:
    def __init__(self, SEQ=8192, NSB=4, PAST=2048):
        self.SEQ, self.NSB, self.PAST = SEQ, NSB, PAST
        self.NMETA = 16
        self.TP = 16 + SEQ
        self.NS = NSB * 64
        self.NTOK = self.TP + self.NS
        self.CACHE = 16 + PAST
        self.blocks = [(0, 16)]
        self.blocks += [(16 + 128 * i, 128) for i in range(SEQ // 128)]
        sb = []
        r = self.TP
        while r < self.NTOK:
            n = min(128, self.NTOK - r)
            sb.append((r, n))
            r += n
        self.samp_blocks = sb
        g0 = [self.blocks[0]] + sb
        self.groups = []
        cur, tot = [], 0
        for b in g0:
            if tot + b[1] > 512:
                self.groups.append(cur)
                cur, tot = [], 0
            cur.append(b)
            tot += b[1]
        if cur:
            self.groups.append(cur)
        fb = self.blocks[1:]
        for i in range(0, len(fb), 4):
            self.groups.append(fb[i:i + 4])


def run_streams(makers, W):
    pending = list(makers)
    active, free = {}, list(range(W))
    while pending or active:
        while pending and free:
            sl = free.pop(0)
            active[sl] = pending.pop(0)(sl)
        for sl in sorted(active):
            try:
                next(active[sl])
            except StopIteration:
                del active[sl]
                free.append(sl)


def group_layout(grp):
    out, off = [], 0
    for (r, n) in grp:
        out.append((r, n, off))
        off += n
    return out, off


class Prog:
    def __init__(self, cfg, debug=()):
        self.cfg = cfg
        self.debug = set(debug)
        self.nc = bass.Bass("TRN2", target_bir_lowering=False)
        self.ins = {}
        self.outs = {}

    def inp(self, name, shape, dtype=F32):
        self.ins[name] = self.nc.dram_tensor(name, list(shape), dtype, kind="ExternalInput").ap()
        return self.ins[name]

    def out(self, name, shape, dtype=F32):
        self.outs[name] = self.nc.dram_tensor(name, list(shape), dtype, kind="ExternalOutput").ap()
        return self.outs[name]

    def build(self, upto=5, dump=()):
        cfg, nc = self.cfg, self.nc
        C = cfg
        I = {}
        for nm, shp in (("xin", [C.NTOK, D]), ("norms", [4, D]), ("q_norm", [384]), ("kv_norm", [256]),
                        ("gdn_norm", [128]), ("a_log", [4]), ("dt_bias", [4]), ("conv_w", [4, 1536]),
                        ("wgu1", [NFC * 128, 2 * KC * 128]), ("wd1", [128 * NFC, D]),
                        ("wgu2", [NFC * 128, 2 * KC * 128]), ("wd2", [128 * NFC, D]),
                        ("wf", [128, KC, 13 * 128]), ("wt", [128, KC, 1160]), ("wuq", [128, 3, 1024]),
                        ("wuk", [128, 2, 512]), ("wuv", [128, 2, 512]), ("wout", [128, KC, D]),
                        ("state_conv", [C.NSB, 3, 1536]), ("state_gdn", [C.NSB, 4, 128, 128]),
                        ("cache_ckv", [C.NSB, C.CACHE, 256]), ("cache_kr", [C.NSB, C.CACHE, 64])):
            I[nm] = self.inp(nm, shp)
        O = {}
        for nm, shp in (("y", [C.NTOK, D]), ("ckv", [C.NTOK, 256]), ("kr", [C.NTOK, 64]), ("pgdn", [4, 128, 128]),
                        ("pconv", [3, 1536]), ("sgdn", [C.NSB, 4, 128, 128]), ("sconv", [C.NSB, 3, 1536])):
            O[nm] = self.out(nm, shp)
        self.I, self.O = I, O
        with contextlib.ExitStack() as st:
            k = K(nc, st)
            self.k = k
            self.consts(k, I["norms"])
            self.proj_consts(k, I)
            self.wgu1_s = self.cast_weight(k, "wgu1_s", I["wgu1"], part_rows=[2 * 128, 6 * 128, 14 * 128])
            self.wd1_s = self.cast_weight(k, "wd1_s", I["wd1"])
            X1 = k.dram("X1", [C.NTOK, D], F32, dma=False)
            self.X1 = X1
            self.banks = [k.psum("bank%d" % i, [128, 512], F32) for i in range(8)]
            self.hT = k.sbuf("hT", [128, KC, 512], BF16)
            self.xnb = [k.sbuf("xnb%d" % i, [128, D], BF16) for i in range(4)]
            self.stat = [k.sbuf("stat%d" % i, [128, 4], F32) for i in range(4)]
            self.junk = k.sbuf("junk", [128, D], BF16)
            self.cnt = {"wgu": 0, "wd": 0, "xnb": 0, "stat": 0, "sg": 0}
            self.rope_tables(k)
            self.proj_scratch(k)
            with k.scope():
                self.ffn_bufs(k)
                self.phase_ffn1(k, I["xin"], X1)
            if upto >= 2:
                with k.scope():
                    self.phase_proj(k, I, O)
                    self.sbuf_left = nc.sbuf_bytes_remaining
            if upto >= 3:
                with k.scope():
                    self.phase_gdn(k, I, O)
                    self.sbuf_left = nc.sbuf_bytes_remaining
            if upto >= 4:
                with k.scope():
                    self.phase_mla_prompt(k, I, O)
                    self.sbuf_left4 = nc.sbuf_bytes_remaining
                with k.scope():
                    self.phase_mla_sample(k, I, O)
            if upto >= 5:
                with k.scope():
                    self.ffn_bufs(k)
                    self.phase_out(k, I, O)
            k.barrier()
            allscr = dict(self.S)
            allscr.update({"X1": X1, "COS2": self.COS2, "SIN2S": self.SIN2S})
            dsl = k.wrap(None, "dbgslot", dma=True)
            for nm in dump:
                src = allscr[nm]
                o = self.out("dbg_" + nm, list(src.t.shape), src.t.dtype)
                k.dma(k.sp, [(o, src.t)], dsl)
            k.finish()
            self.stats = {n: (e.nins, e.nwait) for n, e in k.engs.items()}
        return nc

    def consts(self, k, norms):
        nc = self.nc
        ones = k.sbuf("c_ones", [128, 128], F32)
        k.op(k.pool, lambda e: e.memset(ones[:], 1.0), writes=[ones])
        self.ones = ones
        identf = k.sbuf("c_identf", [128, 128], F32)
        k.op(k.pool, lambda e: e.affine_select(out=identf[:], in_=ones[:], pattern=[[1, 128]],
                                               compare_op=ALU.is_equal, fill=0.0, base=0,
                                               channel_multiplier=-1), reads=[ones], writes=[identf])
        identb = k.sbuf("c_identb", [128, 128], BF16)
        k.op(k.dve, lambda e: e.tensor_copy(out=identb[:], in_=identf[:]), reads=[identf], writes=[identb])
        self.identf, self.identb = identf, identb
        gT = k.sbuf("c_gT", [128, 4, KC], F32, dma=True)
        with nc.allow_non_contiguous_dma(reason="tiny one-time gain transpose load"):
            k.dma(k.sp, [(gT[:, r, :], norms[r, :].rearrange("(kc p) -> p kc", p=128)) for r in range(4)],
                  gT, writes=[gT])
        self.gT = gT
        gfin = k.sbuf("c_gfin", [128, D], F32, dma=True)
        k.dma(k.sp, [(gfin[:], norms[3:4, :].partition_broadcast(128))], gfin, writes=[gfin])
        self.gfin = gfin
        epsb = k.sbuf("c_eps", [128, 1], F32)
        k.op(k.pool, lambda e: e.memset(epsb[:], EPS), writes=[epsb])
        self.epsb = epsb
        l2b = k.sbuf("c_l2b", [128, 2], F32)
        k.op(k.pool, lambda e: e.memset(l2b[:, 0:1], 128.0 * 1e-6), writes=[l2b])
        k.op(k.pool, lambda e: e.memset(l2b[:, 1:2], 1e-6), writes=[l2b])
        self.l2b = l2b

    def cast_weight(self, k, name, src, part_rows=None):
        R, Cc = src.shape
        scr = k.dram(name, [R, Cc], BF16, dma="sw")
        bounds = [0] + list(part_rows or []) + [R]
        parts = []
        for a, b in zip(bounds[:-1], bounds[1:]):
            pb = scr if len(bounds) == 2 else k.wrap(scr.t, "%s_p%d" % (name, a), dma="sw")
            pairs = [(scr[r0:min(b, r0 + 256), :], src[r0:min(b, r0 + 256), :]) for r0 in range(a, b, 256)]
            k.dma(k.pool, pairs, pb, writes=[pb], max_dma_last_dim=2048 * 4)
            parts.append((a, b, pb))
        scr.parts = parts
        return scr

    def ffn_bufs(self, k):
        self.xt = [[k.sbuf("xt%d_%d" % (s, b), [128, D], F32, dma=True) for b in range(4)] for s in range(2)]
        self.xo = [k.sbuf("xo%d" % b, [128, D], F32, dma=True) for b in range(4)]
        self.actT = k.sbuf("actT", [128, NFC, 512], BF16)
        self.wgu_sl = [k.sbuf("wgu_sl%d" % i, [128, 2, KC, 128], BF16, dma=True) for i in range(3)]
        self.wd_sl = [k.sbuf("wd_sl%d" % i, [128, 11, 512], BF16, dma=True) for i in range(2)]
        self.sg = [k.sbuf("sg%d" % i, [128, 512], F32) for i in range(2)]

    def rstd_of(self, k, x_ap, n, Dn, reads):
        stt = self.stat[self.cnt["stat"] % 4]
        self.cnt["stat"] += 1
        junk = self.junk
        k.op(k.act, lambda e: e.activation(out=junk[0:n, 0:Dn], in_=x_ap, func=AF.Square,
                                           accum_out=stt[0:n, 0:1]), reads=reads, writes=[junk, stt])
        k.op(k.act, lambda e: e.activation(out=stt[0:n, 1:2], in_=stt[0:n, 0:1], func=AF.Sqrt,
                                           bias=self.epsb[0:n, :], scale=1.0 / Dn),
             reads=[stt, self.epsb], writes=[stt])
        k.op(k.dve, lambda e: e.reciprocal(out=stt[0:n, 2:3], in_=stt[0:n, 1:2]), reads=[stt], writes=[stt])
        return stt, stt[0:n, 2:3]

    def norm_A(self, k, xbuf, x_ap, n, nkc):
        Dn = nkc * 128
        stt, rstd = self.rstd_of(k, x_ap, n, Dn, [xbuf])
        xnb = self.xnb[self.cnt["xnb"] % len(self.xnb)]
        self.cnt["xnb"] += 1
        k.op(k.act, lambda e: e.activation(out=xnb[0:n, 0:Dn], in_=x_ap, func=AF.Copy, scale=rstd),
             reads=[xbuf, stt], writes=[xnb])
        return xnb

    def norm_B(self, k, xnb, n, nkc, g_ap, dstT, off, tbank, gbuf=None):
        tb = tbank.t[:].bitcast(BF16)
        for kc in range(nkc):
            k.op(k.pe, lambda e, kc=kc: e.transpose(out=tb[:, kc * 128:kc * 128 + n],
                                                    in_=xnb[0:n, kc * 128:(kc + 1) * 128],
                                                    identity=self.identb[0:n, 0:n]),
                 reads=[xnb, self.identb], writes=[tbank], inc=(kc == nkc - 1))
        src = tb[:, 0:nkc * 128].rearrange("p (k t) -> p k t", t=128)[:, :, 0:n]
        k.op(k.dve, lambda e: e.tensor_tensor(out=dstT[:, 0:nkc, off:off + n], in0=src,
                                              in1=g_ap.unsqueeze(2).to_broadcast([128, nkc, n]), op=ALU.mult),
             reads=[tbank, gbuf or self.gT], writes=[dstT])

    def norm_to_T(self, k, xbuf, x_ap, n, nkc, g_ap, dstT, off, tbank, gbuf=None):
        xnb = self.norm_A(k, xbuf, x_ap, n, nkc)
        self.norm_B(k, xnb, n, nkc, g_ap, dstT, off, tbank, gbuf)

    def norm_group(self, k, blocks, xt, g_ap, tbanks):
        xn = [self.norm_A(k, xt[bi], xt[bi][0:n, :], n, KC) for bi, (r0, n, off) in enumerate(blocks)]
        for bi, (r0, n, off) in enumerate(blocks):
            self.norm_B(k, xn[bi], n, KC, g_ap, self.hT, off, tbanks[bi])

    @staticmethod
    def part_of(scr, row):
        for (a, b, pb) in scr.parts:
            if a <= row < b:
                return pb
        raise AssertionError(row)

    def ffn_core(self, k, blocks, NT, wgu_s, wd_s, epilogue):
        hT, actT = self.hT, self.actT
        wg_v = wgu_s.t.rearrange("(f p) (g k j) -> f p g k j", p=128, g=2, k=KC)
        wd_v = wd_s.t.rearrange("(p f) c -> p f c", f=NFC)
        for fc in range(NFC):
            sl = self.wgu_sl[self.cnt["wgu"] % 3]
            self.cnt["wgu"] += 1
            k.dma(k.sp, [(sl[:], wg_v[fc])], sl, reads=[self.part_of(wgu_s, fc * 128)], writes=[sl])
            pg = self.banks[(2 * fc) % 4]
            pu = self.banks[(2 * fc + 1) % 4]
            for gu, pb in ((0, pg), (1, pu)):
                for kc in range(KC):
                    k.op(k.pe, lambda e, gu=gu, kc=kc, pb=pb: e.matmul(
                        pb[:, 0:NT], lhsT=sl[:, gu, kc, :], rhs=hT[:, kc, 0:NT],
                        start=(kc == 0), stop=(kc == KC - 1)),
                         reads=[sl, hT], writes=[pb], inc=(kc == KC - 1))
            sg = self.sg[self.cnt["sg"] % 2]
            self.cnt["sg"] += 1
            k.op(k.act, lambda e: e.activation(out=sg[:, 0:NT], in_=pg[:, 0:NT], func=AF.Silu),
                 reads=[pg], writes=[sg])
            k.op(k.dve, lambda e: e.tensor_tensor(out=actT[:, fc, 0:NT], in0=sg[:, 0:NT], in1=pu[:, 0:NT],
                                                  op=ALU.mult), reads=[sg, pu], writes=[actT])
        for half in range(2):
            for q in range(2):
                ws = self.wd_sl[self.cnt["wd"] % 2]
                self.cnt["wd"] += 1
                k.dma(k.sp, [(ws[:], wd_v[:, q * 11:(q + 1) * 11, half * 512:(half + 1) * 512])], ws,
                      reads=[wd_s], writes=[ws])
                for fi in range(11):
                    fc = q * 11 + fi
                    for bi, (r0, n, off) in enumerate(blocks):
                        last = (fi == 10 and bi == len(blocks) - 1) or fc == NFC - 1
                        k.op(k.pe, lambda e, fc=fc, fi=fi, bi=bi, n=n, off=off: e.matmul(
                            self.banks[4 + bi][0:n, 0:512], lhsT=actT[:, fc, off:off + n], rhs=ws[:, fi, :],
                            start=(fc == 0), stop=(fc == NFC - 1)),
                             reads=[actT, ws], writes=[self.banks[4 + bi]], inc=last)
            epilogue(half)

    def phase_ffn1(self, k, xin, X1):
        C = self.cfg
        def load_group(gj):
            blocks_, _ = group_layout(C.groups[gj])
            xt_ = self.xt[gj % 2]
            for bi, (r0, n, off) in enumerate(blocks_):
                k.dma(k.sp, [(xt_[bi][0:n, :], xin[r0:r0 + n, :])], xt_[bi], writes=[xt_[bi]])
        load_group(0)
        for gi, grp in enumerate(C.groups):
            blocks, NT = group_layout(grp)
            xt = self.xt[gi % 2]
            if gi + 1 < len(C.groups):
                load_group(gi + 1)
            self.norm_group(k, blocks, xt, self.gT[:, 0, :], self.banks[4:8])

            def epi(half, blocks=blocks, xt=xt):
                for bi, (r0, n, off) in enumerate(blocks):
                    cs = slice(half * 512, (half + 1) * 512)
                    k.op(k.dve, lambda e, bi=bi, n=n, cs=cs: e.scalar_tensor_tensor(
                        out=self.xo[bi][0:n, cs], in0=self.banks[4 + bi][0:n, 0:512], scalar=0.5,
                        in1=xt[bi][0:n, cs], op0=ALU.mult, op1=ALU.add),
                         reads=[self.banks[4 + bi], xt[bi]], writes=[self.xo[bi]])
                    if half == 1:
                        k.dma(k.sp, [(X1[r0:r0 + n, :], self.xo[bi][0:n, :])], self.xo[bi],
                              reads=[self.xo[bi]], writes=[X1])
            self.ffn_core(k, blocks, NT, self.wgu1_s, self.wd1_s, epi)
            if gi == 0:
                self.wgu2_s = self.cast_weight(k, "wgu2_s", self.I["wgu2"])
                self.wd2_s = self.cast_weight(k, "wd2_s", self.I["wd2"])


    def proj_consts(self, k, I):
        nc, C = self.nc, self.cfg
        gq = k.sbuf("c_gqT", [128, 3], F32, dma=True)
        cw = k.sbuf("c_cwT", [128, 12, 4], F32, dma=True)
        with nc.allow_non_contiguous_dma(reason="tiny one-time constant transposes"):
            k.dma(k.sp, [(gq[:], I["q_norm"].rearrange("(kc p) -> p kc", p=128))], gq, writes=[gq])
            k.dma(k.sp, [(cw[:, :, j], I["conv_w"][j, :].rearrange("(cc p) -> p cc", p=128)) for j in range(4)],
                  cw, writes=[cw])
        self.gqT, self.cwT = gq, cw
        gkv = k.sbuf("c_gkv", [128, 256], F32, dma=True)
        k.dma(k.sp, [(gkv[:], I["kv_norm"].partition_broadcast(128))], gkv, writes=[gkv])
        self.gkv = gkv
        ab = k.sbuf("c_ab", [128, 16], F32, dma=True)
        k.dma(k.sp, [(ab[:, 0:4], I["a_log"].partition_broadcast(128)),
                     (ab[:, 4:8], I["dt_bias"].partition_broadcast(128))], ab, writes=[ab])
        k.op(k.act, lambda e: e.activation(out=ab[:, 8:12], in_=ab[:, 0:4], func=AF.Exp), reads=[ab], writes=[ab])
        k.op(k.dve, lambda e: e.tensor_scalar(out=ab[:, 8:12], in0=ab[:, 8:12], scalar1=-1.0, scalar2=None,
                                              op0=ALU.mult), reads=[ab], writes=[ab])
        k.op(k.pool, lambda e: e.memset(ab[:, 12:16], 0.0), writes=[ab])
        self.abc = ab

    def load_resident(self, k, name, src, shape):
        t = k.sbuf(name, shape, BF16, dma="sw")
        k.dma(k.pool, [(t[:, a, :], src[:, a, :]) for a in range(shape[1])], t, writes=[t],
              max_dma_last_dim=8192)
        return t

    def rope_tables(self, k):
        C = self.cfg
        NPOS = C.TP + 64
        self.COS2 = k.dram("COS2", [64, NPOS], F32, dma=False)
        self.SIN2S = k.dram("SIN2S", [64, NPOS], F32, dma=False)
        I32 = mybir.dt.int32
        W = 2048
        with k.scope():
            pidx = k.sbuf("r_pidx", [64, 1], F32)
            for h in range(2):
                k.op(k.pool, lambda e, h=h: e.iota(pidx[32 * h:32 * h + 32, :], pattern=[[0, 1]], base=0,
                                                   channel_multiplier=1, allow_small_or_imprecise_dtypes=True),
                     writes=[pidx])
            inv = k.sbuf("r_inv", [64, 1], F32)
            k.op(k.act, lambda e: e.activation(out=inv[:], in_=pidx[:], func=AF.Exp,
                                               scale=-float(np.log(10000.0) / 32)), reads=[pidx], writes=[inv])
            sgn = k.sbuf("r_sgn", [64, 1], F32)
            k.op(k.pool, lambda e: e.memset(sgn[0:32, :], -1.0), writes=[sgn])
            k.op(k.pool, lambda e: e.memset(sgn[32:64, :], 1.0), writes=[sgn])
            T = {n: k.sbuf("r_" + n, [64, W], F32, dma=(n in ("co", "si"))) for n in
                 ("pos", "th", "t", "nf", "u", "w", "p", "su", "cu", "co", "si")}
            ni = k.sbuf("r_ni", [64, W], I32)
            chunks = [(c0, min(W, C.TP - c0), c0) for c0 in range(0, C.TP, W)] + [(C.TP, 64, C.CACHE)]
            HI = 6.28125
            LO = float(2 * np.pi - HI)
            sc = [1.0 / 362880, -1.0 / 5040, 1.0 / 120, -1.0 / 6]
            cc_ = [-1.0 / 3628800, 1.0 / 40320, -1.0 / 720, 1.0 / 24, -0.5]
            for (c0, w, p0) in chunks:
                def ts(out, in0, s1, s2=None, o0=ALU.mult, o1=None, rd=()):
                    kw = dict(out=out.t[:, 0:w], in0=in0.t[:, 0:w], scalar1=s1, scalar2=s2, op0=o0)
                    if o1 is not None:
                        kw["op1"] = o1
                    k.op(k.dve, lambda e: e.tensor_scalar(**kw), reads=[in0] + list(rd), writes=[out])

                def stt(out, in0, sca, in1, o0, o1):
                    k.op(k.dve, lambda e: e.scalar_tensor_tensor(out=out.t[:, 0:w], in0=in0.t[:, 0:w], scalar=sca,
                                                                 in1=in1.t[:, 0:w], op0=o0, op1=o1),
                         reads=[in0, in1], writes=[out])
                k.op(k.pool, lambda e: e.iota(T["pos"].t[:, 0:w], pattern=[[1, w]], base=p0, channel_multiplier=0,
                                              allow_small_or_imprecise_dtypes=True), writes=[T["pos"]])
                ts(T["th"], T["pos"], inv[:], rd=[inv])
                ts(T["t"], T["th"], float(1.0 / (2 * np.pi)))
                k.op(k.dve, lambda e: e.tensor_copy(out=ni[:, 0:w], in_=T["t"].t[:, 0:w]), reads=[T["t"]], writes=[ni])
                k.op(k.dve, lambda e: e.tensor_copy(out=T["nf"].t[:, 0:w], in_=ni[:, 0:w]), reads=[ni], writes=[T["nf"]])
                stt(T["u"], T["nf"], -HI, T["th"], ALU.mult, ALU.add)
                stt(T["u"], T["nf"], -LO, T["u"], ALU.mult, ALU.add)
                ts(T["u"], T["u"], 0.5)
                k.op(k.dve, lambda e: e.tensor_tensor(out=T["w"].t[:, 0:w], in0=T["u"].t[:, 0:w], in1=T["u"].t[:, 0:w],
                                                      op=ALU.mult), reads=[T["u"]], writes=[T["w"]])
                ts(T["p"], T["w"], sc[0])
                for c in sc[1:]:
                    stt(T["p"], T["p"], c, T["w"], ALU.add, ALU.mult)
                stt(T["su"], T["p"], 1.0, T["u"], ALU.add, ALU.mult)
                ts(T["p"], T["w"], cc_[0])
                for c in cc_[1:]:
                    stt(T["p"], T["p"], c, T["w"], ALU.add, ALU.mult)
                ts(T["cu"], T["p"], 1.0, o0=ALU.add)
                stt(T["si"], T["su"], 2.0, T["cu"], ALU.mult, ALU.mult)
                ts(T["si"], T["si"], sgn[:], rd=[sgn])
                k.op(k.dve, lambda e: e.tensor_tensor(out=T["p"].t[:, 0:w], in0=T["su"].t[:, 0:w], in1=T["su"].t[:, 0:w],
                                                      op=ALU.mult), reads=[T["su"]], writes=[T["p"]])
                ts(T["co"], T["p"], -2.0, 1.0, ALU.mult, ALU.add)
                k.dma(k.sp, [(self.COS2[:, c0:c0 + w], T["co"].t[:, 0:w])], T["co"], reads=[T["co"]], writes=[self.COS2])
                k.dma(k.sp, [(self.SIN2S[:, c0:c0 + w], T["si"].t[:, 0:w])], T["si"], reads=[T["si"]], writes=[self.SIN2S])

    def proj_scratch(self, k):
        C = self.cfg
        N = C.NTOK
        S = {}
        for nm, shp, dt in (("GQT", [4, 128, N], F32), ("GKT", [4, 128, N], F32), ("GK", [N, 4, 128], F32),
                            ("GV", [N, 4, 128], F32), ("GBc", [N, 8], F32), ("GBr", [8, N], F32),
                            ("ZS", [N, 512], F32), ("QT", [4, 128, N], BF16), ("QRT", [4, 64, N], BF16),
                            ("QN2", [4, N], F32), ("KT", [4, 128, N], BF16), ("KRT", [64, N], BF16),
                            ("K2", [4, N], F32), ("V1", [N, 4, 129], BF16)):
            S[nm] = k.dram(nm, shp, dt, dma=False)
        self.S = S
        self.MIXT = k.dram("MIXT", [8, 128, N], BF16, dma=False)
        S["MIXT"] = self.MIXT

    def group_segments(self, gi, blocks):
        C = self.cfg
        segs = []
        for (r0, n, off) in blocks:
            if r0 == 0:
                segs.append((off, n, "zero", 0))
            elif r0 < C.TP:
                if segs and segs[-1][2] == "prev":
                    o, L, kd, ix = segs[-1]
                    segs[-1] = (o, L + n, kd, ix)
                else:
                    segs.append((off, n, "prev", 0))
            else:
                for j in range(0, n, 64):
                    segs.append((off + j, 64, "state", (r0 - C.TP + j) // 64))
        return segs

    def phase_proj(self, k, I, O):
        C, S, nc = self.cfg, self.S, self.nc
        B = self.banks
        hT = self.hT
        WF = self.load_resident(k, "WF", I["wf"], [128, KC, 13 * 128])
        WT = self.load_resident(k, "WT", I["wt"], [128, KC, 1160])
        WUQ = self.load_resident(k, "WUQ", I["wuq"], [128, 3, 1024])
        WUK = self.load_resident(k, "WUK", I["wuk"], [128, 2, 512])
        WUV = self.load_resident(k, "WUV", I["wuv"], [128, 2, 512])
        xt = [k.sbuf("p_xt%d" % b, [128, D], F32, dma=True) for b in range(4)]
        rawx = k.sbuf("p_rawx", [128, 12, 528], F32, dma=True)
        rawx_cc = [k.wrap(rawx.t, "p_rawx_cc%d" % c) for c in range(12)]
        ccbuf = [(k.sbuf("p_cvu%d" % i, [128, 512], F32), k.sbuf("p_sqsd%d" % i, [128, 512], F32),
                  k.sbuf("p_un%d" % i, [128, 512], F32, dma=True)) for i in range(4)]
        halo = k.sbuf("p_halo", [128, 12, 3], F32, dma=True)
        k.op(k.pool, lambda e: e.memset(halo[:], 0.0), writes=[halo])
        cqnT = k.sbuf("p_cqnT", [128, 3, 512], BF16)
        cT = k.sbuf("p_cT", [128, 2, 512], BF16)
        cos_t = k.sbuf("p_cos", [64, 512], F32, dma=True)
        sin_t = k.sbuf("p_sin", [64, 512], F32, dma=True)
        ft = [k.sbuf("p_ft%d" % i, [128, 512], F32, dma=True) for i in range(2)]
        fb = [k.sbuf("p_fb%d" % i, [128, 512], BF16, dma=True) for i in range(4)]
        tmp = [k.sbuf("p_tmp%d" % i, [128, 512], F32) for i in range(7)]
        krsq_b = k.sbuf("p_krsq", [64, 512], F32)
        tm = [k.sbuf("p_tm%d" % i, [128, 512], F32, dma=True) for i in range(4)]
        c32 = [k.sbuf("p_c32_%d" % i, [128, 256], F32, dma=True) for i in range(2)]
        cb16 = k.sbuf("p_cb16", [128, 256], BF16)
        gbt = [k.sbuf("p_gb%d" % i, [128, 16], F32, dma=True) for i in range(2)]
        gbr = k.sbuf("p_gbr", [8, 512], F32, dma=True)
        krt = k.sbuf("p_krt", [128, 256], F32, dma=True)
        v1s = [k.sbuf("p_v1_%d" % i, [128, 4, 129], BF16, dma=True) for i in range(2)]
        for v in v1s:
            k.op(k.pool, lambda e, v=v: e.memset(v[:, :, 128:129], 1.0), writes=[v])
        rows = [k.sbuf("p_row%d" % i, [1, 512], F32, dma=True) for i in range(4)]
        rr = {"ft": 0, "fb": 0, "tmp": 0, "tm": 0, "c32": 0, "gb": 0, "v1": 0, "row": 0, "fm": 0}

        def nxt(lst, key):
            b = lst[rr[key] % len(lst)]
            rr[key] += 1
            return b

        def fm_bank():
            return nxt(B[0:2], "fm")

        last_prompt_group = max(gi for gi, g in enumerate(C.groups) if any(r0 < C.TP for (r0, n) in g))
        def load_x(gj):
            for bi, (r0, n) in enumerate(C.groups[gj]):
                k.dma(k.sp, [(xt[bi][0:n, :], self.X1[r0:r0 + n, :])], xt[bi], reads=[self.X1], writes=[xt[bi]])
        load_x(0)
        for gi, grp in enumerate(C.groups):
            blocks, NT = group_layout(grp)
            segs = self.group_segments(gi, blocks)
            self.norm_group(k, blocks, xt, self.gT[:, 1, :], [B[6], B[7], B[6], B[7]])
            if gi + 1 < len(C.groups):
                load_x(gi + 1)
            pr = []
            for (r0, n, off) in blocks:
                if r0 < C.TP:
                    pr.append((off, n, r0))
                else:
                    for j in range(0, n, 64):
                        pr.append((off + j, 64, C.TP))
            k.dma(k.sp, [(cos_t[:, o:o + n], self.COS2[:, c:c + n]) for (o, n, c) in pr], cos_t,
                  reads=[self.COS2], writes=[cos_t])
            k.dma(k.sp, [(sin_t[:, o:o + n], self.SIN2S[:, c:c + n]) for (o, n, c) in pr], sin_t,
                  reads=[self.SIN2S], writes=[sin_t])

            for bi, (r0, n, off) in enumerate(blocks):
                def tok_mm(bank, c0, c1):
                    for kc in range(KC):
                        k.op(k.pe, lambda e, kc=kc: e.matmul(bank[0:n, 0:c1 - c0], lhsT=hT[:, kc, off:off + n],
                                                             rhs=WT[:, kc, c0:c1], start=(kc == 0), stop=(kc == KC - 1)),
                             reads=[hT, WT], writes=[bank], inc=(kc == KC - 1))
                tok_mm(B[3], 0, 512)
                tok_mm(B[4], 512, 896)
                tok_mm(B[5], 896, 1160)
                zs = nxt(tm, "tm")
                k.op(k.act, lambda e: e.activation(out=zs[0:n, :], in_=B[3][0:n, :], func=AF.Silu), reads=[B[3]], writes=[zs])
                k.dma(k.sp, [(S["ZS"][r0:r0 + n, :], zs[0:n, :])], zs, reads=[zs], writes=[S["ZS"]])
                self.norm_to_T(k, B[4], B[4][0:n, 0:384], n, 3, self.gqT[:, :], cqnT, off, B[6], gbuf=self.gqT)
                stt_, rstd = self.rstd_of(k, B[5][0:n, 0:256], n, 256, [B[5]])
                cc = nxt(c32, "c32")
                k.op(k.dve, lambda e: e.scalar_tensor_tensor(out=cc[0:n, :], in0=B[5][0:n, 0:256], scalar=rstd,
                                                             in1=self.gkv[0:n, :], op0=ALU.mult, op1=ALU.mult),
                     reads=[B[5], stt_, self.gkv], writes=[cc])
                k.dma(k.sp, [(O["ckv"][r0:r0 + n, :], cc[0:n, :])], cc, reads=[cc])
                k.op(k.act, lambda e: e.copy(out=cb16[0:n, :], in_=cc[0:n, :]), reads=[cc], writes=[cb16])
                tb = B[6].t[:].bitcast(BF16)
                for kc in range(2):
                    k.op(k.pe, lambda e, kc=kc: e.transpose(out=tb[:, kc * 128:kc * 128 + n],
                                                            in_=cb16[0:n, kc * 128:(kc + 1) * 128],
                                                            identity=self.identb[0:n, 0:n]),
                         reads=[cb16, self.identb], writes=[B[6]], inc=(kc == 1))
                k.op(k.act, lambda e: e.copy(out=cT[:, :, off:off + n],
                                             in_=tb[:, 0:256].rearrange("p (k t) -> p k t", t=128)[:, :, 0:n]),
                     reads=[B[6]], writes=[cT])
                gb = nxt(gbt, "gb")
                k.op(k.dve, lambda e: e.tensor_tensor(out=gb[0:n, 8:12], in0=B[5][0:n, 256:260], in1=self.abc[0:n, 4:8],
                                                      op=ALU.add), reads=[B[5], self.abc], writes=[gb])
                k.op(k.act, lambda e: e.activation(out=gb[0:n, 8:12], in_=gb[0:n, 8:12], func=AF.Exp), reads=[gb], writes=[gb])
                k.op(k.act, lambda e: e.activation(out=gb[0:n, 8:12], in_=gb[0:n, 8:12], func=AF.Ln, bias=1.0),
                     reads=[gb], writes=[gb])
                k.op(k.dve, lambda e: e.tensor_tensor(out=gb[0:n, 0:4], in0=gb[0:n, 8:12], in1=self.abc[0:n, 8:12],
                                                      op=ALU.mult), reads=[gb, self.abc], writes=[gb])
                k.op(k.act, lambda e: e.activation(out=gb[0:n, 12:16], in_=B[5][0:n, 260:264], func=AF.Exp, scale=-1.0),
                     reads=[B[5]], writes=[gb])
                k.op(k.dve, lambda e: e.tensor_scalar(out=gb[0:n, 12:16], in0=gb[0:n, 12:16], scalar1=1.0, scalar2=None,
                                                      op0=ALU.add), reads=[gb], writes=[gb])
                k.op(k.dve, lambda e: e.reciprocal(out=gb[0:n, 4:8], in_=gb[0:n, 12:16]), reads=[gb], writes=[gb])
                k.dma(k.sp, [(S["GBc"][r0:r0 + n, :], gb[0:n, 0:8])], gb, reads=[gb], writes=[S["GBc"]])
                k.op(k.pe, lambda e: e.transpose(out=B[7][0:8, 0:n], in_=gb[0:n, 0:8], identity=self.identf[0:n, 0:n]),
                     reads=[gb, self.identf], writes=[B[7]])
                k.op(k.dve, lambda e: e.tensor_copy(out=gbr[:, off:off + n], in_=B[7][0:8, 0:n]), reads=[B[7]], writes=[gbr])
                for kc in range(2):
                    k.op(k.pe, lambda e, kc=kc: e.matmul(B[3][0:n, :], lhsT=cT[:, kc, off:off + n], rhs=WUV[:, kc, :],
                                                         start=(kc == 0), stop=(kc == 1)),
                         reads=[cT, WUV], writes=[B[3]], inc=(kc == 1))
                v1 = nxt(v1s, "v1")
                k.op(k.act, lambda e: e.copy(out=v1[0:n, :, 0:128], in_=B[3][0:n, :].rearrange("p (h d) -> p h d", d=128)),
                     reads=[B[3]], writes=[v1])
                k.dma(k.sp, [(S["V1"][r0:r0 + n, :, :], v1[0:n, :, :])], v1, reads=[v1], writes=[S["V1"]])
            r00 = blocks[0][0]
            runs = []
            for (r0, n, off) in blocks:
                if runs and runs[-1][2] + runs[-1][1] == r0:
                    runs[-1] = (runs[-1][0], runs[-1][1] + n, runs[-1][2])
                else:
                    runs.append((off, n, r0))

            def store_fm(dst3, stage, rows_=128):
                k.dma(k.sp, [(dst3[:, r:r + n], stage[0:rows_, o:o + n]) for (o, n, r) in runs], stage,
                      reads=[stage], writes=[])
            k.dma(k.sp, [(S["GBr"][:, r:r + n], gbr[:, o:o + n]) for (o, n, r) in runs], gbr, reads=[gbr])

            for si, (o, L, kind, ix) in enumerate(segs):
                xo = o + 3 * si
                if kind == "zero":
                    k.op(k.pool, lambda e, xo=xo: e.memset(rawx[:, :, xo:xo + 3], 0.0), writes=rawx_cc)
                elif kind == "prev":
                    k.op(k.pool, lambda e, xo=xo: e.tensor_copy(out=rawx[:, :, xo:xo + 3], in_=halo[:]),
                         reads=[halo], writes=rawx_cc)
                else:
                    with nc.allow_non_contiguous_dma(reason="3-row conv history, transposed on load"):
                        k.dma(k.sp, [(rawx[:, :, xo + t], I["state_conv"][ix, t, :].rearrange("(cc p) -> p cc", p=128))
                                     for t in range(3)], rawx, writes=rawx_cc)
            def cc_stream(cc_i, sl):
                bank, (cvu, sqsd, un) = B[sl], ccbuf[sl]
                rx = rawx_cc[cc_i]
                h = cc_i % 4
                for kc in range(KC):
                    k.op(k.pe, lambda e, kc=kc: e.matmul(bank[:, 0:NT], lhsT=WF[:, kc, cc_i * 128:(cc_i + 1) * 128],
                                                         rhs=hT[:, kc, 0:NT], start=(kc == 0), stop=(kc == KC - 1)),
                         reads=[WF, hT], writes=[bank], inc=(kc == KC - 1))
                for si, (o, L, kind, ix) in enumerate(segs):
                    xo = o + 3 * si
                    k.op(k.act, lambda e, o=o, L=L, xo=xo: e.copy(out=rawx[:, cc_i, xo + 3:xo + 3 + L], in_=bank[:, o:o + L]),
                         reads=[bank], writes=[rx])
                for si, (o, L, kind, ix) in enumerate(segs):
                    xo = o + 3 * si
                    k.op(k.dve, lambda e, o=o, L=L, xo=xo: e.tensor_scalar(
                        out=cvu[:, o:o + L], in0=rawx[:, cc_i, xo:xo + L], scalar1=self.cwT[:, cc_i, 0:1], scalar2=None,
                        op0=ALU.mult), reads=[rx, self.cwT], writes=[cvu])
                    for j in range(1, 4):
                        k.op(k.dve, lambda e, o=o, L=L, xo=xo, j=j: e.scalar_tensor_tensor(
                            out=cvu[:, o:o + L], in0=rawx[:, cc_i, xo + j:xo + j + L], scalar=self.cwT[:, cc_i, j:j + 1],
                            in1=cvu[:, o:o + L], op0=ALU.mult, op1=ALU.add), reads=[rx, self.cwT, cvu], writes=[cvu])
                k.op(k.act, lambda e: e.activation(out=cvu[:, 0:NT], in_=cvu[:, 0:NT], func=AF.Silu), reads=[cvu], writes=[cvu])
                src = cvu
                if cc_i < 8:
                    k.op(k.act, lambda e: e.activation(out=sqsd[:, 0:NT], in_=cvu[:, 0:NT], func=AF.Square), reads=[cvu], writes=[sqsd])
                    yield
                    ob_ = B[4 + sl % 2]
                    k.op(k.pe, lambda e: e.matmul(ob_[:, 0:NT], lhsT=self.ones[:, :], rhs=sqsd[:, 0:NT], start=True, stop=True),
                         reads=[self.ones, sqsd], writes=[ob_])
                    mul = 128.0 if cc_i < 4 else 1.0
                    k.op(k.act, lambda e: e.activation(out=sqsd[:, 0:NT], in_=ob_[:, 0:NT], func=AF.Sqrt, scale=mul,
                                                       bias=self.l2b[:, (0 if cc_i < 4 else 1):(1 if cc_i < 4 else 2)]),
                         reads=[ob_, self.l2b], writes=[sqsd])
                    k.op(k.dve, lambda e: e.reciprocal(out=sqsd[:, 0:NT], in_=sqsd[:, 0:NT]), reads=[sqsd], writes=[sqsd])
                    k.op(k.dve, lambda e: e.tensor_tensor(out=un[:, 0:NT], in0=cvu[:, 0:NT], in1=sqsd[:, 0:NT], op=ALU.mult),
                         reads=[cvu, sqsd], writes=[un])
                    store_fm(S["GQT" if cc_i < 4 else "GKT"].t[h], un)
                    src = un
                if cc_i >= 4:
                    yield
                    dst = S["GK" if cc_i < 8 else "GV"]
                    tbk = B[6 + cc_i % 2]
                    for bi, (r0, n, off) in enumerate(blocks):
                        k.op(k.pe, lambda e, n=n, off=off, bi=bi: e.transpose(out=tbk[0:n, bi * 128:(bi + 1) * 128], in_=src[:, off:off + n],
                                                                             identity=self.identf[:, :]),
                             reads=[src, self.identf], writes=[tbk], inc=(bi == len(blocks) - 1))
                    st_ = nxt(tm, "tm")
                    nb_ = len(blocks)
                    k.op(k.act, lambda e, st_=st_: e.copy(out=st_[:, 0:nb_ * 128], in_=tbk[:, 0:nb_ * 128]), reads=[tbk], writes=[st_])
                    k.dma(k.sp, [(dst[r0:r0 + n, h, :], st_[0:n, bi * 128:(bi + 1) * 128]) for bi, (r0, n, off) in enumerate(blocks)],
                          st_, reads=[st_], writes=[dst])
            run_streams([(lambda sl, c=c: cc_stream(c, sl)) for c in range(12)], 4)
            for si, (o, L, kind, ix) in enumerate(segs):
                xo = o + 3 * si
                if kind in ("prev", "zero"):
                    k.op(k.pool, lambda e, xo=xo, L=L: e.tensor_copy(out=halo[:], in_=rawx[:, :, xo + L:xo + L + 3]),
                         reads=rawx_cc, writes=[halo])
                else:
                    with nc.allow_non_contiguous_dma(reason="3-row conv state out"):
                        k.dma(k.sp, [(O["sconv"][ix, t, :].rearrange("(cc p) -> p cc", p=128), rawx[:, :, xo + L + t])
                                     for t in range(3)], rawx, reads=rawx_cc)
            if gi == last_prompt_group:
                with nc.allow_non_contiguous_dma(reason="3-row conv state out"):
                    k.dma(k.sp, [(O["pconv"][t, :].rearrange("(cc p) -> p cc", p=128), halo[:, :, t]) for t in range(3)],
                          halo, reads=[halo])

            def rope(bank_a, bank_b, out_t):
                t1 = nxt(tmp, "tmp")
                k.op(k.dve, lambda e: e.tensor_tensor(out=t1[0:64, 0:NT], in0=bank_a[0:64, 0:NT], in1=cos_t[:, 0:NT], op=ALU.mult),
                     reads=[bank_a, cos_t], writes=[t1])
                t2 = nxt(tmp, "tmp")
                k.op(k.dve, lambda e: e.tensor_tensor(out=t2[0:64, 0:NT], in0=bank_b[0:64, 0:NT], in1=sin_t[:, 0:NT], op=ALU.mult),
                     reads=[bank_b, sin_t], writes=[t2])
                k.op(k.pool, lambda e: e.tensor_tensor(out=out_t[0:64, 0:NT], in0=t1[0:64, 0:NT], in1=t2[0:64, 0:NT], op=ALU.add),
                     reads=[t1, t2], writes=[out_t])

            def fm_mm(bank, Wt, nkc, c0, M, rhsT):
                for kc in range(nkc):
                    k.op(k.pe, lambda e, kc=kc: e.matmul(bank[0:M, 0:NT], lhsT=Wt[:, kc, c0:c0 + M], rhs=rhsT[:, kc, 0:NT],
                                                         start=(kc == 0), stop=(kc == nkc - 1)),
                         reads=[Wt, rhsT], writes=[bank], inc=(kc == nkc - 1))
            ba, bb = B[0], B[1]
            fm_mm(ba, WF, KC, 12 * 128, 64, hT)
            fm_mm(bb, WF, KC, 12 * 128 + 64, 64, hT)
            kro = nxt(ft, "ft")
            rope(ba, bb, kro)
            krb = nxt(fb, "fb")
            k.op(k.act, lambda e: e.copy(out=krb[0:64, 0:NT], in_=kro[0:64, 0:NT]), reads=[kro], writes=[krb])
            store_fm(S["KRT"].t, krb, 64)
            krsq = krsq_b
            k.op(k.act, lambda e: e.activation(out=krsq[0:64, 0:NT], in_=kro[0:64, 0:NT], func=AF.Square), reads=[kro], writes=[krsq])
            for bi, (r0, n, off) in enumerate(blocks):
                k.op(k.pe, lambda e, n=n, off=off, bi=bi: e.transpose(out=B[7][0:n, bi * 64:(bi + 1) * 64], in_=kro[0:64, off:off + n],
                                                                     identity=self.identf[0:64, 0:64]),
                     reads=[kro, self.identf], writes=[B[7]], inc=(bi == len(blocks) - 1))
            k.op(k.dve, lambda e: e.tensor_copy(out=krt[:, 0:len(blocks) * 64], in_=B[7][:, 0:len(blocks) * 64]), reads=[B[7]], writes=[krt])
            k.dma(k.sp, [(O["kr"][r0:r0 + n, :], krt[0:n, bi * 64:(bi + 1) * 64]) for bi, (r0, n, off) in enumerate(blocks)],
                  krt, reads=[krt])

            for h in range(4):
                bq = fm_bank()
                fm_mm(bq, WUQ, 3, h * 256, 128, cqnT)
                qb = nxt(fb, "fb")
                k.op(k.act, lambda e: e.copy(out=qb[:, 0:NT], in_=bq[:, 0:NT]), reads=[bq], writes=[qb])
                store_fm(S["QT"].t[h], qb)
                qsq = nxt(tmp, "tmp")
                k.op(k.act, lambda e: e.activation(out=qsq[:, 0:NT], in_=bq[:, 0:NT], func=AF.Square), reads=[bq], writes=[qsq])
                ba, bb = fm_bank(), B[2]
                fm_mm(ba, WUQ, 3, h * 256 + 128, 64, cqnT)
                fm_mm(bb, WUQ, 3, h * 256 + 192, 64, cqnT)
                qr = nxt(tmp, "tmp")
                rope(ba, bb, qr)
                qrb = nxt(fb, "fb")
                k.op(k.act, lambda e: e.copy(out=qrb[0:64, 0:NT], in_=qr[0:64, 0:NT]), reads=[qr], writes=[qrb])
                store_fm(S["QRT"].t[h], qrb, 64)
                qrsq = nxt(tmp, "tmp")
                k.op(k.act, lambda e: e.activation(out=qrsq[0:64, 0:NT], in_=qr[0:64, 0:NT], func=AF.Square), reads=[qr], writes=[qrsq])
                k.op(k.pe, lambda e: e.matmul(B[7][0:1, 0:NT], lhsT=self.ones[:, 0:1], rhs=qsq[:, 0:NT], start=True, stop=False),
                     reads=[self.ones, qsq], writes=[B[7]], inc=False)
                k.op(k.pe, lambda e: e.matmul(B[7][0:1, 0:NT], lhsT=self.ones[0:64, 0:1], rhs=qrsq[0:64, 0:NT], start=False, stop=True),
                     reads=[self.ones, qrsq], writes=[B[7]])
                rw = nxt(rows, "row")
                k.op(k.dve, lambda e: e.tensor_copy(out=rw[:, 0:NT], in_=B[7][0:1, 0:NT]), reads=[B[7]], writes=[rw])
                k.dma(k.sp, [(S["QN2"][h:h + 1, r:r + n], rw[:, o:o + n]) for (o, n, r) in runs], rw, reads=[rw])
                bk = fm_bank()
                fm_mm(bk, WUK, 2, h * 128, 128, cT)
                kb = nxt(fb, "fb")
                k.op(k.act, lambda e: e.copy(out=kb[:, 0:NT], in_=bk[:, 0:NT]), reads=[bk], writes=[kb])
                store_fm(S["KT"].t[h], kb)
                ksq = nxt(tmp, "tmp")
                k.op(k.act, lambda e: e.activation(out=ksq[:, 0:NT], in_=bk[:, 0:NT], func=AF.Square), reads=[bk], writes=[ksq])
                k.op(k.pe, lambda e: e.matmul(B[7][0:1, 0:NT], lhsT=self.ones[:, 0:1], rhs=ksq[:, 0:NT], start=True, stop=False),
                     reads=[self.ones, ksq], writes=[B[7]], inc=False)
                k.op(k.pe, lambda e: e.matmul(B[7][0:1, 0:NT], lhsT=self.ones[0:64, 0:1], rhs=krsq[0:64, 0:NT], start=False, stop=True),
                     reads=[self.ones, krsq], writes=[B[7]])
                rw = nxt(rows, "row")
                k.op(k.dve, lambda e: e.tensor_copy(out=rw[:, 0:NT], in_=B[7][0:1, 0:NT]), reads=[B[7]], writes=[rw])
                k.dma(k.sp, [(S["K2"][h:h + 1, r:r + n], rw[:, o:o + n]) for (o, n, r) in runs], rw, reads=[rw])


    def gdn_consts(self, k):
        def sel(name, src_val, fill, pattern, cm, op):
            t = k.sbuf(name, [128, 128], F32)
            src = self.ones if src_val == 1.0 else self.zeros
            k.op(k.pool, lambda e: e.affine_select(out=t[:], in_=src[:], pattern=pattern, compare_op=op, fill=fill,
                                                   base=0, channel_multiplier=cm), reads=[src], writes=[t])
            return t
        self.zeros = k.sbuf("g_zeros", [128, 128], F32)
        k.op(k.pool, lambda e: e.memset(self.zeros[:], 0.0), writes=[self.zeros])
        BIG = 30000.0
        self.triu = sel("g_triu", 1.0, 0.0, [[1, 128]], -1, ALU.is_ge)
        self.mmin_incl = sel("g_mmin", 0.0, -BIG, [[1, 128]], -1, ALU.is_ge)
        self.strict01 = sel("g_st01", 1.0, 0.0, [[1, 128]], -1, ALU.is_gt)
        self.mmax_strict = sel("g_mmax", 0.0, BIG, [[-1, 128]], 1, ALU.is_gt)
        gg = k.sbuf("g_gain", [128, 128], F32, dma=True)
        k.dma(k.sp, [(gg[:], self.I["gdn_norm"].partition_broadcast(128))], gg, writes=[gg])
        self.ggain = gg

    def phase_gdn(self, k, I, O):
        C, S, B = self.cfg, self.S, self.banks
        self.gdn_consts(k)
        idf = self.identf
        seqs = [("p", 0, [(0, 16)] + [(16 + 128 * i, 128) for i in range(C.SEQ // 128)])]
        for b in range(C.NSB):
            seqs.append(("s", b, [(C.TP + 64 * b, 64)]))
        F = lambda nm, shp, dma=False: k.sbuf(nm, shp, F32, dma=dma)

        def FR(nm, shp):
            return k.sbuf(nm, shp, F32)
        NB = 2
        inp = [dict(qT=F("gi_qT%d" % i, [128, 4, 128], True), kT=F("gi_kT%d" % i, [128, 4, 128], True),
                    ktm=F("gi_ktm%d" % i, [128, 4, 128], True), vtm=F("gi_vtm%d" % i, [128, 4, 128], True),
                    grow=F("gi_grow%d" % i, [128, 4, 128], True), brow=F("gi_brow%d" % i, [128, 4, 128], True),
                    gbc=F("gi_gbc%d" % i, [128, 8], True), zs=F("gi_zs%d" % i, [128, 512], True)) for i in range(3)]
        hand = [dict(WT=FR("gh_WT%d" % i, [128, 4, 128]), U0=F("gh_U0%d" % i, [128, 4, 128]),
                     QKd=FR("gh_QKd%d" % i, [128, 4, 128]), Kw=FR("gh_Kw%d" % i, [128, 4, 128]),
                     qe=FR("gh_qe%d" % i, [128, 4, 128]), gam=F("gh_gam%d" % i, [128, 4])) for i in range(NB)]
        Gb = F("g_Gb", [128, 4, 128]); gcol = F("g_gcol", [128, 4]); egc = F("g_egc", [128, 4]); bec = F("g_bec", [128, 4])
        wcol = F("g_wcol", [128, 4]); egr = F("g_egr", [128, 4, 128]); kbT = FR("g_kbT", [128, 4, 128])
        Kbe = FR("g_Kbe", [128, 4, 128]); bv = FR("g_bv", [128, 4, 128])
        dtmp = F("g_dtmp", [128, 4, 128]); DTi = F("g_DTi", [128, 4, 128]); DTs = F("g_DTs", [128, 4, 128]); Ds = F("g_Ds", [128, 4, 128])
        Np = [FR("g_N%d" % i, [128, 4, 128]) for i in range(2)]
        Bp = [FR("g_B%d" % i, [128, 4, 128]) for i in range(2)]
        R = FR("g_R", [128, 4, 128])
        kTr = FR("g_kTr", [128, 4, 128]); qTr = FR("g_qTr", [128, 4, 128]); Mr = FR("g_Mr", [128, 4, 128])
        g1_r = None
        NpH = [[k.wrap(Np[i].t, "g_N%d_%d" % (i, hf)) for hf in range(2)] for i in range(2)]
        BpH = [[k.wrap(Bp[i].t, "g_B%d_%d" % (i, hf)) for hf in range(2)] for i in range(2)]
        RH = [k.wrap(R.t, "g_R_%d" % hf) for hf in range(2)]
        M = [F("g_M%d" % i, [128, 4, 128], True) for i in range(2)]
        u = FR("g_u", [128, 4, 128])
        onr = F("g_onr", [128, 8]); og = F("g_og", [128, 512]); ogb = k.sbuf("g_ogb", [128, 512], BF16)
        mixs = [k.sbuf("g_mix%d" % i, [128, 4, 128], BF16, dma=True) for i in range(2)]
        g1_r = [kbT, Kbe, bv, Np[0], Np[1], Bp[0], Bp[1], R, kTr, qTr]

        def setmode(tiles, flag):
            for t_ in tiles:
                t_.rmode = flag
        ones_row = F("g_ones1", [128, 128])
        k.op(k.pool, lambda e: e.memset(ones_row[:], 1.0), writes=[ones_row])
        ci = 0
        mcur = 0
        for (kind, bidx, chunks) in seqs:
            Mc = M[mcur]
            if kind == "p":
                k.op(k.pool, lambda e, Mc=Mc: e.memset(Mc[:], 0.0), writes=[Mc])
            else:
                k.dma(k.sp, [(Mc[:, h, :], I["state_gdn"][bidx, h, :, :]) for h in range(4)], Mc, writes=[Mc])
            k.op(k.act, lambda e, Mc=Mc: e.activation(func=AF.Copy, out=Mr.t[:], in_=Mc.t[:]), reads=[Mc], writes=[Mr])
            pend = None
            def issue_loads(item_, ci_):
                r0, Lr = item_
                X = inp[ci_ % 3]
                L = Lr
                r1 = r0 + L
                if Lr < 128:
                    for nm in ("ktm", "vtm", "gbc"):
                        k.op(k.pool, lambda e, nm=nm: e.memset(X[nm][:], 0.0), writes=[X[nm]])
                k.dma(k.sp, [(X["qT"][:, :, 0:L], S["GQT"].t[:, :, r0:r1].rearrange("h p t -> p h t"))], X["qT"], writes=[X["qT"]])
                k.dma(k.sp, [(X["kT"][:, :, 0:L], S["GKT"].t[:, :, r0:r1].rearrange("h p t -> p h t"))], X["kT"], writes=[X["kT"]])
                k.dma(k.sp, [(X["ktm"][0:L, :, :], S["GK"].t[r0:r1, :, :])], X["ktm"], writes=[X["ktm"]])
                k.dma(k.sp, [(X["vtm"][0:L, :, :], S["GV"].t[r0:r1, :, :])], X["vtm"], writes=[X["vtm"]])
                k.dma(k.sp, [(X["grow"][:, h, 0:L], S["GBr"].t[h, r0:r1].partition_broadcast(128)) for h in range(4)],
                      X["grow"], writes=[X["grow"]])
                k.dma(k.sp, [(X["brow"][:, h, 0:L], S["GBr"].t[4 + h, r0:r1].partition_broadcast(128)) for h in range(4)],
                      X["brow"], writes=[X["brow"]])
                k.dma(k.sp, [(X["gbc"][0:L, :], S["GBc"].t[r0:r1, :])], X["gbc"], writes=[X["gbc"]])
                k.dma(k.sp, [(X["zs"][0:L, :], S["ZS"].t[r0:r1, :])], X["zs"], writes=[X["zs"]])
                if Lr < 128:
                    for nm in ("qT", "kT", "grow", "brow"):
                        k.op(k.pool, lambda e, nm=nm: e.memset(X[nm][:, :, Lr:128], 0.0), writes=[X[nm]])

            def g1_gen(item_, ci_, res_):
                r0, Lr = item_
                X, H = inp[ci_ % 3], hand[ci_ % NB]
                setmode(g1_r + [H["WT"], H["QKd"], H["Kw"], H["qe"]], True)
                L = 128
                qT, kT, ktm, vtm, grow, brow, gbc = (X[n] for n in ("qT", "kT", "ktm", "vtm", "grow", "brow", "gbc"))
                for h in range(4):
                    k.op(k.dve, lambda e, h=h: e.tensor_tensor_scan(out=Gb[:, h, 0:L], data0=ones_row[:, 0:L], data1=grow[:, h, 0:L],
                                                                   initial=0.0, op0=ALU.mult, op1=ALU.add),
                         reads=[ones_row, grow], writes=[Gb])
                k.op(k.pe, lambda e: e.matmul(B[0][0:L, 0:4], lhsT=self.triu[0:L, 0:L], rhs=gbc[0:L, 0:4], start=True, stop=True),
                     reads=[self.triu, gbc], writes=[B[0]])
                k.op(k.dve, lambda e: e.tensor_copy(out=gcol[0:L, :], in_=B[0][0:L, 0:4]), reads=[B[0]], writes=[gcol])
                k.op(k.act, lambda e: e.activation(out=egc[0:L, :], in_=gcol[0:L, :], func=AF.Exp), reads=[gcol], writes=[egc])
                k.op(k.dve, lambda e: e.tensor_tensor(out=bec[0:L, :], in0=egc[0:L, :], in1=gbc[0:L, 4:8], op=ALU.mult),
                     reads=[egc, gbc], writes=[bec])
                k.op(k.act, lambda e: e.activation(out=H["gam"][:, :], in_=Gb[:, :, L - 1], func=AF.Exp), reads=[Gb], writes=[H["gam"]])
                k.op(k.dve, lambda e: e.tensor_tensor(out=wcol[0:L, :], in0=Gb[0:L, :, L - 1], in1=gcol[0:L, :], op=ALU.subtract),
                     reads=[Gb, gcol], writes=[wcol])
                k.op(k.act, lambda e: e.activation(out=wcol[0:L, :], in_=wcol[0:L, :], func=AF.Exp), reads=[wcol], writes=[wcol])
                k.op(k.act, lambda e: e.activation(out=egr[:, :, 0:L], in_=Gb[:, :, 0:L], func=AF.Exp), reads=[Gb], writes=[egr])
                k.op(k.dve, lambda e: e.tensor_tensor(out=H["qe"][:, :, 0:L], in0=qT[:, :, 0:L], in1=egr[:, :, 0:L], op=ALU.mult),
                     reads=[qT, egr], writes=[H["qe"]])
                k.op(k.dve, lambda e: e.tensor_tensor(out=kbT[:, :, 0:L], in0=kT[:, :, 0:L], in1=brow[:, :, 0:L], op=ALU.mult),
                     reads=[kT, brow], writes=[kbT])
                k.op(k.act, lambda e: e.activation(func=AF.Copy, out=kTr[:, :, 0:L], in_=kT[:, :, 0:L]), reads=[kT], writes=[kTr])
                k.op(k.act, lambda e: e.activation(func=AF.Copy, out=qTr[:, :, 0:L], in_=qT[:, :, 0:L]), reads=[qT], writes=[qTr])
                bc3 = lambda col: col.unsqueeze(2).to_broadcast([L, 4, 128])
                k.op(k.dve, lambda e: e.tensor_tensor(out=Kbe[0:L], in0=ktm[0:L], in1=bc3(bec[0:L, :]), op=ALU.mult),
                     reads=[ktm, bec], writes=[Kbe])
                k.op(k.dve, lambda e: e.tensor_tensor(out=H["Kw"][0:L], in0=ktm[0:L], in1=bc3(wcol[0:L, :]), op=ALU.mult),
                     reads=[ktm, wcol], writes=[H["Kw"]])
                k.op(k.dve, lambda e: e.tensor_tensor(out=bv[0:L], in0=vtm[0:L], in1=bc3(gbc[0:L, 4:8]), op=ALU.mult),
                     reads=[vtm, gbc], writes=[bv])
                yield
                for h in range(4):
                    k.op(k.dve, lambda e, h=h: e.scalar_tensor_tensor(out=dtmp[0:L, h, 0:L], in0=Gb[0:L, h, 0:L], scalar=gcol[0:L, h:h + 1],
                                                                     in1=self.mmin_incl[0:L, 0:L], op0=ALU.subtract, op1=ALU.min),
                         reads=[Gb, gcol, self.mmin_incl], writes=[dtmp])
                k.op(k.act, lambda e: e.activation(out=DTi[0:L, :, 0:L], in_=dtmp[0:L, :, 0:L], func=AF.Exp), reads=[dtmp], writes=[DTi])
                k.op(k.pool, lambda e: e.tensor_tensor(out=DTs[0:L, :, 0:L], in0=DTi[0:L, :, 0:L],
                                                       in1=self.strict01[0:L, 0:L].unsqueeze(1).to_broadcast([L, 4, L]), op=ALU.mult),
                     reads=[DTi, self.strict01], writes=[DTs])
                for h in range(4):
                    k.op(k.dve, lambda e, h=h: e.scalar_tensor_tensor(out=dtmp[0:L, h, 0:L], in0=Gb[0:L, h, 0:L], scalar=gcol[0:L, h:h + 1],
                                                                     in1=self.mmax_strict[0:L, 0:L], op0=ALU.subtract, op1=ALU.max),
                         reads=[Gb, gcol, self.mmax_strict], writes=[dtmp])
                k.op(k.act, lambda e: e.activation(out=Ds[0:L, :, 0:L], in_=dtmp[0:L, :, 0:L], func=AF.Exp, scale=-1.0),
                     reads=[dtmp], writes=[Ds])
                yield
                N0, B0 = Np[0], Bp[0]
                for h in range(4):
                    cs = slice(h * 128, h * 128 + L)
                    k.op(k.pe, lambda e, h=h, cs=cs: e.matmul(B[0][0:L, cs], lhsT=kbT[:, h, 0:L], rhs=kTr[:, h, 0:L], start=True, stop=True),
                         reads=[kbT, kTr], writes=[B[0]])
                    k.op(k.pe, lambda e, h=h, cs=cs: e.matmul(B[1][0:L, cs], lhsT=kTr[:, h, 0:L], rhs=kbT[:, h, 0:L], start=True, stop=True),
                         reads=[kbT, kTr], writes=[B[1]])
                    k.op(k.pe, lambda e, h=h, cs=cs: e.matmul(B[2][0:L, cs], lhsT=kTr[:, h, 0:L], rhs=qTr[:, h, 0:L], start=True, stop=True),
                         reads=[kTr, qTr], writes=[B[2]])
                v4 = lambda bank: bank.t[:].rearrange("p (h t) -> p h t", t=128)[0:L, :, 0:L]
                k.op(k.dve, lambda e: e.scalar_tensor_tensor(out=N0[0:L, :, 0:L], in0=v4(B[0]), scalar=-1.0, in1=Ds[0:L, :, 0:L],
                                                             op0=ALU.mult, op1=ALU.mult), reads=[B[0], Ds], writes=[N0])
                k.op(k.dve, lambda e: e.scalar_tensor_tensor(out=B0[0:L, :, 0:L], in0=v4(B[1]), scalar=-1.0, in1=DTs[0:L, :, 0:L],
                                                             op0=ALU.mult, op1=ALU.mult), reads=[B[1], DTs], writes=[B0])
                k.op(k.dve, lambda e: e.tensor_tensor(out=H["QKd"][0:L, :, 0:L], in0=v4(B[2]), in1=DTi[0:L, :, 0:L], op=ALU.mult),
                     reads=[B[2], DTi], writes=[H["QKd"]])
                k.op(k.dve, lambda e: e.tensor_tensor(out=R[0:L, :, 0:L], in0=B0.f32((slice(0, L), slice(None), slice(0, L))),
                                                       in1=idf[0:L, 0:L].unsqueeze(1).to_broadcast([L, 4, L]), op=ALU.add),
                     reads=[B0, idf], writes=[R])
                J = 0
                while (1 << (J + 1)) < L:
                    J += 1
                def sq_stream(hs, sl):
                    sqb, rb = (B[3], B[4])[sl], (B[2], B[1])[sl]
                    hsl = slice(hs[0], hs[-1] + 1)
                    p2 = lambda bank, c0: bank.t[:, c0:c0 + 256].rearrange("p (h t) -> p h t", t=128)[0:L, :, 0:L]
                    for j in range(1, J + 1):
                        Nn, Bn, No, Bo = NpH[j % 2][sl], BpH[j % 2][sl], NpH[(j - 1) % 2][sl], BpH[(j - 1) % 2][sl]
                        Nn_t, Bn_t, No_t, Bo_t = Np[j % 2], Bp[j % 2], Np[(j - 1) % 2], Bp[(j - 1) % 2]
                        for hl, h in enumerate(hs):
                            k.op(k.pe, lambda e, h=h, hl=hl: e.matmul(sqb[0:L, hl * 128:hl * 128 + L], lhsT=Bo_t[0:L, h, 0:L], rhs=No_t[0:L, h, 0:L],
                                                                     start=True, stop=True), reads=[Bo, No], writes=[sqb])
                            if j < J:
                                k.op(k.pe, lambda e, h=h, hl=hl: e.matmul(sqb[0:L, 256 + hl * 128:256 + hl * 128 + L], lhsT=No_t[0:L, h, 0:L],
                                                                         rhs=Bo_t[0:L, h, 0:L], start=True, stop=True), reads=[Bo, No], writes=[sqb])
                        k.op(k.act, lambda e: e.activation(func=AF.Copy, out=Nn_t[0:L, hsl, 0:L], in_=p2(sqb, 0)), reads=[sqb], writes=[Nn])
                        if j < J:
                            k.op(k.act, lambda e: e.activation(func=AF.Copy, out=Bn_t[0:L, hsl, 0:L], in_=p2(sqb, 256)), reads=[sqb], writes=[Bn])
                        yield
                        for hl, h in enumerate(hs):
                            k.op(k.pe, lambda e, h=h, hl=hl: e.matmul(rb[0:L, hl * 128:hl * 128 + L], lhsT=Nn_t[0:L, h, 0:L], rhs=R[0:L, h, 0:L],
                                                                     start=True, stop=True), reads=[Nn, RH[sl]], writes=[rb])
                        k.op(k.dve, lambda e: e.tensor_tensor(out=R[0:L, hsl, 0:L], in0=p2(rb, 0), in1=R.f32((slice(0, L), hsl, slice(0, L))), op=ALU.add),
                             reads=[rb, RH[sl]], writes=[RH[sl]])
                        yield
                pairs_ = ((Np[0], NpH[0]), (Np[1], NpH[1]), (Bp[0], BpH[0]), (Bp[1], BpH[1]), (R, RH))
                for whole, halves in pairs_:
                    k.split(whole, halves)
                alive = [sq_stream(hs, sl) for sl, hs in enumerate(((0, 1), (2, 3)))]
                while alive:
                    for sg in list(alive):
                        try:
                            next(sg)
                        except StopIteration:
                            alive.remove(sg)
                    yield
                for whole, halves in pairs_:
                    k.merge(whole, halves)
                for h in range(4):
                    k.op(k.pe, lambda e, h=h: e.matmul(B[0][:, h * 128:h * 128 + L], lhsT=Kbe[0:L, h, :], rhs=R[0:L, h, 0:L], start=True, stop=True),
                         reads=[Kbe, R], writes=[B[0]])
                    k.op(k.pe, lambda e, h=h: e.matmul(B[1][0:L, h * 128:(h + 1) * 128], lhsT=R[0:L, h, 0:L], rhs=bv[0:L, h, :], start=True, stop=True),
                         reads=[R, bv], writes=[B[1]])
                k.op(k.act, lambda e: e.activation(func=AF.Copy, out=H["WT"][:, :, 0:L], in_=B[0].t[:].rearrange("p (h t) -> p h t", t=128)[:, :, 0:L]),
                     reads=[B[0]], writes=[H["WT"]])
                k.op(k.act, lambda e: e.copy(out=H["U0"][0:L], in_=B[1].t[:].rearrange("p (h t) -> p h t", t=128)[0:L]),
                     reads=[B[1]], writes=[H["U0"]])
                res_[0] = (r0, Lr, X, H)

            def g2_gen(pend_, mc_):
                (q0, Lo, Xq, Hq) = pend_
                Lq = 128
                Mo, Mn = M[mc_], M[1 - mc_]
                setmode([u, Mr, Hq["WT"], Hq["QKd"], Hq["Kw"], Hq["qe"]], True)
                for h in range(4):
                    k.op(k.pe, lambda e, h=h: e.matmul(B[5][0:Lq, h * 128:(h + 1) * 128], lhsT=Hq["WT"][:, h, 0:Lq], rhs=Mr[:, h, :], start=True, stop=True),
                         reads=[Hq["WT"], Mr], writes=[B[5]])
                k.op(k.dve, lambda e: e.scalar_tensor_tensor(out=u[0:Lq], in0=B[5].t[:].rearrange("p (h t) -> p h t", t=128)[0:Lq], scalar=-1.0,
                                                             in1=Hq["U0"][0:Lq], op0=ALU.mult, op1=ALU.add),
                     reads=[B[5], Hq["U0"]], writes=[u])
                yield
                for h in range(4):
                    k.op(k.pe, lambda e, h=h: e.matmul(B[6][:, h * 128:(h + 1) * 128], lhsT=Hq["Kw"][0:Lq, h, :], rhs=u[0:Lq, h, :], start=True, stop=True),
                         reads=[Hq["Kw"], u], writes=[B[6]])
                for h in range(4):
                    k.op(k.pe, lambda e, h=h: e.matmul(B[7][0:Lq, h * 128:(h + 1) * 128], lhsT=Hq["qe"][:, h, 0:Lq], rhs=Mr[:, h, :], start=True, stop=False),
                         reads=[Hq["qe"], Mr], writes=[B[7]], inc=False)
                    k.op(k.pe, lambda e, h=h: e.matmul(B[7][0:Lq, h * 128:(h + 1) * 128], lhsT=Hq["QKd"][0:Lq, h, 0:Lq], rhs=u[0:Lq, h, :], start=False, stop=True),
                         reads=[Hq["QKd"], u], writes=[B[7]])
                for h in range(4):
                    k.op(k.dve, lambda e, h=h: e.scalar_tensor_tensor(out=Mn[:, h, :], in0=Mo[:, h, :], scalar=Hq["gam"][:, h:h + 1],
                                                                     in1=B[6][:, h * 128:(h + 1) * 128], op0=ALU.mult, op1=ALU.add),
                         reads=[Mo, Hq["gam"], B[6]], writes=[Mn])
                k.op(k.act, lambda e: e.activation(func=AF.Copy, out=Mr.t[:], in_=Mn.t[:]), reads=[Mn], writes=[Mr])
                yield
                o3 = B[7].t[:].rearrange("p (h d) -> p h d", d=128)
                for h in range(4):
                    k.op(k.act, lambda e, h=h: e.activation(out=self.junk[0:Lo, 0:128], in_=B[7][0:Lo, h * 128:(h + 1) * 128], func=AF.Square,
                                                           accum_out=onr[0:Lo, h:h + 1]), reads=[B[7]], writes=[self.junk, onr])
                k.op(k.act, lambda e: e.activation(out=onr[0:Lo, 4:8], in_=onr[0:Lo, 0:4], func=AF.Sqrt, bias=self.epsb[0:Lo, :], scale=1.0 / 128),
                     reads=[onr, self.epsb], writes=[onr])
                k.op(k.dve, lambda e: e.reciprocal(out=onr[0:Lo, 4:8], in_=onr[0:Lo, 4:8]), reads=[onr], writes=[onr])
                og3 = og.t[:].rearrange("p (h d) -> p h d", d=128)
                k.op(k.dve, lambda e: e.tensor_tensor(out=og3[0:Lo], in0=o3[0:Lo], in1=onr[0:Lo, 4:8].unsqueeze(2).to_broadcast([Lo, 4, 128]), op=ALU.mult),
                     reads=[B[7], onr], writes=[og])
                k.op(k.pool, lambda e: e.tensor_tensor(out=og3[0:Lo], in0=og3[0:Lo], in1=self.ggain[0:Lo, :].unsqueeze(1).to_broadcast([Lo, 4, 128]), op=ALU.mult),
                     reads=[og, self.ggain], writes=[og])
                k.op(k.pool, lambda e: e.tensor_tensor(out=ogb[0:Lo, :], in0=og[0:Lo, :], in1=Xq["zs"][0:Lo, :], op=ALU.mult),
                     reads=[og, Xq["zs"]], writes=[ogb])
                yield
                tb = B[5].t[:].bitcast(BF16)
                for h in range(4):
                    k.op(k.pe, lambda e, h=h: e.transpose(out=tb[:, h * 128:h * 128 + Lo], in_=ogb[0:Lo, h * 128:(h + 1) * 128], identity=self.identb[0:Lo, 0:Lo]),
                         reads=[ogb, self.identb], writes=[B[5]], inc=(h == 3))
                mx = mixs[mc_]
                k.op(k.act, lambda e: e.copy(out=mx[:, :, 0:Lo], in_=tb[:, 0:512].rearrange("p (h t) -> p h t", t=128)[:, :, 0:Lo]),
                     reads=[B[5]], writes=[mx])
                k.dma(k.sp, [(self.MIXT.t[0:4, :, q0:q0 + Lo].rearrange("h p t -> p h t"), mx[:, :, 0:Lo])], mx, reads=[mx], writes=[self.MIXT])

            issue_loads(chunks[0], ci)
            for ix_, item in enumerate(chunks + [None]):
                if item is not None and ix_ + 1 < len(chunks):
                    issue_loads(chunks[ix_ + 1], ci + 1)
                res = [None]
                gens = []
                if item is not None:
                    gens.append(g1_gen(item, ci, res))
                    ci += 1
                if pend is not None:
                    gens.append(g2_gen(pend, mcur))
                    mcur = 1 - mcur
                while gens:
                    for gg in list(gens):
                        try:
                            next(gg)
                        except StopIteration:
                            gens.remove(gg)
                pend = res[0]
            Mf = M[mcur]
            dst = O["pgdn"] if kind == "p" else O["sgdn"][bidx]
            k.dma(k.sp, [(dst[h, :, :], Mf[:, h, :]) for h in range(4)], Mf, reads=[Mf])
            mcur = 1 - mcur


    SM = (128 + 64) ** -0.5

    def attn_finish(self, k, accb, nq, h, rows0, ob, rec):
        B = self.banks
        k.op(k.dve, lambda e: e.reciprocal(out=rec[0:nq, :], in_=accb[0:nq, 128:129]), reads=[accb], writes=[rec])
        k.op(k.act, lambda e: e.activation(out=ob[0:nq, :], in_=accb[0:nq, 0:128], func=AF.Copy, scale=rec[0:nq, :]),
             reads=[accb, rec], writes=[ob])
        tb = B[6].t[:].bitcast(BF16)
        k.op(k.pe, lambda e: e.transpose(out=tb[:, 0:nq], in_=ob[0:nq, :], identity=self.identb[0:nq, 0:nq]),
             reads=[ob, self.identb], writes=[B[6]])
        st = self.a_st[self.a_cnt % 2]
        self.a_cnt += 1
        k.op(k.dve, lambda e: e.tensor_copy(out=st[:, 0:nq], in_=tb[:, 0:nq]), reads=[B[6]], writes=[st])
        k.dma(k.sp, [(self.MIXT.t[4 + h, :, rows0:rows0 + nq], st[:, 0:nq])], st, reads=[st], writes=[self.MIXT])

    def stab_row(self, k, QR, ncol, qn2_src, k2max, tmp65):
        for c0 in range(0, ncol, 2048):
            w = min(2048, ncol - c0)
            k.dma(k.sp, [(tmp65[64:65, 0:w], qn2_src[:, c0:c0 + w])], tmp65, writes=[tmp65])
            k.op(k.dve, lambda e: e.tensor_scalar(out=tmp65[64:65, 0:w], in0=tmp65[64:65, 0:w], scalar1=k2max[64:65, 0:1],
                                                  scalar2=None, op0=ALU.mult), reads=[tmp65, k2max], writes=[tmp65])
            k.op(k.act, lambda e: e.activation(out=tmp65[64:65, 0:w], in_=tmp65[64:65, 0:w], func=AF.Sqrt),
                 reads=[tmp65], writes=[tmp65])
            k.op(k.dve, lambda e, c0=c0: e.tensor_scalar(out=QR[64:65, c0:c0 + w], in0=tmp65[64:65, 0:w], scalar1=-1.0, scalar2=None,
                                                         op0=ALU.mult), reads=[tmp65], writes=[QR])

    def row_max(self, k, src_row, ncol, k2m, tmp65, src_buf=None):
        for ci, c0 in enumerate(range(0, ncol, 2048)):
            w = min(2048, ncol - c0)
            k.dma(k.sp, [(tmp65[64:65, 0:w], src_row[:, c0:c0 + w])], tmp65, reads=([src_buf] if src_buf else []), writes=[tmp65])
            dst = k2m[64:65, 0:1] if ci == 0 else k2m[64:65, 1:2]
            k.op(k.dve, lambda e, dst=dst: e.reduce_max(out=dst, in_=tmp65[64:65, 0:w], axis=mybir.AxisListType.X),
                 reads=[tmp65], writes=[k2m])
            if ci > 0:
                k.op(k.dve, lambda e: e.tensor_tensor(out=k2m[64:65, 0:1], in0=k2m[64:65, 0:1], in1=k2m[64:65, 1:2], op=ALU.max),
                     reads=[k2m], writes=[k2m])

    def attn_bufs(self, k, TWk, TWq, nvt):
        A = dict(KT=k.sbuf("a_KT", [128, TWk], BF16, dma=True), KR=k.sbuf("a_KR", [65, TWk], BF16, dma=True),
                 QT=k.sbuf("a_QT", [128, TWq], BF16, dma=True), QR=k.sbuf("a_QR", [65, TWq], BF16, dma=True),
                 V=k.sbuf("a_V", [128, nvt, 129], BF16, dma=True), t65=k.sbuf("a_t65", [65, 2048], F32, dma=True),
                 k2m=k.sbuf("a_k2m", [65, 2], F32), PT=[k.sbuf("a_PT%d" % i, [128, 512], BF16) for i in range(3)],
                 ob=k.sbuf("a_ob", [128, 128], BF16), rec=k.sbuf("a_rec", [128, 1], F32))
        self.a_st = [k.sbuf("a_st%d" % i, [128, 128], BF16, dma=True) for i in range(2)]
        self.a_cnt = 0
        k.op(k.pool, lambda e: e.memset(A["KR"][64:65, :], 1.0), writes=[A["KR"]])
        return A

    def make_attend(self, k, A):
        B, SM = self.banks, self.SM
        KTt, KRt, QTt, QRt, Vt, PT = A["KT"], A["KR"], A["QT"], A["QR"], A["V"], A["PT"]
        pcnt = [0]

        def attend(keytiles, qcol0, nq, accs, last_tile_of, diag=None, pre_pv=None):
            n = len(keytiles)
            st = {}

            def scores(ti):
                c0, nk, vt = keytiles[ti]
                vis = [bi for bi in range(len(accs)) if last_tile_of[bi] >= ti]
                q0 = accs[vis[0]][1]
                sb = B[pcnt[0] % 2]
                pt = PT[pcnt[0] % 3]
                pcnt[0] += 1
                k.op(k.pe, lambda e: e.matmul(sb[0:nk, q0:nq], lhsT=KTt[:, c0:c0 + nk], rhs=QTt[:, qcol0 + q0:qcol0 + nq], start=True, stop=False),
                     reads=[KTt, QTt], writes=[sb], inc=False)
                k.op(k.pe, lambda e: e.matmul(sb[0:nk, q0:nq], lhsT=KRt[0:65, c0:c0 + nk], rhs=QRt[0:65, qcol0 + q0:qcol0 + nq], start=False, stop=True),
                     reads=[KRt, QRt], writes=[sb])
                k.op(k.act, lambda e: e.activation(out=pt[0:nk, q0:nq], in_=sb[0:nk, q0:nq], func=AF.Exp, scale=SM), reads=[sb], writes=[pt])
                for bi in vis:
                    if diag is not None and diag[bi] == ti:
                        qo = accs[bi][1]
                        k.op(k.pool, lambda e, qo=qo: e.memset(pt[64:128, qo:qo + 64], 0.0), writes=[pt])
                st[ti] = (pt, vis)

            DEPTH = 2
            for t_ in range(min(DEPTH, n)):
                scores(t_)
            if pre_pv is not None:
                pre_pv()
            for ti, (c0, nk, vt) in enumerate(keytiles):
                if ti + DEPTH < n:
                    scores(ti + DEPTH)
                pt, vis = st.pop(ti)
                for bi in vis:
                    bank, qo, nqb = accs[bi]
                    k.op(k.pe, lambda e, bank=bank, qo=qo, nqb=nqb: e.matmul(bank[0:nqb, 0:129], lhsT=pt[0:nk, qo:qo + nqb], rhs=Vt[0:nk, vt, :],
                                                                            start=(ti == 0), stop=(ti == last_tile_of[bi])),
                         reads=[pt, Vt], writes=[bank], inc=(ti == last_tile_of[bi] or bi == vis[-1]))
        return attend

    def phase_mla_prompt(self, k, I, O):
        C, S, B = self.cfg, self.S, self.banks
        TP = C.TP
        NT_ = 1 + C.SEQ // 128
        A0 = self.attn_bufs(k, TP, TP, NT_)
        A1 = dict(A0)
        A1.update(KT=k.sbuf("a_KT_b", [128, TP], BF16, dma=True), QT=k.sbuf("a_QT_b", [128, TP], BF16, dma=True),
                  QR=k.sbuf("a_QR_b", [65, TP], BF16, dma=True), V=k.sbuf("a_V_b", [128, NT_, 129], BF16, dma=True))
        sets = [A0, A1]
        attends = [self.make_attend(k, A0), self.make_attend(k, A1)]
        KRt = A0["KR"]
        k.dma(k.sp, [(KRt[0:64, 0:TP], S["KRT"].t[:, 0:TP])], KRt, writes=[KRt])

        def load_head(h):
            A = sets[h % 2]
            KTt, QTt, QRt, Vt = A["KT"], A["QT"], A["QR"], A["V"]
            k.dma(k.sp, [(KTt[:, 0:TP], S["KT"].t[h, :, 0:TP])], KTt, writes=[KTt])
            k.dma(k.sp, [(QTt[:, 0:TP], S["QT"].t[h, :, 0:TP])], QTt, writes=[QTt])
            k.dma(k.sp, [(QRt[0:64, 0:TP], S["QRT"].t[h, :, 0:TP])], QRt, writes=[QRt])
            vsrc = S["V1"].t[16:TP, h, :].rearrange("(i p) c -> p i c", p=128)
            k.dma(k.sp, [(Vt[0:16, 0, :], S["V1"].t[0:16, h, :])] +
                  [(Vt[:, 1 + i0:1 + min(i0 + 16, NT_ - 1), :], vsrc[:, i0:min(i0 + 16, NT_ - 1), :]) for i0 in range(0, NT_ - 1, 16)],
                  Vt, writes=[Vt])
            self.row_max(k, S["K2"].t[h:h + 1, 0:TP], TP, A["k2m"], A["t65"])
            self.stab_row(k, QRt, TP, S["QN2"].t[h:h + 1, 0:TP], A["k2m"], A["t65"])
        load_head(0)
        for h in range(4):
            A, attend = sets[h % 2], attends[h % 2]
            if h + 1 < 4:
                load_head(h + 1)
            pending_fin = None
            attend([(0, 16, 0)], 0, 16, [(B[2], 0, 16)], [0])
            self.attn_finish(k, B[2], 16, h, 0, A["ob"], A["rec"])
            nfb = C.SEQ // 128
            for g0 in range(0, nfb, 4):
                nb = min(4, nfb - g0)
                tiles = [(0, 16, 0)] + [(16 + 128 * i, 128, 1 + i) for i in range(g0 + nb)]
                accs = [(B[2 + bi], bi * 128, 128) for bi in range(nb)]
                last = [1 + g0 + bi for bi in range(nb)]
                attend(tiles, 16 + 128 * g0, nb * 128, accs, last, diag=last, pre_pv=pending_fin)

                def pending_fin(nb=nb, g0=g0, h=h, A=A):
                    for bi in range(nb):
                        self.attn_finish(k, B[2 + bi], 128, h, 16 + 128 * (g0 + bi), A["ob"], A["rec"])
            if pending_fin is not None:
                pending_fin()
            pending_fin = None

    def phase_mla_sample(self, k, I, O):
        C, S, B = self.cfg, self.S, self.banks
        TP, CA = C.TP, C.CACHE
        ctiles = [(c0, min(128, CA - c0)) for c0 in range(0, CA, 128)]
        nct = len(ctiles)
        A = self.attn_bufs(k, CA + 64, 64, nct + 1)
        attend = self.make_attend(k, A)
        KTt, KRt, QTt, QRt, Vt = A["KT"], A["KR"], A["QT"], A["QR"], A["V"]
        WUK = self.load_resident(k, "aWUK", I["wuk"], [128, 2, 512])
        WUV = self.load_resident(k, "aWUV", I["wuv"], [128, 2, 512])
        cin = [k.sbuf("a_cin%d" % i, [128, 256], F32, dma=True) for i in range(2)]
        kin = [k.sbuf("a_kin%d" % i, [128, 64], F32, dma=True) for i in range(2)]
        cb = k.sbuf("a_cb", [128, 256], BF16)
        cTc = k.sbuf("a_cTc", [128, 2, CA], BF16)
        Vall = k.sbuf("a_Vall", [128, nct + 1, 4, 129], BF16, dma=True)
        k.op(k.pool, lambda e: e.memset(Vall[:, :, :, 128:129], 1.0), writes=[Vall])
        ksq = k.sbuf("a_ksq", [128, 512], F32, dma=True)
        krsq = k.sbuf("a_krsq", [64, CA], F32)
        K2c = k.dram("K2c", [1, CA + 64], F32)
        for b in range(C.NSB):
            rq = TP + 64 * b
            for ti, (c0, nk) in enumerate(ctiles):
                ci_, ki_ = cin[ti % 2], kin[ti % 2]
                k.dma(k.sp, [(ci_[0:nk, :], I["cache_ckv"][b, c0:c0 + nk, :])], ci_, writes=[ci_])
                k.dma(k.sp, [(ki_[0:nk, :], I["cache_kr"][b, c0:c0 + nk, :])], ki_, writes=[ki_])
                k.op(k.act, lambda e: e.copy(out=cb[0:nk, :], in_=ci_[0:nk, :]), reads=[ci_], writes=[cb])
                tb = B[6].t[:].bitcast(BF16)
                for kc in range(2):
                    k.op(k.pe, lambda e, kc=kc: e.transpose(out=tb[:, kc * 128:kc * 128 + nk], in_=cb[0:nk, kc * 128:(kc + 1) * 128],
                                                            identity=self.identb[0:nk, 0:nk]), reads=[cb, self.identb], writes=[B[6]], inc=(kc == 1))
                k.op(k.dve, lambda e: e.tensor_copy(out=cTc[:, :, c0:c0 + nk], in_=tb[:, 0:256].rearrange("p (k t) -> p k t", t=128)[:, :, 0:nk]),
                     reads=[B[6]], writes=[cTc])
                k.op(k.pe, lambda e: e.transpose(out=B[7][0:64, 0:nk], in_=ki_[0:nk, :], identity=self.identf[0:nk, 0:nk]),
                     reads=[ki_, self.identf], writes=[B[7]])
                k.op(k.act, lambda e: e.copy(out=KRt[0:64, c0:c0 + nk], in_=B[7][0:64, 0:nk]), reads=[B[7]], writes=[KRt])
                k.op(k.act, lambda e: e.activation(out=krsq[0:64, c0:c0 + nk], in_=B[7][0:64, 0:nk], func=AF.Square), reads=[B[7]], writes=[krsq])
                for kc in range(2):
                    k.op(k.pe, lambda e, kc=kc: e.matmul(B[5][0:nk, :], lhsT=cTc[:, kc, c0:c0 + nk], rhs=WUV[:, kc, :], start=(kc == 0), stop=(kc == 1)),
                         reads=[cTc, WUV], writes=[B[5]], inc=(kc == 1))
                k.op(k.dve, lambda e: e.tensor_copy(out=Vall[0:nk, ti, :, 0:128], in_=B[5][0:nk, :].rearrange("p (h d) -> p h d", d=128)),
                     reads=[B[5]], writes=[Vall])
            k.dma(k.sp, [(KRt[0:64, CA:CA + 64], S["KRT"].t[:, rq:rq + 64])], KRt, writes=[KRt])
            k.dma(k.sp, [(Vall[0:64, nct, :, :], S["V1"].t[rq:rq + 64, :, :])], Vall, writes=[Vall])
            for h in range(4):
                for c0 in range(0, CA, 512):
                    w = min(512, CA - c0)
                    for kc in range(2):
                        k.op(k.pe, lambda e, kc=kc: e.matmul(B[5][:, 0:w], lhsT=WUK[:, kc, h * 128:(h + 1) * 128], rhs=cTc[:, kc, c0:c0 + w],
                                                             start=(kc == 0), stop=(kc == 1)), reads=[WUK, cTc], writes=[B[5]], inc=(kc == 1))
                    k.op(k.act, lambda e: e.copy(out=KTt[:, c0:c0 + w], in_=B[5][:, 0:w]), reads=[B[5]], writes=[KTt])
                    k.op(k.act, lambda e: e.activation(out=ksq[:, 0:w], in_=B[5][:, 0:w], func=AF.Square), reads=[B[5]], writes=[ksq])
                    k.op(k.pe, lambda e: e.matmul(B[7][0:1, 0:w], lhsT=self.ones[:, 0:1], rhs=ksq[:, 0:w], start=True, stop=False),
                         reads=[self.ones, ksq], writes=[B[7]], inc=False)
                    k.op(k.pe, lambda e: e.matmul(B[7][0:1, 0:w], lhsT=self.ones[0:64, 0:1], rhs=krsq[0:64, c0:c0 + w], start=False, stop=True),
                         reads=[self.ones, krsq], writes=[B[7]])
                    k.op(k.dve, lambda e: e.tensor_copy(out=ksq[0:1, 0:w], in_=B[7][0:1, 0:w]), reads=[B[7]], writes=[ksq])
                    k.dma(k.sp, [(K2c[:, c0:c0 + w], ksq[0:1, 0:w])], ksq, reads=[ksq], writes=[K2c])
                k.dma(k.sp, [(K2c[:, CA:CA + 64], S["K2"].t[h:h + 1, rq:rq + 64])], K2c, writes=[K2c])
                k.dma(k.sp, [(KTt[:, CA:CA + 64], S["KT"].t[h, :, rq:rq + 64])], KTt, writes=[KTt])
                self.row_max_buf(k, K2c, CA + 64, A["k2m"], A["t65"])
                k.dma(k.sp, [(QTt[:, 0:64], S["QT"].t[h, :, rq:rq + 64])], QTt, writes=[QTt])
                k.dma(k.sp, [(QRt[0:64, 0:64], S["QRT"].t[h, :, rq:rq + 64])], QRt, writes=[QRt])
                self.stab_row(k, QRt, 64, S["QN2"].t[h:h + 1, rq:rq + 64], A["k2m"], A["t65"])
                k.op(k.pool, lambda e: e.tensor_copy(out=Vt[:, 0:nct + 1, :], in_=Vall[:, :, h, :]), reads=[Vall], writes=[Vt])
                tiles = [(c0, nk, ti) for ti, (c0, nk) in enumerate(ctiles)] + [(CA, 64, nct)]
                attend(tiles, 0, 64, [(B[2], 0, 64)], [len(tiles) - 1])
                self.attn_finish(k, B[2], 64, h, rq, A["ob"], A["rec"])

    def row_max_buf(self, k, buf, ncol, k2m, tmp65):
        self.row_max(k, buf.t[:, 0:ncol], ncol, k2m, tmp65, src_buf=buf)

    def phase_out(self, k, I, O):
        C, B = self.cfg, self.banks
        WO = self.load_resident(k, "WO", I["wout"], [128, KC, D])
        mixT2 = [k.sbuf("o_mixT%d" % i, [128, 8, 512], BF16, dma=True) for i in range(2)]

        def load_group(gj):
            blocks_, _ = group_layout(C.groups[gj])
            xt_, mx_ = self.xt[gj % 2], mixT2[gj % 2]
            runs = []
            for (r0, n, off) in blocks_:
                if runs and runs[-1][2] + runs[-1][1] == r0:
                    runs[-1] = (runs[-1][0], runs[-1][1] + n, runs[-1][2])
                else:
                    runs.append((off, n, r0))
            k.dma(k.sp, [(mx_[:, :, o:o + n], self.MIXT.t[:, :, r:r + n].rearrange("c p t -> p c t")) for (o, n, r) in runs],
                  mx_, reads=[self.MIXT], writes=[mx_])
            for bi, (r0, n, off) in enumerate(blocks_):
                k.dma(k.sp, [(xt_[bi][0:n, :], self.X1[r0:r0 + n, :])], xt_[bi], reads=[self.X1], writes=[xt_[bi]])
        load_group(0)
        for gi, grp in enumerate(C.groups):
            blocks, NT = group_layout(grp)
            xt, mixT = self.xt[gi % 2], mixT2[gi % 2]
            if gi + 1 < len(C.groups):
                load_group(gi + 1)
            for bi, (r0, n, off) in enumerate(blocks):
                for half in range(2):
                    cs = slice(half * 512, (half + 1) * 512)
                    bank = B[4 + (2 * bi + half) % 4]
                    for kc in range(8):
                        k.op(k.pe, lambda e, kc=kc: e.matmul(bank[0:n, :], lhsT=mixT[:, kc, off:off + n], rhs=WO[:, kc, cs],
                                                             start=(kc == 0), stop=(kc == 7)), reads=[mixT, WO], writes=[bank], inc=(kc == 7))
                    k.op(k.dve, lambda e: e.tensor_tensor(out=xt[bi][0:n, cs], in0=bank[0:n, :], in1=xt[bi][0:n, cs], op=ALU.add),
                         reads=[bank, xt[bi]], writes=[xt[bi]])
            self.norm_group(k, blocks, xt, self.gT[:, 2, :], B[4:8])

            def epi(half, blocks=blocks, xt=xt):
                for bi, (r0, n, off) in enumerate(blocks):
                    cs = slice(half * 512, (half + 1) * 512)
                    xo = self.xo[bi]
                    k.op(k.dve, lambda e, bi=bi, n=n, cs=cs, xo=xo: e.scalar_tensor_tensor(
                        out=xo[0:n, cs], in0=B[4 + bi][0:n, 0:512], scalar=0.5, in1=xt[bi][0:n, cs], op0=ALU.mult, op1=ALU.add),
                         reads=[B[4 + bi], xt[bi]], writes=[xo])
                    if half == 1:
                        stt_, rstd = self.rstd_of(k, xo[0:n, :], n, D, [xo])
                        k.op(k.dve, lambda e, n=n, xo=xo, rstd=rstd: e.scalar_tensor_tensor(
                            out=xo[0:n, :], in0=xo[0:n, :], scalar=rstd, in1=self.gfin[0:n, :], op0=ALU.mult, op1=ALU.mult),
                             reads=[xo, stt_, self.gfin], writes=[xo])
                        k.dma(k.sp, [(O["y"][r0:r0 + n, :], xo[0:n, :])], xo, reads=[xo])
            self.ffn_core(k, blocks, NT, self.wgu2_s, self.wd2_s, epi)

def lay_wgu(wg, wu):
    a = np.stack([wg, wu], 0).reshape(2, KC, 128, NFC, 128)
    a = a.transpose(3, 2, 0, 1, 4)
    return np.ascontiguousarray(a).reshape(NFC * 128, 2 * KC * 128)


def lay_wd(wd):
    a = wd.reshape(NFC, 128, D).transpose(1, 0, 2)
    return np.ascontiguousarray(a).reshape(128 * NFC, D)


def _kcp(w):
    K_, Cc = w.shape
    return np.ascontiguousarray(w.reshape(K_ // 128, 128, Cc).transpose(1, 0, 2))


def _swap(w):
    return np.concatenate([w[:, 32:64], w[:, 0:32]], axis=1)


def lay_win(w_in):
    kr = w_in[:, 2696:2760]
    wf = np.concatenate([w_in[:, 0:1536], kr, _swap(kr)], axis=1)
    wt = np.concatenate([w_in[:, 1536:2048], w_in[:, 2056:2440], w_in[:, 2440:2696], w_in[:, 2048:2056]], axis=1)
    return _kcp(wf), _kcp(wt)


def lay_wuq(w_uq):
    cols = []
    for h in range(4):
        qn = w_uq[:, h * 192:h * 192 + 128]
        qr = w_uq[:, h * 192 + 128:h * 192 + 192]
        cols += [qn, qr, _swap(qr)]
    return _kcp(np.concatenate(cols, axis=1))


def lay_wukv(w_ukv):
    kn = np.concatenate([w_ukv[:, h * 256:h * 256 + 128] for h in range(4)], axis=1)
    v = np.concatenate([w_ukv[:, h * 256 + 128:h * 256 + 256] for h in range(4)], axis=1)
    return _kcp(kn), _kcp(v)


_PROG = {}


def _program():
    if "nc" not in _PROG:
        cfg = Cfg(SEQ=8192, NSB=4, PAST=2048)
        p = Prog(cfg)
        _PROG["nc"] = p.build(upto=5)
        _PROG["cfg"] = cfg
    return _PROG["nc"], _PROG["cfg"]


def kernel(x_prompt, x_sample, cache_mla_ckv, cache_mla_krope, state_gdn, state_conv, meta,
           ffn1_norm, ffn1_wg, ffn1_wu, ffn1_wd, mix_norm, w_in, conv_w, a_log, dt_bias, gdn_norm,
           q_norm, kv_norm, w_uq, w_ukv, w_out, ffn2_norm, ffn2_wg, ffn2_wu, ffn2_wd, final_norm):
    f = lambda a: np.ascontiguousarray(np.asarray(a, dtype=np.float32))
    nc, cfg = _program()
    NB, NSB, TP = 4, cfg.NSB, cfg.TP
    wf, wt = lay_win(f(w_in)[0])
    wuk, wuv = lay_wukv(f(w_ukv)[0])
    shared = dict(
        norms=np.stack([f(ffn1_norm)[0], f(mix_norm)[0], f(ffn2_norm)[0], f(final_norm)]),
        q_norm=f(q_norm)[0], kv_norm=f(kv_norm)[0], gdn_norm=f(gdn_norm)[0], a_log=f(a_log)[0], dt_bias=f(dt_bias)[0],
        conv_w=f(conv_w)[0],
        wgu1=lay_wgu(f(ffn1_wg)[0], f(ffn1_wu)[0]), wd1=lay_wd(f(ffn1_wd)[0]),
        wgu2=lay_wgu(f(ffn2_wg)[0], f(ffn2_wu)[0]), wd2=lay_wd(f(ffn2_wd)[0]),
        wf=wf, wt=wt, wuq=lay_wuq(f(w_uq)[0]), wuk=wuk, wuv=wuv, wout=_kcp(f(w_out)[0]))
    xp, xs, mt = f(x_prompt), f(x_sample), f(meta)
    in_maps = []
    for c in range(8):
        sb = slice(NSB * c, NSB * (c + 1))
        m = dict(shared)
        m["xin"] = np.concatenate([mt, xp[c % NB], xs[sb].reshape(-1, D)], axis=0)
        m["state_conv"] = f(state_conv)[0, sb]
        m["state_gdn"] = f(state_gdn)[0, sb]
        m["cache_ckv"] = f(cache_mla_ckv)[0, sb]
        m["cache_kr"] = f(cache_mla_krope)[0, sb]
        in_maps.append(m)
    res = run_bass_kernel_spmd(nc, in_maps, core_ids=list(range(8))).results
    y_p = np.stack([res[b]["y"][16:TP] for b in range(NB)])
    y_s = np.concatenate([res[c]["y"][TP:].reshape(NSB, 64, D) for c in range(8)])
    p_ckv = np.stack([res[b]["ckv"][0:TP] for b in range(NB)])[None]
    p_kr = np.stack([res[b]["kr"][0:TP] for b in range(NB)])[None]
    p_gdn = np.stack([res[b]["pgdn"] for b in range(NB)])[None]
    p_conv = np.stack([res[b]["pconv"] for b in range(NB)])[None]
    s_ckv = np.concatenate([res[c]["ckv"][TP:].reshape(NSB, 64, 256) for c in range(8)])[None]
    s_kr = np.concatenate([res[c]["kr"][TP:].reshape(NSB, 64, 64) for c in range(8)])[None]
    s_gdn = np.concatenate([res[c]["sgdn"] for c in range(8)])[None]
    s_conv = np.concatenate([res[c]["sconv"] for c in range(8)])[None]
    return tuple(np.ascontiguousarray(a, dtype=np.float32) for a in
                 (y_p, y_s, p_ckv, p_kr, p_gdn, p_conv, s_ckv, s_kr, s_gdn, s_conv))
```
